# Optimizing a Trainium2 kernel written in Bass

```python
import jax, jax.numpy as jnp
from jax import lax
import numpy as np

D_MODEL = 1024
BATCH = 8
SEQ = 4096
DEPTH = 1
DEC_BATCH = 16
DEC_SEQ = 64
PAST_LEN = 4096

CHUNK = 64
N_PREV_CHUNKS = 8
BAND_PAST = N_PREV_CHUNKS * CHUNK
BAND = BAND_PAST + CHUNK
HEAD_DIM = 64
H_A = 8
H_B = 8
WIDTH_A = H_A * HEAD_DIM
WIDTH_B = H_B * HEAD_DIM
MIX_WIDTH = WIDTH_A + WIDTH_B
IN_COLS = 3 * WIDTH_A + 3 * WIDTH_B + H_B
REL_CLIP = 128
Q_BLOCK = 128
N_MEM = 256
H_M = 4
HEAD_DIM_M = D_MODEL // H_M
D_FF = 2816
CONV_W = 3
EPS = 1e-6

kernel_name = "hybrid_chunk_band_fox_stream_step"


def rms_norm(x, g):
    x32 = x.astype(jnp.float32)
    y = x32 * lax.rsqrt(jnp.mean(x32 * x32, axis=-1, keepdims=True) + EPS)
    return (y * g.astype(jnp.float32)).astype(x.dtype)


def attend(q, k, v, bias, mask):
    scale = q.shape[-1] ** -0.5
    logits = jnp.einsum('bqhd,bkhd->bhqk', q, k).astype(jnp.float32) * scale
    if bias is not None:
        logits = logits + bias
    if mask is not None:
        logits = jnp.where(mask, logits, -jnp.inf)
    p = jax.nn.softmax(logits, axis=-1)
    return jnp.einsum('bhqk,bkhd->bqhd', p.astype(v.dtype), v)


def in_proj(h, w_in, b_f, g_qa, g_ka, g_qb, g_kb):
    b, t, _ = h.shape
    p = h @ w_in
    def heads(lo, width, nh):
        return p[..., lo:lo + width].reshape(b, t, nh, HEAD_DIM)
    qa = rms_norm(heads(0, WIDTH_A, H_A), g_qa)
    ka = rms_norm(heads(WIDTH_A, WIDTH_A, H_A), g_ka)
    va = heads(2 * WIDTH_A, WIDTH_A, H_A)
    o = 3 * WIDTH_A
    qb = rms_norm(heads(o, WIDTH_B, H_B), g_qb)
    kb = rms_norm(heads(o + WIDTH_B, WIDTH_B, H_B), g_kb)
    vb = heads(o + 2 * WIDTH_B, WIDTH_B, H_B)
    f_logit = p[..., o + 3 * WIDTH_B:]
    logf = jax.nn.log_sigmoid(f_logit.astype(jnp.float32) + b_f.astype(jnp.float32))
    return qa, ka, va, qb, kb, vb, logf


def rel_bias_lookup(table, tq, ts):
    rel = jnp.clip(tq[:, None] - ts[None, :], -REL_CLIP, REL_CLIP) + REL_CLIP
    return table.astype(jnp.float32)[:, rel][None]


def chunk_band_attn_prompt(qa, ka, va, table):
    b, t, h, d = qa.shape
    pad = jnp.zeros((b, BAND_PAST, h, d), ka.dtype)
    kp = jnp.concatenate([pad, ka], axis=1)
    vp = jnp.concatenate([pad.astype(va.dtype), va], axis=1)
    def one_chunk(c):
        start = c * CHUNK
        q_c = lax.dynamic_slice_in_dim(qa, start, CHUNK, 1)
        k_c = lax.dynamic_slice_in_dim(kp, start, BAND, 1)
        v_c = lax.dynamic_slice_in_dim(vp, start, BAND, 1)
        tq = start + jnp.arange(CHUNK)
        ts = start - BAND_PAST + jnp.arange(BAND)
        bias = rel_bias_lookup(table, tq, ts)
        mask = (ts >= 0)[None, None, None, :]
        return attend(q_c, k_c, v_c, bias, mask)
    out = lax.map(one_chunk, jnp.arange(t // CHUNK))
    return jnp.moveaxis(out, 0, 1).reshape(b, t, h * d)


def chunk_band_attn_sample(qa, ka, va, cache_k, cache_v, table):
    b, s, h, d = qa.shape
    la = cache_k.shape[1]
    k = jnp.concatenate([cache_k, ka], axis=1)
    v = jnp.concatenate([cache_v, va], axis=1)
    tq = la + jnp.arange(s)
    ts = jnp.arange(la + s)
    bias = rel_bias_lookup(table, tq, ts)
    return attend(qa, k, v, bias, None).reshape(b, s, h * d)


def forget_attn_prompt(qb, kb, vb, logf):
    b, t, h, d = qb.shape
    cum_h = jnp.cumsum(logf, axis=1).transpose(0, 2, 1)
    ts = jnp.arange(t)
    def one_block(i):
        start = i * Q_BLOCK
        q_i = lax.dynamic_slice_in_dim(qb, start, Q_BLOCK, 1)
        c_i = lax.dynamic_slice_in_dim(cum_h, start, Q_BLOCK, 2)
        bias = c_i[..., :, None] - cum_h[..., None, :]
        tq = start + jnp.arange(Q_BLOCK)
        mask = (tq[:, None] >= ts[None, :])[None, None]
        return attend(q_i, kb, vb, bias, mask)
    out = lax.map(one_block, jnp.arange(t // Q_BLOCK))
    return jnp.moveaxis(out, 0, 1).reshape(b, t, h * d)


def forget_attn_sample(qb, kb, vb, logf, cache_k, cache_v, cache_logf):
    b, s, h, d = qb.shape
    p = cache_k.shape[1]
    k = jnp.concatenate([cache_k, kb], axis=1)
    v = jnp.concatenate([cache_v, vb], axis=1)
    all_logf = jnp.concatenate([cache_logf.astype(jnp.float32), logf], axis=1)
    cum_h = jnp.cumsum(all_logf, axis=1).transpose(0, 2, 1)
    bias = cum_h[..., p:, None] - cum_h[..., None, :]
    tq = p + jnp.arange(s)
    ts = jnp.arange(p + s)
    mask = (tq[:, None] >= ts[None, :])[None, None]
    return attend(qb, k, v, bias, mask).reshape(b, s, h * d)


def mem_kv(mem, g_mem, w_mkv, g_mk):
    b, n, _ = mem.shape
    kv = rms_norm(mem, g_mem) @ w_mkv
    k = rms_norm(kv[..., :D_MODEL].reshape(b, n, H_M, HEAD_DIM_M), g_mk)
    v = kv[..., D_MODEL:].reshape(b, n, H_M, HEAD_DIM_M)
    return k, v


def mem_attend(h, mk, mv, w_mq, g_mq, w_mo):
    b, t, _ = h.shape
    q = rms_norm((h @ w_mq).reshape(b, t, H_M, HEAD_DIM_M), g_mq)
    return attend(q, mk, mv, None, None).reshape(b, t, D_MODEL) @ w_mo


def conv_ffn(h, prev, w_up, w_conv, b_conv, w_down):
    gu = h @ w_up
    gate, val = gu[..., :D_FF], gu[..., D_FF:]
    t = gate.shape[1]
    gp = jnp.concatenate([prev.astype(gate.dtype), gate], axis=1)
    conv = sum(gp[:, j:j + t] * w_conv[j] for j in range(CONV_W)) + b_conv
    y = (jax.nn.silu(conv) * val) @ w_down
    return y, gp[:, -(CONV_W - 1):]


def setup_inputs(seed: int = 0) -> dict:
    key = jax.random.key(seed)
    ks = jax.random.split(key, 32)
    f32 = jnp.float32
    L = DEPTH
    la = min(BAND_PAST, PAST_LEN)
    def nrm(k, shape, scale=1.0):
        return jax.random.normal(k, shape, f32) * scale
    def gain(k, shape):
        return 1.0 + 0.1 * nrm(k, shape)
    return {
        'x_prompt': nrm(ks[0], (BATCH, SEQ, D_MODEL)),
        'x_sample': nrm(ks[1], (DEC_BATCH, DEC_SEQ, D_MODEL)),
        'cache_a_k': nrm(ks[2], (L, DEC_BATCH, la, H_A, HEAD_DIM)),
        'cache_a_v': nrm(ks[3], (L, DEC_BATCH, la, H_A, HEAD_DIM)),
        'cache_b_k': nrm(ks[4], (L, DEC_BATCH, PAST_LEN, H_B, HEAD_DIM)),
        'cache_b_v': nrm(ks[5], (L, DEC_BATCH, PAST_LEN, H_B, HEAD_DIM)),
        'cache_b_logf': jax.nn.log_sigmoid(3.0 + nrm(ks[6], (L, DEC_BATCH, PAST_LEN, H_B))),
        'cache_mem_k': nrm(ks[7], (L, DEC_BATCH, N_MEM, H_M, HEAD_DIM_M)),
        'cache_mem_v': nrm(ks[8], (L, DEC_BATCH, N_MEM, H_M, HEAD_DIM_M)),
        'state_conv': nrm(ks[9], (L, DEC_BATCH, CONV_W - 1, D_FF)),
        'mem_prompt': nrm(ks[10], (BATCH, N_MEM, D_MODEL)),
        'w_in': nrm(ks[11], (L, D_MODEL, IN_COLS), D_MODEL ** -0.5),
        'b_f': 3.0 + 0.1 * nrm(ks[12], (L, H_B)),
        'g_qa': gain(ks[13], (L, HEAD_DIM)),
        'g_ka': gain(ks[14], (L, HEAD_DIM)),
        'rel_bias': nrm(ks[15], (L, H_A, 2 * REL_CLIP + 1), 0.1),
        'g_qb': gain(ks[16], (L, HEAD_DIM)),
        'g_kb': gain(ks[17], (L, HEAD_DIM)),
        'w_o': nrm(ks[18], (L, MIX_WIDTH, D_MODEL), MIX_WIDTH ** -0.5),
        'g_norm1': gain(ks[19], (L, D_MODEL)),
        'g_norm2': gain(ks[20], (L, D_MODEL)),
        'g_mem': gain(ks[21], (L, D_MODEL)),
        'w_mq': nrm(ks[22], (L, D_MODEL, D_MODEL), D_MODEL ** -0.5),
        'w_mkv': nrm(ks[23], (L, D_MODEL, 2 * D_MODEL), D_MODEL ** -0.5),
        'g_mq': gain(ks[24], (L, HEAD_DIM_M)),
        'g_mk': gain(ks[25], (L, HEAD_DIM_M)),
        'w_mo': nrm(ks[26], (L, D_MODEL, D_MODEL), D_MODEL ** -0.5),
        'g_norm3': gain(ks[27], (L, D_MODEL)),
        'w_up': nrm(ks[28], (L, D_MODEL, 2 * D_FF), D_MODEL ** -0.5),
        'w_conv': nrm(ks[29], (L, CONV_W, D_FF), CONV_W ** -0.5),
        'b_conv': nrm(ks[30], (L, D_FF), 0.02),
        'w_down': nrm(ks[31], (L, D_FF, D_MODEL), D_FF ** -0.5),
    }


def reference(x_prompt, x_sample, cache_a_k, cache_a_v, cache_b_k, cache_b_v, cache_b_logf,
              cache_mem_k, cache_mem_v, state_conv, mem_prompt,
              w_in, b_f, g_qa, g_ka, rel_bias, g_qb, g_kb, w_o, g_norm1, g_norm2, g_mem,
              w_mq, w_mkv, g_mq, g_mk, w_mo, g_norm3, w_up, w_conv, b_conv, w_down):
    xp, xs = x_prompt, x_sample
    bp, tp = xp.shape[0], xp.shape[1]
    la_p = min(BAND_PAST, tp)
    ak_p, av_p, bk_p, bv_p, bl_p, mk_p, mv_p, cv_p = [], [], [], [], [], [], [], []
    ak_s, av_s, bk_s, bv_s, bl_s, cv_s = [], [], [], [], [], []
    for l in range(DEPTH):
        h = rms_norm(xp, g_norm1[l])
        qa, ka, va, qb, kb, vb, logf = in_proj(h, w_in[l], b_f[l], g_qa[l], g_ka[l], g_qb[l], g_kb[l])
        oa = chunk_band_attn_prompt(qa, ka, va, rel_bias[l])
        ob = forget_attn_prompt(qb, kb, vb, logf)
        xp = xp + jnp.concatenate([oa, ob], axis=-1) @ w_o[l]
        mk, mv = mem_kv(mem_prompt, g_mem[l], w_mkv[l], g_mk[l])
        xp = xp + mem_attend(rms_norm(xp, g_norm2[l]), mk, mv, w_mq[l], g_mq[l], w_mo[l])
        zeros_prev = jnp.zeros((bp, CONV_W - 1, D_FF), xp.dtype)
        f, conv_p = conv_ffn(rms_norm(xp, g_norm3[l]), zeros_prev, w_up[l], w_conv[l], b_conv[l], w_down[l])
        xp = xp + f
        ak_p.append(ka[:, tp - la_p:]); av_p.append(va[:, tp - la_p:])
        bk_p.append(kb); bv_p.append(vb); bl_p.append(logf)
        mk_p.append(mk); mv_p.append(mv); cv_p.append(conv_p)

        h = rms_norm(xs, g_norm1[l])
        qa, ka, va, qb, kb, vb, logf = in_proj(h, w_in[l], b_f[l], g_qa[l], g_ka[l], g_qb[l], g_kb[l])
        oa = chunk_band_attn_sample(qa, ka, va, cache_a_k[l], cache_a_v[l], rel_bias[l])
        ob = forget_attn_sample(qb, kb, vb, logf, cache_b_k[l], cache_b_v[l], cache_b_logf[l])
        xs = xs + jnp.concatenate([oa, ob], axis=-1) @ w_o[l]
        xs = xs + mem_attend(rms_norm(xs, g_norm2[l]), cache_mem_k[l], cache_mem_v[l], w_mq[l], g_mq[l], w_mo[l])
        f, conv_s = conv_ffn(rms_norm(xs, g_norm3[l]), state_conv[l], w_up[l], w_conv[l], b_conv[l], w_down[l])
        xs = xs + f
        ak_s.append(ka); av_s.append(va)
        bk_s.append(kb); bv_s.append(vb); bl_s.append(logf)
        cv_s.append(conv_s)

    st = lambda lst: jnp.stack(lst, axis=0)
    return (xp, xs,
            st(ak_p), st(av_p), st(bk_p), st(bv_p), st(bl_p), st(mk_p), st(mv_p), st(cv_p),
            st(ak_s), st(av_s), st(bk_s), st(bv_s), st(bl_s), st(cv_s))
```

```python
import contextlib
import numpy as np
import concourse.bass as bass
import concourse.mybir as mybir
from concourse.bass_utils import run_bass_kernel_spmd

F32 = mybir.dt.float32
BF16 = mybir.dt.bfloat16
AF = mybir.ActivationFunctionType
ALU = mybir.AluOpType
AX = mybir.AxisListType
NEG = -30000.0


class Buf:
    __slots__ = ("name", "w", "r")

    def __init__(self, name):
        self.name = name
        self.w = {}
        self.r = {}


class Tl(Buf):
    __slots__ = ("t",)

    def __init__(self, t, name):
        super().__init__(name)
        self.t = t

    def __getitem__(self, i):
        return self.t[i]


class Rot:
    def __init__(self, tls):
        self.tls = tls
        self.i = 0

    def nxt(self):
        t = self.tls[self.i]
        self.i = (self.i + 1) % len(self.tls)
        return t


class Sched:
    NDMA = 32

    def __init__(self, nc, es):
        self.nc = nc
        self.eng = {"pe": nc.tensor, "act": nc.scalar, "dve": nc.vector,
                    "pool": nc.gpsimd, "sp": nc.sync}
        self.sems = {}
        for e in self.eng:
            self.sems[e] = es.enter_context(nc.semaphore("s_" + e))
        for i in range(self.NDMA):
            self.sems[("dma", i)] = es.enter_context(nc.semaphore("s_dma%d" % i))
        self.cnt = {k: 0 for k in self.sems}
        self.waited = {e: {} for e in self.eng}
        self.rr = 0
        self.rrq = [0, 0]
        self.nwaits = 0
        self.nops = {e: 0 for e in self.eng}
        self.stopped = False

    def _wait(self, e, deps):
        w = self.waited[e]
        for k, v in deps.items():
            if k == e and e in ("pe", "sp"):
                continue
            if w.get(k, 0) >= v:
                continue
            self.eng[e].wait_ge(self.sems[k], v)
            self.nwaits += 1
            w[k] = v

    @staticmethod
    def _merge(d, src):
        for k, v in src.items():
            if d.get(k, 0) < v:
                d[k] = v

    def op(self, e, fn, reads=(), writes=(), sig=True):
        if self.stopped:
            return None
        deps = {}
        for b in reads:
            self._merge(deps, b.w)
        for b in writes:
            self._merge(deps, b.w)
            self._merge(deps, b.r)
        self._wait(e, deps)
        ins = fn(self.eng[e])
        self.nops[e] += 1
        if sig:
            self.cnt[e] += 1
            ins.then_inc(self.sems[e], 1)
            t = self.cnt[e]
        else:
            t = self.cnt[e] + 1
        for b in reads:
            if b.r.get(e, 0) < t:
                b.r[e] = t
        for b in writes:
            b.w = {e: t}
            b.r = {}
        return ins

    def dma(self, q, out, in_, reads=(), writes=(), **kw):
        if self.stopped:
            return None
        half = self.NDMA // 2
        qi = 0 if q == "sp" else 1
        i = qi * half + self.rrq[qi]
        self.rrq[qi] = (self.rrq[qi] + 1) % half
        k = ("dma", i)
        deps = {}
        if self.cnt[k] > 0:
            deps[k] = self.cnt[k]
        for b in reads:
            self._merge(deps, b.w)
        for b in writes:
            self._merge(deps, b.w)
            self._merge(deps, b.r)
        self._wait(q, deps)
        ins = self.eng[q].dma_start(out=out, in_=in_, **kw)
        self.nops[q] += 1
        self.cnt[k] += 16
        ins.then_inc(self.sems[k], 16)
        t = self.cnt[k]
        for b in reads:
            b.r[k] = t
        for b in writes:
            b.w = {k: t}
            b.r = {}
        return ins

    def barrier(self, engines=("pe", "act", "dve", "pool", "sp")):
        if self.stopped:
            return
        deps = {k: v for k, v in self.cnt.items() if v > 0}
        for e in engines:
            d = {k: v for k, v in deps.items() if k != e}
            w = self.waited[e]
            for k, v in d.items():
                if w.get(k, 0) >= v:
                    continue
                self.eng[e].wait_ge(self.sems[k], v)
                w[k] = v


D = 1024
SEQ = 4096
NT = 32
DFF = 2816
NF = 22
INC = 3080
NS = 64


class _Stop(Exception):
    pass


def build(phases=3, stage=99):
    nc = bass.Bass("TRN2", target_bir_lowering=False)

    def chk(k):
        if stage == k:
            S.stopped = True

    H = {}

    def din(n, shape):
        H[n] = nc.dram_tensor(n, list(shape), F32, kind="ExternalInput")
        return H[n].ap()

    def dout(n, shape):
        H[n] = nc.dram_tensor(n, list(shape), F32, kind="ExternalOutput")
        return H[n].ap()

    xp = din("xp", [SEQ, D]); xs = din("xs", [2 * NS, D])
    cak = din("cak", [2, 512, 512]); cav = din("cav", [2, 512, 512])
    cbk = din("cbk", [2, SEQ, 512]); cbv = din("cbv", [2, SEQ, 512]); cbl = din("cbl", [2, SEQ, 8])
    cmk = din("cmk", [2, 256, D]); cmv = din("cmv", [2, 256, D]); scv = din("scv", [2, 2, DFF])
    memp = din("memp", [256, D])
    w_in = din("w_in", [D, INC]); b_f = din("b_f", [1, 8])
    g_qa = din("g_qa", [1, 64]); g_ka = din("g_ka", [1, 64]); rel = din("rel", [8, 257])
    g_qb = din("g_qb", [1, 64]); g_kb = din("g_kb", [1, 64])
    w_o = din("w_o", [D, D]); g1 = din("g1", [1, D]); g2 = din("g2", [1, D]); gmem = din("gmem", [1, D])
    w_mq = din("w_mq", [D, D]); w_mkv = din("w_mkv", [D, 2 * D]); g_mq = din("g_mq", [1, 256]); g_mk = din("g_mk", [1, 256])
    w_mo = din("w_mo", [D, D]); g3 = din("g3", [1, D]); w_up = din("w_up", [D, 2 * DFF])
    w_conv = din("w_conv", [3, DFF]); b_conv = din("b_conv", [1, DFF]); w_down = din("w_down", [DFF, D])

    y_p = dout("y_p", [SEQ, D]); y_s = dout("y_s", [2 * NS, D])
    akp = dout("akp", [512, 512]); avp = dout("avp", [512, 512])
    bkp = dout("bkp", [SEQ, 512]); bvp = dout("bvp", [SEQ, 512]); blp = dout("blp", [SEQ, 8])
    mkp = dout("mkp", [256, D]); mvp = dout("mvp", [256, D]); cvp = dout("cvp", [2, DFF])
    aks = dout("aks", [2 * NS, 512]); avs = dout("avs", [2 * NS, 512])
    bks = dout("bks", [2 * NS, 512]); bvs = dout("bvs", [2 * NS, 512]); bls = dout("bls", [2 * NS, 8])
    cvs = dout("cvs", [2, 2, DFF])

    X1 = nc.dram_tensor("X1", [SEQ + 2 * NS, D], F32, kind="Internal").ap()
    X2 = nc.dram_tensor("X2", [SEQ + 2 * NS, D], F32, kind="Internal").ap()
    Esc_h = nc.dram_tensor("Esc", [8, 512], F32, kind="Internal")
    Esc = Esc_h.ap()
    Osc = nc.dram_tensor("Osc", [SEQ + 2 * NS, D], BF16, kind="Internal").ap()
    Ob = [Buf("Osc%d" % i) for i in range(NT + 2)]
    WinS = nc.dram_tensor("WinS", [7, 128, 8 * 512], BF16, kind="Internal").ap()
    X1b = [Buf("X1_%d" % i) for i in range(NT + 2)]
    X2b = [Buf("X2_%d" % i) for i in range(NT + 2)]
    outb = Buf("outputs")
    WinSb = [Buf("WinS%d" % i) for i in range(7)]

    es = contextlib.ExitStack()
    with es:
        S = Sched(nc, es)
        try:

            def sb(st, name, shape, dt=F32):
                return Tl(st.enter_context(nc.sbuf_tensor(name, list(shape), dt)), name)

            def psb(st, name, shape, dt=F32):
                return Tl(st.enter_context(nc.psum_tensor(name, list(shape), dt)), name)

            bank = [psb(es, "bk%d" % i, [128, 512]) for i in range(7)]
            ptb = psb(es, "bkT", [128, 1024], BF16)

            cf = sb(es, "cf", [128, 128])
            ident = sb(es, "ident", [128, 128], BF16)
            J = sb(es, "J", [128, 128], BF16)
            U = sb(es, "U", [128, 128])
            ones_f = sb(es, "ones_f", [128, 128])
            ones_b = sb(es, "ones_b", [128, 128], BF16)
            Mdiag = sb(es, "Mdiag", [128, 128], BF16)
            Mask0 = sb(es, "Mask0", [128, 128], BF16)
            mhalf = sb(es, "mhalf", [128, 8])
            one_col = sb(es, "one_col", [128, 1])
            eps_col = sb(es, "eps_col", [128, 1])
            Hb = sb(es, "Hb", [128, 8, 2, 128], BF16)

            def aff(dst_bf, init, pattern, cmp, fill, base, cm):
                S.op("pool", lambda e: e.memset(cf[:], init), writes=[cf])
                S.op("pool", lambda e: e.affine_select(out=cf[:], in_=cf[:], pattern=pattern, compare_op=cmp, fill=fill, base=base, channel_multiplier=cm), reads=[cf], writes=[cf])
                if dst_bf is not None:
                    S.op("pool", lambda e: e.tensor_copy(out=dst_bf[:], in_=cf[:]), reads=[cf], writes=[dst_bf])

            aff(ident, 0.0, [[-1, 128]], ALU.not_equal, 1.0, 0, 1)
            aff(J, 0.0, [[1, 128]], ALU.not_equal, 1.0, -127, 1)
            aff(Mdiag, 0.0, [[1, 128]], ALU.is_ge, NEG, 0, -1)
            S.op("pool", lambda e: e.memset(U[:], 1.0), writes=[U])
            S.op("pool", lambda e: e.affine_select(out=U[:], in_=U[:], pattern=[[1, 128]], compare_op=ALU.is_ge, fill=0.0, base=0, channel_multiplier=-1), reads=[U], writes=[U])
            S.op("pool", lambda e: e.memset(ones_f[:], 1.0), writes=[ones_f])
            S.op("pool", lambda e: e.memset(ones_b[:], 1.0), writes=[ones_b])
            S.op("pool", lambda e: e.memset(Mask0[:], 0.0), writes=[Mask0])
            S.op("pool", lambda e: e.memset(Mask0[0:64, 64:128], NEG), writes=[Mask0])
            S.op("pool", lambda e: e.memset(mhalf[:], -0.5), writes=[mhalf])
            S.op("pool", lambda e: e.memset(one_col[:], 1.0), writes=[one_col])
            S.op("pool", lambda e: e.memset(eps_col[:], 1e-6), writes=[eps_col])

            def bc_load(name, src, n):
                t = sb(es, name, [128, n])
                S.dma("sp", t[:], src[0:1, 0:n].broadcast_to([128, n]), writes=[t])
                return t

            gqa_bc = bc_load("gqa_bc", g_qa, 64); gka_bc = bc_load("gka_bc", g_ka, 64)
            gqb_bc = bc_load("gqb_bc", g_qb, 64); gkb_bc = bc_load("gkb_bc", g_kb, 64)
            gmq_bc = bc_load("gmq_bc", g_mq, 256); gmk_bc = bc_load("gmk_bc", g_mk, 256)
            bf_bc = bc_load("bf_bc", b_f, 8)

            def col_load(name, src):
                t = sb(es, name, [128, 8])
                S.dma("sp", t[:], src.rearrange("o (c p) -> p (o c)", p=128), writes=[t], allow_slow_non_contiguous=True)
                return t

            g1c = col_load("g1c", g1); g2c = col_load("g2c", g2); g3c = col_load("g3c", g3); gmc = col_load("gmc", gmem)
            wc = sb(es, "wc", [128, 3, NF])
            for j in range(3):
                S.dma("sp", wc[:, j, :], w_conv[j:j + 1, :].rearrange("o (c p) -> p (o c)", p=128), writes=[wc], allow_slow_non_contiguous=True)
            bcv = sb(es, "bcv", [128, NF])
            S.dma("sp", bcv[:], b_conv.rearrange("o (c p) -> p (o c)", p=128), writes=[bcv], allow_slow_non_contiguous=True)

            stg = None

            def alloc_stg(st, tag, k):
                return Rot([sb(st, "stg%s%d" % (tag, i), [128, 512]) for i in range(k)])
            cast_i = [0]

            def cast(out_ap, in_ap, gcol_ap, reads, writes):
                e = "act" if cast_i[0] % 2 == 0 else "dve"
                cast_i[0] += 1
                if e == "act":
                    if gcol_ap is None:
                        S.op(e, lambda en: en.copy(out=out_ap, in_=in_ap), reads=reads, writes=writes)
                    else:
                        S.op(e, lambda en: en.activation(out=out_ap, in_=in_ap, func=AF.Copy, scale=gcol_ap), reads=reads, writes=writes)
                elif gcol_ap is None:
                    S.op(e, lambda en: en.tensor_copy(out=out_ap, in_=in_ap), reads=reads, writes=writes)
                else:
                    S.op(e, lambda en: en.tensor_scalar(out=out_ap, in0=in_ap, scalar1=gcol_ap, scalar2=None, op0=ALU.mult), reads=reads, writes=writes)

            def load_w(dst, src, nch, ncols, gcol):
                for c in range(nch):
                    for c0 in range(0, ncols, 512):
                        c1 = min(ncols, c0 + 512)
                        s = stg.nxt()
                        S.dma("sp", s[:, 0:c1 - c0], src[c * 128:(c + 1) * 128, c0:c1], writes=[s])
                        cast(dst[:, c, c0:c1], s[:, 0:c1 - c0], None if gcol is None else gcol[:, c:c + 1], [s] + ([gcol] if gcol is not None else []), [dst])

            xt_r = Rot([sb(es, "xt%d" % i, [128, D]) for i in range(2)])
            sqn = sb(es, "sqn", [128, D], BF16)
            xn_r = Rot([sb(es, "xn%d" % i, [128, D], BF16) for i in range(2)])
            st_r = Rot([sb(es, "st%d" % i, [128, 2]) for i in range(4)])

            EV = ["act"]

            def _cp(e):
                return e.copy if EV[0] == "act" else e.tensor_copy

            def evac(fn, reads=(), writes=()):
                S.op(EV[0], fn, reads=reads, writes=writes)

            RS = ["auto"]

            def rstd(out_ap, ss_ap, dim, n, k, buf):
                if EV[0] == "act" and RS[0] != "pool":
                    S.op("act", lambda e: e.activation(out=out_ap, in_=ss_ap, func=AF.Ln, scale=1.0 / dim, bias=eps_col[0:n, :]), reads=[buf, eps_col], writes=[buf])
                    S.op("act", lambda e: e.activation(out=out_ap, in_=out_ap, func=AF.Exp, scale=-0.5), reads=[buf], writes=[buf])
                else:
                    S.op("pool", lambda e: e.tensor_scalar(out=ss_ap, in0=ss_ap, scalar1=1.0 / dim, scalar2=1e-6, op0=ALU.mult, op1=ALU.add), reads=[buf], writes=[buf])
                    S.op("pool", lambda e: e.tensor_tensor(out=out_ap, in0=ss_ap, in1=mhalf[0:n, 0:k], op=ALU.pow), reads=[buf, mhalf], writes=[buf])

            def norm_a1(x, n):
                st = st_r.nxt()
                S.op("dve", lambda e: e.tensor_tensor(out=sqn[0:n, :], in0=x[0:n, :], in1=x[0:n, :], op=ALU.mult), reads=[x], writes=[sqn])
                S.op("dve", lambda e: e.tensor_reduce(out=st[0:n, 0:1], in_=sqn[0:n, :], axis=AX.X, op=ALU.add), reads=[sqn], writes=[st])
                rstd(st[0:n, 1:2], st[0:n, 0:1], D, n, 1, st)
                return st

            def norm_a2(x, n, st):
                xn = xn_r.nxt()
                if EV[0] == "act" and RS[0] != "pool":
                    S.op("act", lambda e: e.activation(out=xn[0:n, :], in_=x[0:n, :], func=AF.Copy, scale=st[0:n, 1:2]), reads=[x, st], writes=[xn])
                else:
                    S.op("dve", lambda e: e.tensor_scalar(out=xn[0:n, :], in0=x[0:n, :], scalar1=st[0:n, 1:2], scalar2=None, op0=ALU.mult), reads=[x, st], writes=[xn])
                return xn

            def norm_a(x, n):
                return norm_a2(x, n, norm_a1(x, n))

            def norm_b(xn, n, dst_ap, dst_buf):
                for c in range(8):
                    S.op("pe", lambda e: e.transpose(out=ptb[:, c * n:(c + 1) * n], in_=xn[0:n, c * 128:(c + 1) * 128], identity=ident[0:n, 0:n]), reads=[xn, ident], writes=[ptb], sig=(c == 7))
                evac(lambda e: _cp(e)(out=dst_ap, in_=ptb[:, 0:8 * n].rearrange("p (c t) -> p c t", c=8)), reads=[ptb], writes=[dst_buf])

            def norm_T(x, n, dst_ap, dst_buf):
                norm_b(norm_a(x, n), n, dst_ap, dst_buf)

            tcp_r = sqh_r = knb_r = None

            def alloc_hn(st, tag):
                return (Rot([sb(st, "tcp%s%d" % (tag, i), [128, 512]) for i in range(4)]),
                        Rot([sb(st, "sqh%s%d" % (tag, i), [128, 512], BF16) for i in range(2)]),
                        Rot([sb(st, "knb%s%d" % (tag, i), [128, 512], BF16) for i in range(4)]))
            hs_r = Rot([sb(es, "hs%d" % i, [128, 16]) for i in range(6)])

            def hn_a(src, n, nh):
                hd = 512 // nh
                tcp = tcp_r.nxt(); sqh = sqh_r.nxt(); hs = hs_r.nxt()
                v3 = lambda t: t[0:n, :].rearrange("p (h d) -> p h d", h=nh)
                evac(lambda e: _cp(e)(out=tcp[0:n, :], in_=src[0:n, :]), reads=[src], writes=[tcp])
                S.op("dve", lambda e: e.tensor_tensor(out=sqh[0:n, :], in0=tcp[0:n, :], in1=tcp[0:n, :], op=ALU.mult), reads=[tcp], writes=[sqh])
                S.op("dve", lambda e: e.tensor_reduce(out=hs[0:n, 0:nh], in_=v3(sqh), axis=AX.X, op=ALU.add), reads=[sqh], writes=[hs])
                rstd(hs[0:n, 8:8 + nh], hs[0:n, 0:nh], hd, n, nh, hs)
                return (tcp, hs, n, nh)

            def hn_b(stt_, gbc, want_f32):
                (tcp, hs, n, nh) = stt_
                hd = 512 // nh
                knb = knb_r.nxt()
                v3 = lambda t: t[0:n, :].rearrange("p (h d) -> p h d", h=nh)
                if EV[0] == "act" and nh == 2:
                    for hh_ in range(2):
                        S.op("act", lambda e: e.activation(out=tcp[0:n, hh_ * hd:(hh_ + 1) * hd], in_=tcp[0:n, hh_ * hd:(hh_ + 1) * hd], func=AF.Copy, scale=hs[0:n, 8 + hh_:9 + hh_]), reads=[tcp, hs], writes=[tcp])
                else:
                    S.op("dve", lambda e: e.tensor_tensor(out=v3(tcp), in0=v3(tcp), in1=hs[0:n, 8:8 + nh].unsqueeze(2).broadcast_to([n, nh, hd]), op=ALU.mult), reads=[tcp, hs], writes=[tcp])
                gb3 = gbc[0:n, 0:hd].unsqueeze(1).broadcast_to([n, nh, hd])
                if want_f32:
                    S.op("dve", lambda e: e.tensor_tensor(out=v3(tcp), in0=v3(tcp), in1=gb3, op=ALU.mult), reads=[tcp, gbc], writes=[tcp])
                    evac(lambda e: _cp(e)(out=knb[0:n, :], in_=tcp[0:n, :]), reads=[tcp], writes=[knb])
                    return tcp, knb
                S.op("dve", lambda e: e.tensor_tensor(out=v3(knb), in0=v3(tcp), in1=gb3, op=ALU.mult), reads=[tcp, gbc], writes=[knb])
                return None, knb

            def headnorm(src, n, nh, gbc, want_f32):
                return hn_b(hn_a(src, n, nh), gbc, want_f32)

            def transp_to(src_bf, n, nblk, dst_ap, dst_bufs):
                for i in range(nblk):
                    S.op("pe", lambda e: e.transpose(out=ptb[:, i * n:(i + 1) * n], in_=src_bf[0:n, i * 128:(i + 1) * 128], identity=ident[0:n, 0:n]), reads=[src_bf, ident], writes=[ptb], sig=(i == nblk - 1))
                evac(lambda e: _cp(e)(out=dst_ap, in_=ptb[:, 0:nblk * n].rearrange("p (c t) -> p c t", c=nblk)), reads=[ptb], writes=dst_bufs)

            def mm(out, lhsT, rhs, start, stop, reads, writes, sig=True, **kw):
                S.op("pe", lambda e: e.matmul(out, lhsT=lhsT, rhs=rhs, start=start, stop=stop, **kw), reads=reads, writes=writes, sig=sig)

            pm = Rot([bank[0], bank[1]])
            with contextlib.ExitStack() as st0:
                Esb = sb(st0, "Esb", [8, 512]); t256 = sb(st0, "t256", [8, 1]); Hf = sb(st0, "Hf", [128, 8, 2, 128])
                S.dma("sp", Esb[:, 0:257], rel[:, :], writes=[Esb])
                S.op("dve", lambda e: e.tensor_copy(out=t256[:], in_=Esb[:, 256:257]), reads=[Esb], writes=[t256])
                S.op("dve", lambda e: e.tensor_copy(out=Esb[:, 257:512], in_=t256[:, 0:1].broadcast_to([8, 255])), reads=[t256], writes=[Esb])
                S.op("dve", lambda e: e.tensor_scalar(out=Esb[:], in0=Esb[:], scalar1=t256[:, 0:1], scalar2=8.0, op0=ALU.subtract, op1=ALU.mult), reads=[Esb, t256], writes=[Esb])
                eb = Buf("Esc")
                S.dma("pool", Esc[:, :], Esb[:], reads=[Esb], writes=[eb])
                stg = alloc_stg(st0, "a", 8)
                wst_r = Rot([sb(st0, "wst%d" % i, [128, 8, 512], BF16) for i in range(2)])
                for gi in range(7):
                    c0 = gi * 512
                    w = min(512, INC - c0)
                    wst = wst_r.nxt()
                    for c in range(8):
                        s = stg.nxt()
                        S.dma("sp", s[:, 0:w], w_in[c * 128:(c + 1) * 128, c0:c0 + w], writes=[s])
                        cast(wst[:, c, 0:w], s[:, 0:w], g1c[:, c:c + 1], [s, g1c], [wst])
                    S.dma("pool", WinS[gi].rearrange("p (c n) -> p c n", c=8)[:, :, 0:w], wst[:, :, 0:w], reads=[wst], writes=[WinSb[gi]])
                for h in range(8):
                    S.dma("sp", Hf[:, h, :, :], bass.AP(Esc_h, h * 512 + 1, [[1, 128], [128, 2], [1, 128]]), reads=[eb], writes=[Hf])
                S.op("dve", lambda e: e.tensor_copy(out=Hb[:], in_=Hf[:]), reads=[Hf], writes=[Hb])
                S.op("pool", lambda e: e.memset(Hb[0:64, :, 0, 0:64], NEG), reads=[Hb], writes=[Hb])
                chk(2)
                S.barrier()

            with contextlib.ExitStack() as st1:
                tcp_r, sqh_r, knb_r = alloc_hn(st1, "a")
                EV[0] = "dve"
                wg_r = Rot([sb(st1, "wg%d" % i, [128, 8, 512], BF16) for i in range(2)])
                KTb = sb(st1, "KTb", [128, 4, SEQ + NS], BF16); KTb_b = [Buf("KTb%d" % j) for j in range(NT + 1)]
                Vst = sb(st1, "Vst", [128, NT + 1, 8, 65], BF16); Vst_b = [Buf("Vst%d" % j) for j in range(NT + 1)]
                KTa = sb(st1, "KTa", [128, 4, 1024], BF16); KTa_b = [Buf("KTa%d" % j) for j in range(8)]
                VA = sb(st1, "VA", [128, 8, 8, 65], BF16); VA_b = [Buf("VA%d" % j) for j in range(8)]
                S.op("pool", lambda e: e.memset(Vst[:, :, :, 64:65], 1.0), writes=Vst_b)
                S.op("pool", lambda e: e.memset(VA[:, :, :, 64:65], 1.0), writes=VA_b)
                hT = sb(st1, "hT", [128, 8, 512], BF16); hT_b = [Buf("hT%d" % t) for t in range(4)]
                QTb_r = Rot([sb(st1, "QTb%d" % i, [128, 8, 512], BF16) for i in range(2)])
                QTa = sb(st1, "QTa", [128, 8, 512], BF16)
                for qt_ in QTb_r.tls + [QTa]:
                    S.op("pool", lambda e: e.memset(qt_[:], 0.0), writes=[qt_])
                vf_r = Rot([sb(st1, "vf%d" % i, [128, 512]) for i in range(2)])
                PT_r = Rot([sb(st1, "PT%d" % i, [128, 512], BF16) for i in range(4)])
                PTa_r = Rot([sb(st1, "PTa%d" % i, [128, 5, 128], BF16) for i in range(2)])
                Oblk = sb(st1, "Oblk", [128, 4, D], BF16)
                lfb_r = Rot([sb(st1, "lfb%d" % i, [128, 4, 8]) for i in range(2)])
                lft = sb(st1, "lft", [128, 8]); lfe = sb(st1, "lfe", [128, 8])
                c_all = sb(st1, "c_all", [128, NT + 1, 8])
                S.op("pool", lambda e: e.memset(c_all[:], 0.0), writes=[c_all])
                nbias_r = Rot([sb(st1, "nbias%d" % i, [128, NT + 1, 8]) for i in range(2)])
                carry = sb(st1, "carry", [128, NT + 2, 8])
                rec_r = Rot([sb(st1, "rec%d" % i, [128, 4, 1]) for i in range(4)])
                lfc = sb(st1, "lfc", [128, NT, 8])
                psT = Rot([bank[2], bank[3], bank[4]])
                po = Rot([bank[5], bank[6]])

                def in_proj_group(gi, wg, hslice, hbuf, n):
                    pb = pm.nxt()
                    w = min(512, INC - gi * 512)
                    for c in range(8):
                        mm(pb[0:n, 0:w], hslice(c), wg[:, c, 0:w], c == 0, c == 7, [hbuf, wg], [pb], sig=(c == 7))
                    return pb

                def logf_from(pb, n, dst_ap, dst_buf):
                    S.op("dve", lambda e: e.tensor_tensor(out=lft[0:n, :], in0=pb[0:n, 0:8], in1=bf_bc[0:n, :], op=ALU.add), reads=[pb, bf_bc], writes=[lft])
                    S.op("act", lambda e: e.activation(out=lfe[0:n, :], in_=lft[0:n, :], func=AF.Exp, scale=-1.0), reads=[lft], writes=[lfe])
                    S.op("act", lambda e: e.activation(out=lft[0:n, :], in_=lfe[0:n, :], func=AF.Ln, bias=one_col[0:n, :], scale=1.0), reads=[lfe, one_col], writes=[lft])
                    S.op("dve", lambda e: e.tensor_scalar(out=dst_ap, in0=lft[0:n, :], scalar1=-1.0, scalar2=None, op0=ALU.mult), reads=[lft], writes=[dst_buf])

                def tile_groups(gi, pb, n, QTbt, qcol, kt_idx, a_slot, outs, lf_ap, lf_buf):
                    o_ak, o_av, o_bk, o_bv = outs
                    if gi in (0, 1, 3, 4):
                        sta = hn_a(pb, n, 8)
                        box = {}

                        def part_b():
                            if gi == 0:
                                box["r"] = hn_b(sta, gqa_bc, False)
                            elif gi == 3:
                                box["r"] = hn_b(sta, gqb_bc, False)
                            elif gi == 1:
                                box["r"] = hn_b(sta, gka_bc, True)
                                if o_ak is not None:
                                    S.dma("pool", o_ak, box["r"][0][0:n, :], reads=[box["r"][0]], writes=[outb])
                            else:
                                box["r"] = hn_b(sta, gkb_bc, True)
                                S.dma("pool", o_bk, box["r"][0][0:n, :], reads=[box["r"][0]], writes=[outb])

                        def part_t():
                            kb = box["r"][1]
                            if gi == 0:
                                transp_q(kb, n, QTa, qcol)
                            elif gi == 3:
                                transp_q(kb, n, QTbt, qcol)
                            elif gi == 1:
                                transp_to(kb, n, 4, KTa[:, :, a_slot * 128:a_slot * 128 + n], [KTa_b[a_slot]])
                            else:
                                transp_to(kb, n, 4, KTb[:, :, kt_idx * 128:kt_idx * 128 + n], [KTb_b[kt_idx]])
                        return [part_b, part_t]
                    elif gi in (2, 5):
                        vf = vf_r.nxt()
                        evac(lambda e: _cp(e)(out=vf[0:n, :], in_=pb[0:n, :]), reads=[pb], writes=[vf])
                        o = o_av if gi == 2 else o_bv
                        if o is not None:
                            S.dma("pool", o, vf[0:n, :], reads=[vf], writes=[outb])
                        if gi == 2:
                            S.op("dve", lambda e: e.tensor_copy(out=VA[0:n, a_slot, :, 0:64], in_=vf[0:n, :].rearrange("p (h d) -> p h d", h=8)), reads=[vf], writes=[VA_b[a_slot]])
                        else:
                            S.op("dve", lambda e: e.tensor_copy(out=Vst[0:n, kt_idx, :, 0:64], in_=vf[0:n, :].rearrange("p (h d) -> p h d", h=8)), reads=[vf], writes=[Vst_b[kt_idx]])
                    else:
                        logf_from(pb, n, lf_ap, lf_buf)
                    return None

                def transp_q(src_bf, n, QT, qcol):
                    for i in range(4):
                        S.op("pe", lambda e: e.transpose(out=ptb[:, i * n:(i + 1) * n], in_=src_bf[0:n, i * 128:(i + 1) * 128], identity=ident[0:n, 0:n]), reads=[src_bf, ident], writes=[ptb], sig=(i == 3))
                    for e2 in range(2):
                        evac(lambda e: _cp(e)(out=QT.t[:, :, :].rearrange("p (a e) t -> p a e t", e=2)[e2 * 64:(e2 + 1) * 64, :, e2, qcol:qcol + n],
                                                     in_=ptb[e2 * 64:(e2 + 1) * 64, 0:4 * n].rearrange("p (c t) -> p c t", c=4)), reads=[ptb], writes=[QT])

                pend = []

                def defer(fns):
                    for ent in list(pend):
                        ent.pop(0)()
                        if not ent:
                            pend.remove(ent)
                    if fns:
                        pend.append(list(fns))

                def flush():
                    while pend:
                        defer(None)

                def a_attn(q0, nq, ktiles, trow):
                    jjs = [k[2] for k in ktiles]
                    jlo = min(j for j in jjs if j >= 1)
                    kts = sorted(ktiles, key=lambda k: (k[2] == 0, k[2]))
                    state = {}

                    def qk(h):
                        pr, bp = h // 2, 64 * (h % 2)
                        pX = psT.nxt()
                        pY = psT.nxt() if 0 in jjs else None
                        for (slot, nk, jj) in kts:
                            dst = pY[0:nk, 0:nq] if jj == 0 else pX[0:nk, (jj - 1) * 128:(jj - 1) * 128 + nq]
                            dbuf = pY if jj == 0 else pX
                            extra = jj in (3, 4) or (jj == 0 and nq == 128)
                            mm(dst, KTa[:, pr, slot * 128:slot * 128 + nk], QTa[:, h, q0:q0 + nq], True, not extra,
                               [KTa_b[slot], QTa], [dbuf], sig=not extra)
                            if jj == 3:
                                mm(dst, J[:, :], Hb[:, h, 1, 0:nq], False, True, [J, Hb], [dbuf])
                            elif jj == 4:
                                mm(dst, J[:, 0:nk], Hb[:, h, 0, 0:nq], False, True, [J, Hb], [dbuf])
                            elif jj == 0 and nq == 128:
                                mm(dst, ident[:, :], Mask0[:, :], False, True, [ident, Mask0], [dbuf])
                        PTa = PTa_r.nxt()
                        nk4 = [k[1] for k in kts if k[2] == 4][0]
                        jhi = 5 if nk4 == 128 else 4
                        if jhi > jlo:
                            S.op("act", lambda e: e.activation(out=PTa[:, jlo:jhi, 0:nq], in_=pX[:, (jlo - 1) * 128:(jhi - 1) * 128].rearrange("p (j c) -> p j c", c=128)[:, :, 0:nq], func=AF.Exp, scale=0.125),
                                 reads=[pX], writes=[PTa])
                        if jhi == 4:
                            S.op("act", lambda e: e.activation(out=PTa[0:nk4, 4, 0:nq], in_=pX[0:nk4, 384:384 + nq], func=AF.Exp, scale=0.125), reads=[pX], writes=[PTa])
                        if pY is not None:
                            S.op("act", lambda e: e.activation(out=PTa[:, 0, 0:nq], in_=pY[:, 0:nq], func=AF.Exp, scale=0.125), reads=[pY], writes=[PTa])
                        return PTa

                    def pv(h, PTa):
                        hg, hh = h // 4, h % 4
                        if hh == 0:
                            state["pob"] = po.nxt()
                        pob = state["pob"]
                        for i, (slot, nk, jj) in enumerate(ktiles):
                            last = i == len(ktiles) - 1
                            mm(pob[0:nq, hh * 65:(hh + 1) * 65], PTa[0:nk, jj, 0:nq], VA[0:nk, slot, h, :], i == 0, last, [PTa, VA_b[slot]], [pob], sig=last)
                        if hh == 3:
                            rec = rec_r.nxt()
                            p3 = pob[0:nq, 0:260].rearrange("p (t c) -> p t c", c=65)
                            S.op("dve", lambda e: e.reciprocal(out=rec[0:nq, :, :], in_=p3[:, :, 64:65]), reads=[pob], writes=[rec])
                            S.op("dve", lambda e: e.tensor_tensor(out=Oblk[0:nq, trow, hg * 256:(hg + 1) * 256].rearrange("p (h d) -> p h d", h=4), in0=p3[:, :, 0:64],
                                                                  in1=rec[0:nq, :, :].broadcast_to([nq, 4, 64]), op=ALU.mult), reads=[pob, rec], writes=[Oblk])

                    prev = None
                    for h in range(8):
                        PTa = qk(h)
                        if prev is not None:
                            pv(*prev)
                        prev = (h, PTa)
                        yield
                    pv(*prev)

                def b_attn(QTbt, nq, ktiles, nbias):
                    ntq = (nq + 127) // 128
                    qw = min(nq, 128)
                    pobs = {}

                    def qk(h, kt):
                        (j, nk, c0, diag) = kt
                        pr, bp = h // 2, 64 * (h % 2)
                        pst = psT.nxt()
                        mm(pst[0:nk, c0:nq], KTb[:, pr, j * 128:j * 128 + nk], QTbt[:, h, c0:nq], True, not diag,
                           [KTb_b[j], QTbt], [pst], sig=not diag)
                        if diag:
                            w = min(128, nq - c0)
                            mm(pst[0:nk, c0:c0 + w], ident[:, 0:nk], Mdiag[:, 0:w], False, True, [ident, Mdiag], [pst])
                        PT = PT_r.nxt()
                        S.op("act", lambda e: e.activation(out=PT[0:nk, c0:nq], in_=pst[0:nk, c0:nq], func=AF.Exp, scale=0.125, bias=nbias[0:nk, j, h:h + 1]),
                             reads=[pst, nbias], writes=[PT])
                        return PT

                    def pv(h, kt, PT, first, last):
                        (j, nk, c0, diag) = kt
                        if first:
                            pobs[h] = po.nxt()
                            assert c0 == 0
                        pob = pobs[h]
                        for t in range(c0 // 128, ntq):
                            mm(pob[0:qw, t * 65:(t + 1) * 65], PT[0:nk, t * 128:t * 128 + qw], Vst[0:nk, j, h, :], bool(first and t == 0), False,
                               [PT, Vst_b[j]], [pob], sig=(t == ntq - 1), skip_group_check=True)
                        if last:
                            rec = rec_r.nxt()
                            p3 = pob[0:qw, 0:ntq * 65].rearrange("p (t c) -> p t c", c=65)
                            S.op("dve", lambda e: e.reciprocal(out=rec[0:qw, 0:ntq, :], in_=p3[:, :, 64:65]), reads=[pob], writes=[rec])
                            S.op("dve", lambda e: e.tensor_tensor(out=Oblk[0:qw, 0:ntq, 512 + h * 64:512 + (h + 1) * 64], in0=p3[:, :, 0:64],
                                                                  in1=rec[0:qw, 0:ntq, :].broadcast_to([qw, ntq, 64]), op=ALU.mult), reads=[pob, rec], writes=[Oblk])

                    items = [(h, kt, i == 0, i == len(ktiles) - 1) for h in range(8) for i, kt in enumerate(ktiles)]
                    pq = []
                    for (h, kt, first, last) in items:
                        PT = qk(h, kt)
                        pq.append((h, kt, PT, first, last))
                        if len(pq) > 2:
                            pv(*pq.pop(0))
                        yield
                    while pq:
                        pv(*pq.pop(0))

                def run(gen):
                    for _ in gen:
                        pass

                def interleave(g1, n1, g2, n2):
                    acc = 0
                    done2 = False
                    for _ in g1:
                        acc += n2
                        while acc >= n1 and not done2:
                            acc -= n1
                            try:
                                next(g2)
                            except StopIteration:
                                done2 = True
                    if not done2:
                        run(g2)

                def o_out(trow, n, gidx):
                    r0 = gidx * 128 if gidx < NT else SEQ + (gidx - NT) * NS
                    S.dma("pool", Osc[r0:r0 + n, :], Oblk[0:n, trow, :], reads=[Oblk], writes=[Ob[gidx]])

                def cumsum_tiles(lf_ap, lf_buf, nt, n, j0, carry_idx):
                    pcs = pm.nxt(); ptot = pm.nxt()
                    mm(pcs[0:n, 0:nt * 8], U[0:n, 0:n], lf_ap, True, True, [U, lf_buf], [pcs])
                    mm(ptot[:, 0:nt * 8], ones_f[0:n, :], lf_ap, True, True, [ones_f, lf_buf], [ptot])
                    for i in range(nt):
                        S.op("dve", lambda e: e.tensor_tensor(out=c_all[0:n, j0 + i, :], in0=pcs[0:n, i * 8:(i + 1) * 8], in1=carry[0:n, carry_idx + i, :], op=ALU.add), reads=[pcs, carry], writes=[c_all])
                        S.op("dve", lambda e: e.tensor_tensor(out=carry[:, carry_idx + i + 1, :], in0=ptot[:, i * 8:(i + 1) * 8], in1=carry[:, carry_idx + i, :], op=ALU.add), reads=[ptot, carry], writes=[carry])

                def make_nbias(nj, carry_idx):
                    nb = nbias_r.nxt()
                    S.op("dve", lambda e: e.scalar_tensor_tensor(out=nb[:, 0:nj, :], in0=c_all[:, 0:nj, :], scalar=-1.0, in1=carry[:, carry_idx:carry_idx + 1, :].broadcast_to([128, nj, 8]), op0=ALU.mult, op1=ALU.add),
                         reads=[c_all, carry], writes=[nb])
                    return nb

                for b in range(2):
                    n = NS
                    for j in range(NT):
                        kc = tcp_r.nxt()
                        S.dma("sp", kc[:], cbk[b, j * 128:(j + 1) * 128, :], writes=[kc])
                        kb = knb_r.nxt()
                        cast(kb[:], kc[:], None, [kc], [kb])
                        transp_to(kb, 128, 4, KTb[:, :, j * 128:(j + 1) * 128], [KTb_b[j]])
                        vc = tcp_r.nxt()
                        S.dma("sp", vc[:], cbv[b, j * 128:(j + 1) * 128, :], writes=[vc])
                        cast(Vst[:, j, :, 0:64], vc[:].rearrange("p (h d) -> p h d", h=8), None, [vc], [Vst_b[j]])
                    for j in range(4):
                        kc = tcp_r.nxt()
                        S.dma("sp", kc[:], cak[b, j * 128:(j + 1) * 128, :], writes=[kc])
                        kb = knb_r.nxt()
                        cast(kb[:], kc[:], None, [kc], [kb])
                        transp_to(kb, 128, 4, KTa[:, :, j * 128:(j + 1) * 128], [KTa_b[j]])
                        vc = tcp_r.nxt()
                        S.dma("sp", vc[:], cav[b, j * 128:(j + 1) * 128, :], writes=[vc])
                        cast(VA[:, j, :, 0:64], vc[:].rearrange("p (h d) -> p h d", h=8), None, [vc], [VA_b[j]])
                    S.dma("sp", lfc[:], cbl[b].rearrange("(j p) h -> p j h", p=128), writes=[lfc])
                    S.op("pool", lambda e: e.memset(carry[:, 0, :], 0.0), writes=[carry])
                    chk(3)
                    cumsum_tiles(lfc[:, :, :].rearrange("p j h -> p (j h)"), lfc, NT, 128, 0, 0)
                    chk(31)
                    x = xt_r.nxt()
                    S.dma("sp", x[0:n, :], xs[b * NS:(b + 1) * NS, :], writes=[x])
                    norm_T(x, n, hT[:, :, 0:n], hT_b[0])
                    chk(32)
                    QTbt = QTb_r.nxt()
                    lfb = lfb_r.nxt()
                    rows = slice(b * NS, (b + 1) * NS)
                    for gi in range(7):
                        wg = wg_r.nxt()
                        S.dma("sp", wg[:, :, 0:min(512, INC - gi * 512)], WinS[gi].rearrange("p (c n) -> p c n", c=8)[:, :, 0:min(512, INC - gi * 512)], reads=[WinSb[gi]], writes=[wg])
                        pb = in_proj_group(gi, wg, lambda c: hT[:, c, 0:n], hT_b[0], n)
                        if gi == 0:
                            chk(33)
                        defer(tile_groups(gi, pb, n, QTbt, 0, NT, 4, (aks[rows, :], avs[rows, :], bks[rows, :], bvs[rows, :]), lfb[0:n, 0, :], lfb))
                        chk(34 + gi)
                    flush()
                    S.dma("pool", bls[rows, :], lfb[0:n, 0, :], reads=[lfb], writes=[outb])
                    cumsum_tiles(lfb[0:n, 0, :], lfb, 1, n, NT, NT)
                    nb = make_nbias(NT + 1, NT)
                    chk(4)
                    run(a_attn(0, n, [(0, 128, 0), (1, 128, 1), (2, 128, 2), (3, 128, 3), (4, n, 4)], 0))
                    chk(5)
                    run(b_attn(QTbt, n, [(j, 128, 0, False) for j in range(NT)] + [(NT, n, 0, True)], nb))
                    o_out(0, n, NT + b)
                    chk(6)

                chk(7)
                S.op("pool", lambda e: e.memset(carry[:, 0, :], 0.0), reads=[carry], writes=[carry])
                ipres = {}

                def inproj_gen(I):
                    QTbt = QTb_r.nxt()
                    lfb = lfb_r.nxt()
                    prevn = None
                    for t in range(4):
                        g = 4 * I + t
                        x = xt_r.nxt()
                        S.dma("sp", x[:], xp[g * 128:(g + 1) * 128, :], writes=[x])
                        xn_ = norm_a(x, 128)
                        if prevn is not None:
                            norm_b(prevn[0], 128, hT[:, :, prevn[1] * 128:(prevn[1] + 1) * 128], hT_b[prevn[1]])
                        prevn = (xn_, t)
                        yield
                    norm_b(prevn[0], 128, hT[:, :, prevn[1] * 128:(prevn[1] + 1) * 128], hT_b[prevn[1]])
                    for gi in (3, 4, 5, 6, 0, 1, 2):
                        wg = wg_r.nxt()
                        S.dma("sp", wg[:, :, 0:min(512, INC - gi * 512)], WinS[gi].rearrange("p (c n) -> p c n", c=8)[:, :, 0:min(512, INC - gi * 512)], reads=[WinSb[gi]], writes=[wg])
                        for t in range(4):
                            g = 4 * I + t
                            rows = slice(g * 128, (g + 1) * 128)
                            pb = in_proj_group(gi, wg, lambda c: hT[:, c, t * 128:(t + 1) * 128], hT_b[t], 128)
                            if g >= 28:
                                ar = slice((g - 28) * 128, (g - 27) * 128)
                                outs = (akp[ar, :], avp[ar, :], bkp[rows, :], bvp[rows, :])
                            else:
                                outs = (None, None, bkp[rows, :], bvp[rows, :])
                            defer(tile_groups(gi, pb, 128, QTbt, t * 128, g, g % 8, outs, lfb[:, t, :], lfb))
                            yield
                    flush()
                    S.dma("pool", blp[I * 512:(I + 1) * 512, :].rearrange("(t p) h -> p t h", p=128), lfb[:, :, :], reads=[lfb], writes=[outb])
                    cumsum_tiles(lfb[:, :, :].rearrange("p t h -> p (t h)"), lfb, 4, 128, 4 * I, 4 * I)
                    nb = make_nbias(4 * I + 4, 4 * I)
                    ipres[I] = (QTbt, nb)

                run(inproj_gen(0))
                for I in range(8):
                    QTbt, nb = ipres[I]
                    def attn_chain(I=I, QTbt=QTbt, nb=nb):
                        for t in range(4):
                            g = 4 * I + t
                            ktsa = [(gg % 8, 128, gg - g + 4) for gg in range(max(0, g - 4), g + 1)]
                            yield from a_attn(t * 128, 128, ktsa, t)
                        ktsb = [(j, 128, 0, False) for j in range(4 * I)] + [(4 * I + t, 128, 128 * t, True) for t in range(4)]
                        yield from b_attn(QTbt, 512, ktsb, nb)
                    nsteps = 32 + 8 * (4 * I + 4)
                    if I < 7:
                        interleave(attn_chain(), nsteps, inproj_gen(I + 1), 33)
                    else:
                        run(attn_chain())
                    for t in range(4):
                        g = 4 * I + t
                        o_out(t, 128, g)
                    if I == 0:
                        chk(8)
                chk(9)
                S.barrier()

            with contextlib.ExitStack() as st2:
                stg = alloc_stg(st2, "b", 5)
                Wmq = sb(st2, "Wmq", [128, 8, D], BF16); Wmo = sb(st2, "Wmo", [128, 8, D], BF16)
                load_w(Wmq, w_mq, 8, D, g2c)
                load_w(Wmo, w_mo, 8, D, None)
                tcp_r, sqh_r, knb_r = alloc_hn(st2, "b")
                EV[0] = "act"
                Wo = sb(st2, "Wo", [128, 8, D], BF16)
                load_w(Wo, w_o, 8, D, None)
                KmT_p = sb(st2, "KmT_p", [128, 8, 256], BF16)
                Vm_p = sb(st2, "Vm_p", [128, 2, D], BF16)
                with contextlib.ExitStack() as stk:
                    wkv = sb(stk, "wkv", [128, 8, 2 * D], BF16)
                    load_w(wkv, w_mkv, 8, 2 * D, gmc)
                    hTm = sb(stk, "hTm", [128, 8, 128], BF16)
                    vfm_r = Rot([sb(stk, "vfm%d" % i, [128, 512]) for i in range(2)])
                    for a in range(2):
                        x = xt_r.nxt()
                        S.dma("sp", x[:], memp[a * 128:(a + 1) * 128, :], writes=[x])
                        norm_T(x, 128, hTm[:], hTm)
                        for g in range(4):
                            pb = pm.nxt()
                            for c in range(8):
                                mm(pb[:, :], hTm[:, c, :], wkv[:, c, g * 512:(g + 1) * 512], c == 0, c == 7, [hTm, wkv], [pb], sig=(c == 7))
                            if g < 2:
                                kf, kb = headnorm(pb, 128, 2, gmk_bc, True)
                                S.dma("pool", mkp[a * 128:(a + 1) * 128, g * 512:(g + 1) * 512], kf[:], reads=[kf], writes=[outb])
                                transp_to(kb, 128, 4, KmT_p[:, g * 4:(g + 1) * 4, a * 128:(a + 1) * 128], [KmT_p])
                            else:
                                vf = vfm_r.nxt()
                                S.op("act", lambda e: e.copy(out=vf[:], in_=pb[:, :]), reads=[pb], writes=[vf])
                                S.dma("pool", mvp[a * 128:(a + 1) * 128, (g - 2) * 512:(g - 1) * 512], vf[:], reads=[vf], writes=[outb])
                                S.op("dve", lambda e: e.tensor_copy(out=Vm_p[:, a, (g - 2) * 512:(g - 1) * 512], in_=vf[:, :]), reads=[vf], writes=[Vm_p])
                    S.barrier()
                OT_r = Rot([sb(st2, "OT%d" % i, [128, 8, 128], BF16) for i in range(2)])
                ob_r = Rot([sb(st2, "ob%d" % i, [128, D], BF16) for i in range(2)])
                KmT_s = sb(st2, "KmT_s", [128, 8, 256], BF16); Vm_s = sb(st2, "Vm_s", [128, 2, D], BF16)
                hT2s = [sb(st2, "hT2_%d" % i, [128, 8, 512], BF16) for i in range(2)]
                hT2_bs = [[Buf("hT2_%d_%d" % (i, t)) for t in range(4)] for i in range(2)]
                qTs = [sb(st2, "qT_%d" % i, [128, 8, 512], BF16) for i in range(2)]
                qT_bs = [[Buf("qT%d_%d" % (i, t)) for t in range(4)] for i in range(2)]
                x1_r = Rot([sb(st2, "x1t%d" % i, [128, D]) for i in range(8)])
                PTm_r = Rot([sb(st2, "PTm%d" % i, [128, 2, 512], BF16) for i in range(3)])
                OmT = sb(st2, "OmT", [128, 8, 512], BF16)
                rcm_r = Rot([sb(st2, "rcm%d" % i, [128, 512]) for i in range(2)])
                psS = Rot([bank[2], bank[3]])
                psO = Rot([bank[4], bank[5], bank[6]])

                def mb_front(tiles, n, par, out):
                    nt = len(tiles)
                    ntok = nt * n
                    xts = [None] * nt
                    stt = [dict() for _ in range(nt)]
                    hT2 = hT2s[par]; hT2_b = hT2_bs[par]; qT = qTs[par]; qT_b = qT_bs[par]
                    out.update(xts=xts, tiles=tiles, n=n, par=par)

                    def s0(t):
                        (src, osrc, obuf, dst, dbuf) = tiles[t]
                        x = x1_r.nxt()
                        xts[t] = x
                        S.dma("sp", x[0:n, :], src, writes=[x])
                        ob = ob_r.nxt()
                        S.dma("sp", ob[0:n, :], osrc, reads=[obuf], writes=[ob])
                        OT = OT_r.nxt()
                        transp_to(ob, n, 8, OT[:, :, 0:n], [OT])
                        stt[t]["OT"] = OT

                    def s1(t):
                        x = xts[t]
                        OT = stt[t]["OT"]
                        for half in range(2):
                            pb = pm.nxt()
                            for c in range(8):
                                mm(pb[0:n, :], OT[:, c, 0:n], Wo[:, c, half * 512:(half + 1) * 512], c == 0, c == 7, [OT, Wo], [pb], sig=(c == 7))
                            S.op("dve", lambda e: e.tensor_tensor(out=x[0:n, half * 512:(half + 1) * 512], in0=pb[0:n, :], in1=x[0:n, half * 512:(half + 1) * 512], op=ALU.add), reads=[pb, x], writes=[x])
                        stt[t]["xn"] = norm_a(x, n)

                    def s2(t):
                        norm_b(stt[t]["xn"], n, hT2[:, :, t * n:(t + 1) * n], hT2_b[t])

                    def mq(t, half):
                        pb = pm.nxt()
                        for c in range(8):
                            mm(pb[0:n, :], hT2[:, c, t * n:(t + 1) * n], Wmq[:, c, half * 512:(half + 1) * 512], c == 0, c == 7, [hT2_b[t], Wmq], [pb], sig=(c == 7))
                        stt[t]["hn%d" % half] = hn_a(pb, n, 2)

                    def mqb(t, half):
                        _, qb = hn_b(stt[t]["hn%d" % half], gmq_bc, False)
                        stt[t]["qb%d" % half] = qb

                    def qtr(t, half):
                        transp_to(stt[t]["qb%d" % half], n, 4, qT[:, half * 4:(half + 1) * 4, t * n:(t + 1) * n], [qT_b[t]])

                    def s3(t):
                        mq(t, 0)

                    def s4(t):
                        mqb(t, 0)
                        mq(t, 1)

                    def s5(t):
                        mqb(t, 1)
                        qtr(t, 0)

                    def s6(t):
                        qtr(t, 1)

                    stages = [s0, s1, s2, s3, s4, s5, s6]
                    for step in range(nt + len(stages) - 1):
                        for si in reversed(range(len(stages))):
                            t = step - si
                            if 0 <= t < nt:
                                stages[si](t)
                        yield

                def mb_back(st_, KmT, Vm):
                    xts = st_["xts"]; tiles = st_["tiles"]; n = st_["n"]; par = st_["par"]
                    nt = len(tiles)
                    ntok = nt * n
                    qT = qTs[par]; qT_b = qT_bs[par]

                    def hA(h):
                        PTm = PTm_r.nxt()
                        for a in range(2):
                            pst = psS.nxt()
                            for dc in range(2):
                                mm(pst[:, 0:ntok], KmT[:, h * 2 + dc, a * 128:(a + 1) * 128], qT[:, h * 2 + dc, 0:ntok], dc == 0, dc == 1, [KmT] + qT_b[0:nt], [pst], sig=(dc == 1))
                            S.op("act", lambda e: e.activation(out=PTm[:, a, 0:ntok], in_=pst[:, 0:ntok], func=AF.Exp, scale=1.0 / 16), reads=[pst], writes=[PTm])
                        return PTm

                    def hB(h, PTm):
                        psum_ = psO.nxt()
                        for a in range(2):
                            mm(psum_[:, 0:ntok], ones_b[:, :], PTm[:, a, 0:ntok], a == 0, a == 1, [ones_b, PTm], [psum_], sig=(a == 1))
                        rcm = rcm_r.nxt()
                        S.op("dve", lambda e: e.reciprocal(out=rcm[:, 0:ntok], in_=psum_[:, 0:ntok]), reads=[psum_], writes=[rcm])
                        for dc in range(2):
                            pov = psO.nxt()
                            for a in range(2):
                                mm(pov[:, 0:ntok], Vm[:, a, h * 256 + dc * 128:h * 256 + (dc + 1) * 128], PTm[:, a, 0:ntok], a == 0, a == 1, [Vm, PTm], [pov], sig=(a == 1))
                            S.op("dve", lambda e: e.tensor_tensor(out=OmT[:, h * 2 + dc, 0:ntok], in0=pov[:, 0:ntok], in1=rcm[:, 0:ntok], op=ALU.mult), reads=[pov, rcm], writes=[OmT])

                    prev = None
                    for h in range(4):
                        PTm = hA(h)
                        if prev is not None:
                            hB(*prev)
                        prev = (h, PTm)
                        yield
                    hB(*prev)
                    yield
                    for t, (src, osrc, obuf, dst, dbuf) in enumerate(tiles):
                        x = xts[t]
                        for half in range(2):
                            pb = pm.nxt()
                            for c in range(8):
                                mm(pb[0:n, :], OmT[:, c, t * n:(t + 1) * n], Wmo[:, c, half * 512:(half + 1) * 512], c == 0, c == 7, [OmT, Wmo], [pb], sig=(c == 7))
                            S.op("dve", lambda e: e.tensor_tensor(out=x[0:n, half * 512:(half + 1) * 512], in0=pb[0:n, :], in1=x[0:n, half * 512:(half + 1) * 512], op=ALU.add), reads=[pb, x], writes=[x])
                        S.dma("pool", dst, x[0:n, :], reads=[x], writes=[dbuf])
                        yield

                def mem_block(tiles, n, KmT, Vm):
                    st_ = {}
                    run(mb_front(tiles, n, 0, st_))
                    run(mb_back(st_, KmT, Vm))

                if phases >= 2:
                    pblocks = []
                    for I in range(8):
                        tiles = []
                        for t in range(4):
                            g = 4 * I + t
                            tiles.append((xp[g * 128:(g + 1) * 128, :], Osc[g * 128:(g + 1) * 128, :], Ob[g], X2[g * 128:(g + 1) * 128, :], X2b[g]))
                        pblocks.append(tiles)
                    sts = [dict() for _ in range(8)]
                    run(mb_front(pblocks[0], 128, 0, sts[0]))
                    for I in range(8):
                        back = mb_back(sts[I], KmT_p, Vm_p)
                        if I < 7:
                            interleave(back, 9, mb_front(pblocks[I + 1], 128, (I + 1) % 2, sts[I + 1]), 10)
                        else:
                            run(back)
                    for b in range(2):
                        for a in range(2):
                            kc = x1_r.nxt()
                            S.dma("sp", kc[:], cmk[b, a * 128:(a + 1) * 128, :], writes=[kc])
                            for half in range(2):
                                kb = knb_r.nxt()
                                cast(kb[:], kc[:, half * 512:(half + 1) * 512], None, [kc], [kb])
                                transp_to(kb, 128, 4, KmT_s[:, half * 4:(half + 1) * 4, a * 128:(a + 1) * 128], [KmT_s])
                            vc = x1_r.nxt()
                            S.dma("sp", vc[:], cmv[b, a * 128:(a + 1) * 128, :], writes=[vc])
                            cast(Vm_s[:, a, :], vc[:], None, [vc], [Vm_s])
                        r0 = SEQ + b * NS
                        mem_block([(xs[b * NS:(b + 1) * NS, :], Osc[r0:r0 + NS, :], Ob[NT + b], X2[r0:r0 + NS, :], X2b[NT + b])], NS, KmT_s, Vm_s)
                chk(10)
                S.barrier()

            with contextlib.ExitStack() as st3:
                Wup = sb(st3, "Wup", [128, 8, 2 * DFF], BF16)
                Wdn = sb(st3, "Wdn", [128, NF, D], BF16)
                if phases >= 3:
                    with contextlib.ExitStack() as stw:
                        stg = alloc_stg(stw, "c", 8)
                        load_w(Wup, w_up, 8, 2 * DFF, g3c)
                        load_w(Wdn, w_down, NF, D, None)
                        S.barrier()
                TB = 256
                hT3s = [sb(st3, "hT3_%d" % i, [128, 8, TB], BF16) for i in range(2)]
                hT3_b = [[Buf("hT3_%d_%d" % (i, t)) for t in range(2)] for i in range(2)]
                actT = sb(st3, "actT", [128, NF, TB], BF16)
                x2_r = Rot([sb(st3, "x2t%d" % i, [128, D]) for i in range(4)])
                gst_r = Rot([sb(st3, "gst%d" % i, [128, TB + 2]) for i in range(3)])
                cv_r = Rot([sb(st3, "cv%d" % i, [128, TB]) for i in range(3)])
                sl_r = Rot([sb(st3, "sl%d" % i, [128, TB]) for i in range(3)])
                gprev = sb(st3, "gprev", [128, NF, 2])
                psG = Rot([bank[2], bank[3]])
                psV = Rot([bank[4], bank[5], bank[6]])

                def ffn_prep_a(tiles, n):
                    xts, sts = [], []
                    for t, (src, sbuf_, dst) in enumerate(tiles):
                        x = x2_r.nxt()
                        xts.append(x)
                        S.dma("sp", x[0:n, :], src, reads=[sbuf_], writes=[x])
                        sts.append(norm_a1(x, n))
                    return [xts, sts, n]

                def ffn_prep_a2(prep):
                    prep[1] = [norm_a2(x, prep[2], st) for x, st in zip(prep[0], prep[1])]

                def ffn_prep_b(prep, n, hb):
                    for t, xn in enumerate(prep[1]):
                        norm_b(xn, n, hT3s[hb][:, :, t * n:(t + 1) * n], hT3_b[hb][t])

                def ffn_main(tiles, n, xts, hb, hook):
                    nt = len(tiles)
                    ntok = nt * n
                    hT3 = hT3s[hb]
                    hbufs = hT3_b[hb][0:nt]
                    pendm = []
                    pends = []
                    for f in range(NF):
                        pg = psG.nxt(); pv = psV.nxt()
                        hook(f)
                        for c in range(8):
                            mm(pg[:, 0:ntok], Wup[:, c, f * 128:(f + 1) * 128], hT3[:, c, 0:ntok], c == 0, c == 7, [Wup] + hbufs, [pg], sig=(c == 7))
                        for c in range(8):
                            mm(pv[:, 0:ntok], Wup[:, c, DFF + f * 128:DFF + (f + 1) * 128], hT3[:, c, 0:ntok], c == 0, c == 7, [Wup] + hbufs, [pv], sig=(c == 7))
                        gst = gst_r.nxt(); cv = cv_r.nxt(); sl = sl_r.nxt()
                        S.op("pool", lambda e: e.tensor_copy(out=gst[:, 0:2], in_=gprev[:, f, :]), reads=[gprev], writes=[gst])
                        S.op("act", lambda e: e.copy(out=gst[:, 2:2 + ntok], in_=pg[:, 0:ntok]), reads=[pg], writes=[gst])
                        S.op("pool", lambda e: e.tensor_copy(out=gprev[:, f, :], in_=gst[:, ntok:ntok + 2]), reads=[gst], writes=[gprev])
                        S.op("act", lambda e: e.activation(out=cv[:, 0:ntok], in_=gst[:, 0:ntok], func=AF.Identity, scale=wc[:, 0, f:f + 1], bias=bcv[:, f:f + 1]), reads=[gst, wc, bcv], writes=[cv])
                        if pends:
                            pends.pop(0)()
                        S.op("dve", lambda e: e.scalar_tensor_tensor(out=cv[:, 0:ntok], in0=gst[:, 1:1 + ntok], scalar=wc[:, 1, f:f + 1], in1=cv[:, 0:ntok], op0=ALU.mult, op1=ALU.add), reads=[gst, wc, cv], writes=[cv])
                        S.op("dve", lambda e: e.scalar_tensor_tensor(out=cv[:, 0:ntok], in0=gst[:, 2:2 + ntok], scalar=wc[:, 2, f:f + 1], in1=cv[:, 0:ntok], op0=ALU.mult, op1=ALU.add), reads=[gst, wc, cv], writes=[cv])
                        if pendm:
                            pendm.pop(0)()
                        pends.append(lambda cv=cv, sl=sl: S.op("act", lambda e: e.activation(out=sl[:, 0:ntok], in_=cv[:, 0:ntok], func=AF.Silu), reads=[cv], writes=[sl]))
                        pendm.append(lambda f=f, pv=pv, sl=sl: S.op("dve", lambda e: e.tensor_tensor(out=actT[:, f, 0:ntok], in0=pv[:, 0:ntok], in1=sl[:, 0:ntok], op=ALU.mult), reads=[pv, sl], writes=[actT]))
                    while pends:
                        pends.pop(0)()
                    while pendm:
                        pendm.pop(0)()
                    for t, (src, sbuf_, dst) in enumerate(tiles):
                        x = xts[t]
                        for half in range(2):
                            pb = pm.nxt()
                            for f in range(NF):
                                mm(pb[0:n, :], actT[:, f, t * n:(t + 1) * n], Wdn[:, f, half * 512:(half + 1) * 512], f == 0, f == NF - 1, [actT, Wdn], [pb], sig=(f == NF - 1))
                            S.op("dve", lambda e: e.tensor_tensor(out=x[0:n, half * 512:(half + 1) * 512], in0=pb[0:n, :], in1=x[0:n, half * 512:(half + 1) * 512], op=ALU.add), reads=[pb, x], writes=[x])
                        S.dma("pool", dst, x[0:n, :], reads=[x], writes=[outb])

                if phases >= 3:
                    S.op("pool", lambda e: e.memset(gprev[:], 0.0), writes=[gprev])
                    blocks = []
                    for I in range(SEQ // TB):
                        tiles = []
                        for t in range(TB // 128):
                            g = (TB // 128) * I + t
                            tiles.append((X2[g * 128:(g + 1) * 128, :], X2b[g], y_p[g * 128:(g + 1) * 128, :]))
                        blocks.append((tiles, 128, None))
                    for b in range(2):
                        r0 = SEQ + b * NS
                        blocks.append(([(X2[r0:r0 + NS, :], X2b[NT + b], y_s[b * NS:(b + 1) * NS, :])], NS, b))
                    RS[0] = "pool"
                    preps = {0: ffn_prep_a(blocks[0][0], blocks[0][1])}
                    ffn_prep_a2(preps[0])
                    ffn_prep_b(preps[0], blocks[0][1], 0)
                    for i, (tiles, n, sb_) in enumerate(blocks):
                        nxt = blocks[i + 1] if i + 1 < len(blocks) else None

                        def hook(f, i=i, nxt=nxt):
                            if nxt is None:
                                return
                            if f == 3:
                                preps[i + 1] = ffn_prep_a(nxt[0], nxt[1])
                            if f == 7:
                                ffn_prep_a2(preps[i + 1])
                            if f == 12:
                                ffn_prep_b(preps[i + 1], nxt[1], (i + 1) % 2)
                        if sb_ is not None:
                            if sb_ == 0:
                                for j in range(2):
                                    S.dma("pool", cvp[j:j + 1, :].rearrange("o (c p) -> p (o c)", p=128), gprev[:, :, j], reads=[gprev], writes=[outb], allow_slow_non_contiguous=True)
                            for j in range(2):
                                S.dma("sp", gprev[:, :, j], scv[sb_, j:j + 1, :].rearrange("o (c p) -> p (o c)", p=128), reads=[gprev], writes=[gprev], allow_slow_non_contiguous=True)
                        ffn_main(tiles, n, preps[i][0], i % 2, hook)
                        if sb_ is not None:
                            for j in range(2):
                                S.dma("pool", cvs[sb_, j:j + 1, :].rearrange("o (c p) -> p (o c)", p=128), gprev[:, :, j], reads=[gprev], writes=[outb], allow_slow_non_contiguous=True)
                S.barrier()
        except _Stop:
            pass
        S.stopped = False
        S.barrier()
        print("ops", S.nops, "waits", S.nwaits, flush=True)
    return nc


_NC = None


def kernel(**inp):
    global _NC
    f = lambda a: np.ascontiguousarray(np.asarray(a, dtype=np.float32))
    if _NC is None:
        _NC = build()
    nc = _NC
    shared = dict(
        w_in=f(inp["w_in"][0]), b_f=f(inp["b_f"]), g_qa=f(inp["g_qa"]), g_ka=f(inp["g_ka"]), rel=f(inp["rel_bias"][0]),
        g_qb=f(inp["g_qb"]), g_kb=f(inp["g_kb"]), w_o=f(inp["w_o"][0]), g1=f(inp["g_norm1"]), g2=f(inp["g_norm2"]),
        gmem=f(inp["g_mem"]), w_mq=f(inp["w_mq"][0]), w_mkv=f(inp["w_mkv"][0]), g_mq=f(inp["g_mq"]), g_mk=f(inp["g_mk"]),
        w_mo=f(inp["w_mo"][0]), g3=f(inp["g_norm3"]), w_up=f(inp["w_up"][0]), w_conv=f(inp["w_conv"][0]),
        b_conv=f(inp["b_conv"]), w_down=f(inp["w_down"][0]))
    in_maps = []
    for c in range(8):
        s = slice(2 * c, 2 * c + 2)
        m = dict(shared)
        m.update(
            xp=f(inp["x_prompt"][c]), xs=f(inp["x_sample"][s]).reshape(2 * NS, D),
            cak=f(inp["cache_a_k"][0, s]).reshape(2, 512, 512), cav=f(inp["cache_a_v"][0, s]).reshape(2, 512, 512),
            cbk=f(inp["cache_b_k"][0, s]).reshape(2, SEQ, 512), cbv=f(inp["cache_b_v"][0, s]).reshape(2, SEQ, 512),
            cbl=f(inp["cache_b_logf"][0, s]), cmk=f(inp["cache_mem_k"][0, s]).reshape(2, 256, D),
            cmv=f(inp["cache_mem_v"][0, s]).reshape(2, 256, D), scv=f(inp["state_conv"][0, s]), memp=f(inp["mem_prompt"][c]))
        in_maps.append(m)
    res = run_bass_kernel_spmd(nc, in_maps, core_ids=list(range(8)))
    R = res.results
    cat = lambda k: np.stack([np.asarray(R[c][k], dtype=np.float32) for c in range(8)], 0)
    y_p = cat("y_p")
    y_s = cat("y_s").reshape(16, NS, D)
    akp = cat("akp").reshape(1, 8, 512, 8, 64); avp = cat("avp").reshape(1, 8, 512, 8, 64)
    bkp = cat("bkp").reshape(1, 8, SEQ, 8, 64); bvp = cat("bvp").reshape(1, 8, SEQ, 8, 64)
    blp = cat("blp").reshape(1, 8, SEQ, 8)
    mkp = cat("mkp").reshape(1, 8, 256, 4, 256); mvp = cat("mvp").reshape(1, 8, 256, 4, 256)
    cvp = cat("cvp").reshape(1, 8, 2, DFF)
    aks = cat("aks").reshape(1, 16, NS, 8, 64); avs = cat("avs").reshape(1, 16, NS, 8, 64)
    bks = cat("bks").reshape(1, 16, NS, 8, 64); bvs = cat("bvs").reshape(1, 16, NS, 8, 64)
    bls = cat("bls").reshape(1, 16, NS, 8)
    cvs = cat("cvs").reshape(1, 16, 2, DFF)
    return (y_p, y_s, akp, avp, bkp, bvp, blp, mkp, mvp, cvp, aks, avs, bks, bvs, bls, cvs)
```

```python
import contextlib
import numpy as np
import concourse.bass as bass
import concourse.mybir as mybir
from concourse.bass_utils import run_bass_kernel_spmd

F32 = mybir.dt.float32
BF16 = mybir.dt.bfloat16
AF = mybir.ActivationFunctionType
ALU = mybir.AluOpType
AX = mybir.AxisListType
NEG = -30000.0


class Buf:
    __slots__ = ("name", "w", "r")

    def __init__(self, name):
        self.name = name
        self.w = {}
        self.r = {}


class Tl(Buf):
    __slots__ = ("t",)

    def __init__(self, t, name):
        super().__init__(name)
        self.t = t

    def __getitem__(self, i):
        return self.t[i]


class Rot:
    def __init__(self, tls):
        self.tls = tls
        self.i = 0

    def nxt(self):
        t = self.tls[self.i]
        self.i = (self.i + 1) % len(self.tls)
        return t


class Sched:
    NDMA = 32

    def __init__(self, nc, es):
        self.nc = nc
        self.eng = {"pe": nc.tensor, "act": nc.scalar, "dve": nc.vector,
                    "pool": nc.gpsimd, "sp": nc.sync}
        self.sems = {}
        for e in self.eng:
            self.sems[e] = es.enter_context(nc.semaphore("s_" + e))
        for i in range(self.NDMA):
            self.sems[("dma", i)] = es.enter_context(nc.semaphore("s_dma%d" % i))
        self.cnt = {k: 0 for k in self.sems}
        self.waited = {e: {} for e in self.eng}
        self.rr = 0
        self.rrq = [0, 0]
        self.nwaits = 0
        self.nops = {e: 0 for e in self.eng}
        self.stopped = False

    def _wait(self, e, deps):
        w = self.waited[e]
        for k, v in deps.items():
            if k == e and e in ("pe", "sp"):
                continue
            if w.get(k, 0) >= v:
                continue
            self.eng[e].wait_ge(self.sems[k], v)
            self.nwaits += 1
            w[k] = v

    @staticmethod
    def _merge(d, src):
        for k, v in src.items():
            if d.get(k, 0) < v:
                d[k] = v

    def op(self, e, fn, reads=(), writes=(), sig=True):
        if self.stopped:
            return None
        deps = {}
        for b in reads:
            self._merge(deps, b.w)
        for b in writes:
            self._merge(deps, b.w)
            self._merge(deps, b.r)
        self._wait(e, deps)
        ins = fn(self.eng[e])
        self.nops[e] += 1
        if sig:
            self.cnt[e] += 1
            ins.then_inc(self.sems[e], 1)
            t = self.cnt[e]
        else:
            t = self.cnt[e] + 1
        for b in reads:
            if b.r.get(e, 0) < t:
                b.r[e] = t
        for b in writes:
            b.w = {e: t}
            b.r = {}
        return ins

    def dma(self, q, out, in_, reads=(), writes=(), **kw):
        if self.stopped:
            return None
        half = self.NDMA // 2
        qi = 0 if q == "sp" else 1
        i = qi * half + self.rrq[qi]
        self.rrq[qi] = (self.rrq[qi] + 1) % half
        k = ("dma", i)
        deps = {}
        if self.cnt[k] > 0:
            deps[k] = self.cnt[k]
        for b in reads:
            self._merge(deps, b.w)
        for b in writes:
            self._merge(deps, b.w)
            self._merge(deps, b.r)
        self._wait(q, deps)
        ins = self.eng[q].dma_start(out=out, in_=in_, **kw)
        self.nops[q] += 1
        self.cnt[k] += 16
        ins.then_inc(self.sems[k], 16)
        t = self.cnt[k]
        for b in reads:
            b.r[k] = t
        for b in writes:
            b.w = {k: t}
            b.r = {}
        return ins

    def barrier(self, engines=("pe", "act", "dve", "pool", "sp")):
        if self.stopped:
            return
        deps = {k: v for k, v in self.cnt.items() if v > 0}
        for e in engines:
            d = {k: v for k, v in deps.items() if k != e}
            w = self.waited[e]
            for k, v in d.items():
                if w.get(k, 0) >= v:
                    continue
                self.eng[e].wait_ge(self.sems[k], v)
                w[k] = v


D = 1024
SEQ = 4096
NT = 32
DFF = 2816
NF = 22
INC = 3080
NS = 64


class _Stop(Exception):
    pass


def build(phases=3, stage=99):
    nc = bass.Bass("TRN2", target_bir_lowering=False)

    def chk(k):
        if stage == k:
            S.stopped = True

    H = {}

    def din(n, shape):
        H[n] = nc.dram_tensor(n, list(shape), F32, kind="ExternalInput")
        return H[n].ap()

    def dout(n, shape):
        H[n] = nc.dram_tensor(n, list(shape), F32, kind="ExternalOutput")
        return H[n].ap()

    xp = din("xp", [SEQ, D]); xs = din("xs", [2 * NS, D])
    cak = din("cak", [2, 512, 512]); cav = din("cav", [2, 512, 512])
    cbk = din("cbk", [2, SEQ, 512]); cbv = din("cbv", [2, SEQ, 512]); cbl = din("cbl", [2, SEQ, 8])
    cmk = din("cmk", [2, 256, D]); cmv = din("cmv", [2, 256, D]); scv = din("scv", [2, 2, DFF])
    memp = din("memp", [256, D])
    w_in = din("w_in", [D, INC]); b_f = din("b_f", [1, 8])
    g_qa = din("g_qa", [1, 64]); g_ka = din("g_ka", [1, 64]); rel = din("rel", [8, 257])
    g_qb = din("g_qb", [1, 64]); g_kb = din("g_kb", [1, 64])
    w_o = din("w_o", [D, D]); g1 = din("g1", [1, D]); g2 = din("g2", [1, D]); gmem = din("gmem", [1, D])
    w_mq = din("w_mq", [D, D]); w_mkv = din("w_mkv", [D, 2 * D]); g_mq = din("g_mq", [1, 256]); g_mk = din("g_mk", [1, 256])
    w_mo = din("w_mo", [D, D]); g3 = din("g3", [1, D]); w_up = din("w_up", [D, 2 * DFF])
    w_conv = din("w_conv", [3, DFF]); b_conv = din("b_conv", [1, DFF]); w_down = din("w_down", [DFF, D])

    y_p = dout("y_p", [SEQ, D]); y_s = dout("y_s", [2 * NS, D])
    akp = dout("akp", [512, 512]); avp = dout("avp", [512, 512])
    bkp = dout("bkp", [SEQ, 512]); bvp = dout("bvp", [SEQ, 512]); blp = dout("blp", [SEQ, 8])
    mkp = dout("mkp", [256, D]); mvp = dout("mvp", [256, D]); cvp = dout("cvp", [2, DFF])
    aks = dout("aks", [2 * NS, 512]); avs = dout("avs", [2 * NS, 512])
    bks = dout("bks", [2 * NS, 512]); bvs = dout("bvs", [2 * NS, 512]); bls = dout("bls", [2 * NS, 8])
    cvs = dout("cvs", [2, 2, DFF])

    X1 = nc.dram_tensor("X1", [SEQ + 2 * NS, D], F32, kind="Internal").ap()
    X2 = nc.dram_tensor("X2", [SEQ + 2 * NS, D], F32, kind="Internal").ap()
    Esc_h = nc.dram_tensor("Esc", [8, 512], F32, kind="Internal")
    Esc = Esc_h.ap()
    Osc = nc.dram_tensor("Osc", [SEQ + 2 * NS, D], BF16, kind="Internal").ap()
    Ob = [Buf("Osc%d" % i) for i in range(NT + 2)]
    WinS = nc.dram_tensor("WinS", [7, 128, 8 * 512], BF16, kind="Internal").ap()
    X1b = [Buf("X1_%d" % i) for i in range(NT + 2)]
    X2b = [Buf("X2_%d" % i) for i in range(NT + 2)]
    outb = Buf("outputs")
    WinSb = [Buf("WinS%d" % i) for i in range(7)]

    es = contextlib.ExitStack()
    with es:
        S = Sched(nc, es)
        try:

            def sb(st, name, shape, dt=F32):
                return Tl(st.enter_context(nc.sbuf_tensor(name, list(shape), dt)), name)

            def psb(st, name, shape, dt=F32):
                return Tl(st.enter_context(nc.psum_tensor(name, list(shape), dt)), name)

            bank = [psb(es, "bk%d" % i, [128, 512]) for i in range(7)]
            ptb = psb(es, "bkT", [128, 1024], BF16)

            cf = sb(es, "cf", [128, 128])
            ident = sb(es, "ident", [128, 128], BF16)
            J = sb(es, "J", [128, 128], BF16)
            U = sb(es, "U", [128, 128])
            ones_f = sb(es, "ones_f", [128, 128])
            ones_b = sb(es, "ones_b", [128, 128], BF16)
            Mdiag = sb(es, "Mdiag", [128, 128], BF16)
            Mask0 = sb(es, "Mask0", [128, 128], BF16)
            mhalf = sb(es, "mhalf", [128, 8])
            one_col = sb(es, "one_col", [128, 1])
            eps_col = sb(es, "eps_col", [128, 1])
            Hb = sb(es, "Hb", [128, 8, 2, 128], BF16)

            def aff(dst_bf, init, pattern, cmp, fill, base, cm):
                S.op("pool", lambda e: e.memset(cf[:], init), writes=[cf])
                S.op("pool", lambda e: e.affine_select(out=cf[:], in_=cf[:], pattern=pattern, compare_op=cmp, fill=fill, base=base, channel_multiplier=cm), reads=[cf], writes=[cf])
                if dst_bf is not None:
                    S.op("pool", lambda e: e.tensor_copy(out=dst_bf[:], in_=cf[:]), reads=[cf], writes=[dst_bf])

            aff(ident, 0.0, [[-1, 128]], ALU.not_equal, 1.0, 0, 1)
            aff(J, 0.0, [[1, 128]], ALU.not_equal, 1.0, -127, 1)
            aff(Mdiag, 0.0, [[1, 128]], ALU.is_ge, NEG, 0, -1)
            S.op("pool", lambda e: e.memset(U[:], 1.0), writes=[U])
            S.op("pool", lambda e: e.affine_select(out=U[:], in_=U[:], pattern=[[1, 128]], compare_op=ALU.is_ge, fill=0.0, base=0, channel_multiplier=-1), reads=[U], writes=[U])
            S.op("pool", lambda e: e.memset(ones_f[:], 1.0), writes=[ones_f])
            S.op("pool", lambda e: e.memset(ones_b[:], 1.0), writes=[ones_b])
            S.op("pool", lambda e: e.memset(Mask0[:], 0.0), writes=[Mask0])
            S.op("pool", lambda e: e.memset(Mask0[0:64, 64:128], NEG), writes=[Mask0])
            S.op("pool", lambda e: e.memset(mhalf[:], -0.5), writes=[mhalf])
            S.op("pool", lambda e: e.memset(one_col[:], 1.0), writes=[one_col])
            S.op("pool", lambda e: e.memset(eps_col[:], 1e-6), writes=[eps_col])

            def bc_load(name, src, n):
                t = sb(es, name, [128, n])
                S.dma("sp", t[:], src[0:1, 0:n].broadcast_to([128, n]), writes=[t])
                return t

            gqa_bc = bc_load("gqa_bc", g_qa, 64); gka_bc = bc_load("gka_bc", g_ka, 64)
            gqb_bc = bc_load("gqb_bc", g_qb, 64); gkb_bc = bc_load("gkb_bc", g_kb, 64)
            gmq_bc = bc_load("gmq_bc", g_mq, 256); gmk_bc = bc_load("gmk_bc", g_mk, 256)
            bf_bc = bc_load("bf_bc", b_f, 8)

            def col_load(name, src):
                t = sb(es, name, [128, 8])
                S.dma("sp", t[:], src.rearrange("o (c p) -> p (o c)", p=128), writes=[t], allow_slow_non_contiguous=True)
                return t

            g1c = col_load("g1c", g1); g2c = col_load("g2c", g2); g3c = col_load("g3c", g3); gmc = col_load("gmc", gmem)
            wc = sb(es, "wc", [128, 3, NF])
            for j in range(3):
                S.dma("sp", wc[:, j, :], w_conv[j:j + 1, :].rearrange("o (c p) -> p (o c)", p=128), writes=[wc], allow_slow_non_contiguous=True)
            bcv = sb(es, "bcv", [128, NF])
            S.dma("sp", bcv[:], b_conv.rearrange("o (c p) -> p (o c)", p=128), writes=[bcv], allow_slow_non_contiguous=True)

            stg = None

            def alloc_stg(st, tag, k):
                return Rot([sb(st, "stg%s%d" % (tag, i), [128, 512]) for i in range(k)])
            cast_i = [0]

            def cast(out_ap, in_ap, gcol_ap, reads, writes):
                e = "act" if cast_i[0] % 2 == 0 else "dve"
                cast_i[0] += 1
                if e == "act":
                    if gcol_ap is None:
                        S.op(e, lambda en: en.copy(out=out_ap, in_=in_ap), reads=reads, writes=writes)
                    else:
                        S.op(e, lambda en: en.activation(out=out_ap, in_=in_ap, func=AF.Copy, scale=gcol_ap), reads=reads, writes=writes)
                elif gcol_ap is None:
                    S.op(e, lambda en: en.tensor_copy(out=out_ap, in_=in_ap), reads=reads, writes=writes)
                else:
                    S.op(e, lambda en: en.tensor_scalar(out=out_ap, in0=in_ap, scalar1=gcol_ap, scalar2=None, op0=ALU.mult), reads=reads, writes=writes)

            def load_w(dst, src, nch, ncols, gcol):
                for c in range(nch):
                    for c0 in range(0, ncols, 512):
                        c1 = min(ncols, c0 + 512)
                        s = stg.nxt()
                        S.dma("sp", s[:, 0:c1 - c0], src[c * 128:(c + 1) * 128, c0:c1], writes=[s])
                        cast(dst[:, c, c0:c1], s[:, 0:c1 - c0], None if gcol is None else gcol[:, c:c + 1], [s] + ([gcol] if gcol is not None else []), [dst])

            xt_r = Rot([sb(es, "xt%d" % i, [128, D]) for i in range(2)])
            sqn = sb(es, "sqn", [128, D], BF16)
            xn_r = Rot([sb(es, "xn%d" % i, [128, D], BF16) for i in range(2)])
            st_r = Rot([sb(es, "st%d" % i, [128, 2]) for i in range(4)])

            EV = ["act"]

            def _cp(e):
                return e.copy if EV[0] == "act" else e.tensor_copy

            def evac(fn, reads=(), writes=()):
                S.op(EV[0], fn, reads=reads, writes=writes)

            RS = ["auto"]

            def rstd(out_ap, ss_ap, dim, n, k, buf):
                if EV[0] == "act" and RS[0] != "pool":
                    S.op("act", lambda e: e.activation(out=out_ap, in_=ss_ap, func=AF.Ln, scale=1.0 / dim, bias=eps_col[0:n, :]), reads=[buf, eps_col], writes=[buf])
                    S.op("act", lambda e: e.activation(out=out_ap, in_=out_ap, func=AF.Exp, scale=-0.5), reads=[buf], writes=[buf])
                else:
                    S.op("pool", lambda e: e.tensor_scalar(out=ss_ap, in0=ss_ap, scalar1=1.0 / dim, scalar2=1e-6, op0=ALU.mult, op1=ALU.add), reads=[buf], writes=[buf])
                    S.op("pool", lambda e: e.tensor_tensor(out=out_ap, in0=ss_ap, in1=mhalf[0:n, 0:k], op=ALU.pow), reads=[buf, mhalf], writes=[buf])

            def norm_a1(x, n):
                st = st_r.nxt()
                S.op("dve", lambda e: e.tensor_tensor(out=sqn[0:n, :], in0=x[0:n, :], in1=x[0:n, :], op=ALU.mult), reads=[x], writes=[sqn])
                S.op("dve", lambda e: e.tensor_reduce(out=st[0:n, 0:1], in_=sqn[0:n, :], axis=AX.X, op=ALU.add), reads=[sqn], writes=[st])
                rstd(st[0:n, 1:2], st[0:n, 0:1], D, n, 1, st)
                return st

            def norm_a2(x, n, st):
                xn = xn_r.nxt()
                S.op("dve", lambda e: e.tensor_scalar(out=xn[0:n, :], in0=x[0:n, :], scalar1=st[0:n, 1:2], scalar2=None, op0=ALU.mult), reads=[x, st], writes=[xn])
                return xn

            def norm_a(x, n):
                return norm_a2(x, n, norm_a1(x, n))

            def norm_b(xn, n, dst_ap, dst_buf):
                for c in range(8):
                    S.op("pe", lambda e: e.transpose(out=ptb[:, c * n:(c + 1) * n], in_=xn[0:n, c * 128:(c + 1) * 128], identity=ident[0:n, 0:n]), reads=[xn, ident], writes=[ptb], sig=(c == 7))
                evac(lambda e: _cp(e)(out=dst_ap, in_=ptb[:, 0:8 * n].rearrange("p (c t) -> p c t", c=8)), reads=[ptb], writes=[dst_buf])

            def norm_T(x, n, dst_ap, dst_buf):
                norm_b(norm_a(x, n), n, dst_ap, dst_buf)

            tcp_r = sqh_r = knb_r = None

            def alloc_hn(st, tag):
                return (Rot([sb(st, "tcp%s%d" % (tag, i), [128, 512]) for i in range(4)]),
                        Rot([sb(st, "sqh%s%d" % (tag, i), [128, 512], BF16) for i in range(2)]),
                        Rot([sb(st, "knb%s%d" % (tag, i), [128, 512], BF16) for i in range(4)]))
            hs_r = Rot([sb(es, "hs%d" % i, [128, 16]) for i in range(6)])

            def hn_a(src, n, nh):
                hd = 512 // nh
                tcp = tcp_r.nxt(); sqh = sqh_r.nxt(); hs = hs_r.nxt()
                v3 = lambda t: t[0:n, :].rearrange("p (h d) -> p h d", h=nh)
                evac(lambda e: _cp(e)(out=tcp[0:n, :], in_=src[0:n, :]), reads=[src], writes=[tcp])
                S.op("dve", lambda e: e.tensor_tensor(out=sqh[0:n, :], in0=tcp[0:n, :], in1=tcp[0:n, :], op=ALU.mult), reads=[tcp], writes=[sqh])
                S.op("dve", lambda e: e.tensor_reduce(out=hs[0:n, 0:nh], in_=v3(sqh), axis=AX.X, op=ALU.add), reads=[sqh], writes=[hs])
                rstd(hs[0:n, 8:8 + nh], hs[0:n, 0:nh], hd, n, nh, hs)
                return (tcp, hs, n, nh)

            def hn_b(stt_, gbc, want_f32):
                (tcp, hs, n, nh) = stt_
                hd = 512 // nh
                knb = knb_r.nxt()
                v3 = lambda t: t[0:n, :].rearrange("p (h d) -> p h d", h=nh)
                S.op("dve", lambda e: e.tensor_tensor(out=v3(tcp), in0=v3(tcp), in1=hs[0:n, 8:8 + nh].unsqueeze(2).broadcast_to([n, nh, hd]), op=ALU.mult), reads=[tcp, hs], writes=[tcp])
                gb3 = gbc[0:n, 0:hd].unsqueeze(1).broadcast_to([n, nh, hd])
                if want_f32:
                    S.op("dve", lambda e: e.tensor_tensor(out=v3(tcp), in0=v3(tcp), in1=gb3, op=ALU.mult), reads=[tcp, gbc], writes=[tcp])
                    evac(lambda e: _cp(e)(out=knb[0:n, :], in_=tcp[0:n, :]), reads=[tcp], writes=[knb])
                    return tcp, knb
                S.op("dve", lambda e: e.tensor_tensor(out=v3(knb), in0=v3(tcp), in1=gb3, op=ALU.mult), reads=[tcp, gbc], writes=[knb])
                return None, knb

            def headnorm(src, n, nh, gbc, want_f32):
                return hn_b(hn_a(src, n, nh), gbc, want_f32)

            def transp_to(src_bf, n, nblk, dst_ap, dst_bufs):
                for i in range(nblk):
                    S.op("pe", lambda e: e.transpose(out=ptb[:, i * n:(i + 1) * n], in_=src_bf[0:n, i * 128:(i + 1) * 128], identity=ident[0:n, 0:n]), reads=[src_bf, ident], writes=[ptb], sig=(i == nblk - 1))
                evac(lambda e: _cp(e)(out=dst_ap, in_=ptb[:, 0:nblk * n].rearrange("p (c t) -> p c t", c=nblk)), reads=[ptb], writes=dst_bufs)

            def mm(out, lhsT, rhs, start, stop, reads, writes, sig=True, **kw):
                S.op("pe", lambda e: e.matmul(out, lhsT=lhsT, rhs=rhs, start=start, stop=stop, **kw), reads=reads, writes=writes, sig=sig)

            pm = Rot([bank[0], bank[1]])
            with contextlib.ExitStack() as st0:
                Esb = sb(st0, "Esb", [8, 512]); t256 = sb(st0, "t256", [8, 1]); Hf = sb(st0, "Hf", [128, 8, 2, 128])
                S.dma("sp", Esb[:, 0:257], rel[:, :], writes=[Esb])
                S.op("dve", lambda e: e.tensor_copy(out=t256[:], in_=Esb[:, 256:257]), reads=[Esb], writes=[t256])
                S.op("dve", lambda e: e.tensor_copy(out=Esb[:, 257:512], in_=t256[:, 0:1].broadcast_to([8, 255])), reads=[t256], writes=[Esb])
                S.op("dve", lambda e: e.tensor_scalar(out=Esb[:], in0=Esb[:], scalar1=t256[:, 0:1], scalar2=8.0, op0=ALU.subtract, op1=ALU.mult), reads=[Esb, t256], writes=[Esb])
                eb = Buf("Esc")
                S.dma("pool", Esc[:, :], Esb[:], reads=[Esb], writes=[eb])
                stg = alloc_stg(st0, "a", 8)
                wst_r = Rot([sb(st0, "wst%d" % i, [128, 8, 512], BF16) for i in range(2)])
                for gi in range(7):
                    c0 = gi * 512
                    w = min(512, INC - c0)
                    wst = wst_r.nxt()
                    for c in range(8):
                        s = stg.nxt()
                        S.dma("sp", s[:, 0:w], w_in[c * 128:(c + 1) * 128, c0:c0 + w], writes=[s])
                        cast(wst[:, c, 0:w], s[:, 0:w], g1c[:, c:c + 1], [s, g1c], [wst])
                    S.dma("pool", WinS[gi].rearrange("p (c n) -> p c n", c=8)[:, :, 0:w], wst[:, :, 0:w], reads=[wst], writes=[WinSb[gi]])
                for h in range(8):
                    S.dma("sp", Hf[:, h, :, :], bass.AP(Esc_h, h * 512 + 1, [[1, 128], [128, 2], [1, 128]]), reads=[eb], writes=[Hf])
                S.op("dve", lambda e: e.tensor_copy(out=Hb[:], in_=Hf[:]), reads=[Hf], writes=[Hb])
                S.op("pool", lambda e: e.memset(Hb[0:64, :, 0, 0:64], NEG), reads=[Hb], writes=[Hb])
                chk(2)
                S.barrier()

            with contextlib.ExitStack() as st1:
                tcp_r, sqh_r, knb_r = alloc_hn(st1, "a")
                EV[0] = "dve"
                wg_r = Rot([sb(st1, "wg%d" % i, [128, 8, 512], BF16) for i in range(2)])
                KTb = sb(st1, "KTb", [128, 4, SEQ + NS], BF16); KTb_b = [Buf("KTb%d" % j) for j in range(NT + 1)]
                Vst = sb(st1, "Vst", [128, NT + 1, 8, 65], BF16); Vst_b = [Buf("Vst%d" % j) for j in range(NT + 1)]
                KTa = sb(st1, "KTa", [128, 4, 1024], BF16); KTa_b = [Buf("KTa%d" % j) for j in range(8)]
                VA = sb(st1, "VA", [128, 8, 8, 65], BF16); VA_b = [Buf("VA%d" % j) for j in range(8)]
                S.op("pool", lambda e: e.memset(Vst[:, :, :, 64:65], 1.0), writes=Vst_b)
                S.op("pool", lambda e: e.memset(VA[:, :, :, 64:65], 1.0), writes=VA_b)
                hT = sb(st1, "hT", [128, 8, 512], BF16); hT_b = [Buf("hT%d" % t) for t in range(4)]
                QTb_r = Rot([sb(st1, "QTb%d" % i, [128, 8, 512], BF16) for i in range(2)])
                QTa = sb(st1, "QTa", [128, 8, 512], BF16)
                for qt_ in QTb_r.tls + [QTa]:
                    S.op("pool", lambda e: e.memset(qt_[:], 0.0), writes=[qt_])
                vf_r = Rot([sb(st1, "vf%d" % i, [128, 512]) for i in range(2)])
                PT_r = Rot([sb(st1, "PT%d" % i, [128, 512], BF16) for i in range(4)])
                PTa_r = Rot([sb(st1, "PTa%d" % i, [128, 5, 128], BF16) for i in range(2)])
                Oblk = sb(st1, "Oblk", [128, 4, D], BF16)
                lfb_r = Rot([sb(st1, "lfb%d" % i, [128, 4, 8]) for i in range(2)])
                lft = sb(st1, "lft", [128, 8]); lfe = sb(st1, "lfe", [128, 8])
                c_all = sb(st1, "c_all", [128, NT + 1, 8])
                S.op("pool", lambda e: e.memset(c_all[:], 0.0), writes=[c_all])
                nbias_r = Rot([sb(st1, "nbias%d" % i, [128, NT + 1, 8]) for i in range(2)])
                carry = sb(st1, "carry", [128, NT + 2, 8])
                rec_r = Rot([sb(st1, "rec%d" % i, [128, 4, 1]) for i in range(4)])
                lfc = sb(st1, "lfc", [128, NT, 8])
                psT = Rot([bank[2], bank[3], bank[4]])
                po = Rot([bank[5], bank[6]])

                def in_proj_group(gi, wg, hslice, hbuf, n):
                    pb = pm.nxt()
                    w = min(512, INC - gi * 512)
                    for c in range(8):
                        mm(pb[0:n, 0:w], hslice(c), wg[:, c, 0:w], c == 0, c == 7, [hbuf, wg], [pb], sig=(c == 7))
                    return pb

                def logf_from(pb, n, dst_ap, dst_buf):
                    S.op("dve", lambda e: e.tensor_tensor(out=lft[0:n, :], in0=pb[0:n, 0:8], in1=bf_bc[0:n, :], op=ALU.add), reads=[pb, bf_bc], writes=[lft])
                    S.op("act", lambda e: e.activation(out=lfe[0:n, :], in_=lft[0:n, :], func=AF.Exp, scale=-1.0), reads=[lft], writes=[lfe])
                    S.op("act", lambda e: e.activation(out=lft[0:n, :], in_=lfe[0:n, :], func=AF.Ln, bias=one_col[0:n, :], scale=1.0), reads=[lfe, one_col], writes=[lft])
                    S.op("dve", lambda e: e.tensor_scalar(out=dst_ap, in0=lft[0:n, :], scalar1=-1.0, scalar2=None, op0=ALU.mult), reads=[lft], writes=[dst_buf])

                def tile_groups(gi, pb, n, QTbt, qcol, kt_idx, a_slot, outs, lf_ap, lf_buf):
                    o_ak, o_av, o_bk, o_bv = outs
                    if gi in (0, 1, 3, 4):
                        sta = hn_a(pb, n, 8)
                        box = {}

                        def part_b():
                            if gi == 0:
                                box["r"] = hn_b(sta, gqa_bc, False)
                            elif gi == 3:
                                box["r"] = hn_b(sta, gqb_bc, False)
                            elif gi == 1:
                                box["r"] = hn_b(sta, gka_bc, True)
                                if o_ak is not None:
                                    S.dma("pool", o_ak, box["r"][0][0:n, :], reads=[box["r"][0]], writes=[outb])
                            else:
                                box["r"] = hn_b(sta, gkb_bc, True)
                                S.dma("pool", o_bk, box["r"][0][0:n, :], reads=[box["r"][0]], writes=[outb])

                        def part_t():
                            kb = box["r"][1]
                            if gi == 0:
                                transp_q(kb, n, QTa, qcol)
                            elif gi == 3:
                                transp_q(kb, n, QTbt, qcol)
                            elif gi == 1:
                                transp_to(kb, n, 4, KTa[:, :, a_slot * 128:a_slot * 128 + n], [KTa_b[a_slot]])
                            else:
                                transp_to(kb, n, 4, KTb[:, :, kt_idx * 128:kt_idx * 128 + n], [KTb_b[kt_idx]])
                        return [part_b, part_t]
                    elif gi in (2, 5):
                        vf = vf_r.nxt()
                        evac(lambda e: _cp(e)(out=vf[0:n, :], in_=pb[0:n, :]), reads=[pb], writes=[vf])
                        o = o_av if gi == 2 else o_bv
                        if o is not None:
                            S.dma("pool", o, vf[0:n, :], reads=[vf], writes=[outb])
                        if gi == 2:
                            S.op("dve", lambda e: e.tensor_copy(out=VA[0:n, a_slot, :, 0:64], in_=vf[0:n, :].rearrange("p (h d) -> p h d", h=8)), reads=[vf], writes=[VA_b[a_slot]])
                        else:
                            S.op("dve", lambda e: e.tensor_copy(out=Vst[0:n, kt_idx, :, 0:64], in_=vf[0:n, :].rearrange("p (h d) -> p h d", h=8)), reads=[vf], writes=[Vst_b[kt_idx]])
                    else:
                        logf_from(pb, n, lf_ap, lf_buf)
                    return None

                def transp_q(src_bf, n, QT, qcol):
                    for i in range(4):
                        S.op("pe", lambda e: e.transpose(out=ptb[:, i * n:(i + 1) * n], in_=src_bf[0:n, i * 128:(i + 1) * 128], identity=ident[0:n, 0:n]), reads=[src_bf, ident], writes=[ptb], sig=(i == 3))
                    for e2 in range(2):
                        evac(lambda e: _cp(e)(out=QT.t[:, :, :].rearrange("p (a e) t -> p a e t", e=2)[e2 * 64:(e2 + 1) * 64, :, e2, qcol:qcol + n],
                                                     in_=ptb[e2 * 64:(e2 + 1) * 64, 0:4 * n].rearrange("p (c t) -> p c t", c=4)), reads=[ptb], writes=[QT])

                pend = []

                def defer(fns):
                    for ent in list(pend):
                        ent.pop(0)()
                        if not ent:
                            pend.remove(ent)
                    if fns:
                        pend.append(list(fns))

                def flush():
                    while pend:
                        defer(None)

                def a_attn(q0, nq, ktiles, trow):
                    jjs = [k[2] for k in ktiles]
                    jlo = min(j for j in jjs if j >= 1)
                    kts = sorted(ktiles, key=lambda k: (k[2] == 0, k[2]))
                    state = {}

                    def qk(h):
                        pr, bp = h // 2, 64 * (h % 2)
                        pX = psT.nxt()
                        pY = psT.nxt() if 0 in jjs else None
                        for (slot, nk, jj) in kts:
                            dst = pY[0:nk, 0:nq] if jj == 0 else pX[0:nk, (jj - 1) * 128:(jj - 1) * 128 + nq]
                            dbuf = pY if jj == 0 else pX
                            extra = jj in (3, 4) or (jj == 0 and nq == 128)
                            mm(dst, KTa[:, pr, slot * 128:slot * 128 + nk], QTa[:, h, q0:q0 + nq], True, not extra,
                               [KTa_b[slot], QTa], [dbuf], sig=not extra)
                            if jj == 3:
                                mm(dst, J[:, :], Hb[:, h, 1, 0:nq], False, True, [J, Hb], [dbuf])
                            elif jj == 4:
                                mm(dst, J[:, 0:nk], Hb[:, h, 0, 0:nq], False, True, [J, Hb], [dbuf])
                            elif jj == 0 and nq == 128:
                                mm(dst, ident[:, :], Mask0[:, :], False, True, [ident, Mask0], [dbuf])
                        PTa = PTa_r.nxt()
                        nk4 = [k[1] for k in kts if k[2] == 4][0]
                        jhi = 5 if nk4 == 128 else 4
                        if jhi > jlo:
                            S.op("act", lambda e: e.activation(out=PTa[:, jlo:jhi, 0:nq], in_=pX[:, (jlo - 1) * 128:(jhi - 1) * 128].rearrange("p (j c) -> p j c", c=128)[:, :, 0:nq], func=AF.Exp, scale=0.125),
                                 reads=[pX], writes=[PTa])
                        if jhi == 4:
                            S.op("act", lambda e: e.activation(out=PTa[0:nk4, 4, 0:nq], in_=pX[0:nk4, 384:384 + nq], func=AF.Exp, scale=0.125), reads=[pX], writes=[PTa])
                        if pY is not None:
                            S.op("act", lambda e: e.activation(out=PTa[:, 0, 0:nq], in_=pY[:, 0:nq], func=AF.Exp, scale=0.125), reads=[pY], writes=[PTa])
                        return PTa

                    def pv(h, PTa):
                        hg, hh = h // 4, h % 4
                        if hh == 0:
                            state["pob"] = po.nxt()
                        pob = state["pob"]
                        for i, (slot, nk, jj) in enumerate(ktiles):
                            last = i == len(ktiles) - 1
                            mm(pob[0:nq, hh * 65:(hh + 1) * 65], PTa[0:nk, jj, 0:nq], VA[0:nk, slot, h, :], i == 0, last, [PTa, VA_b[slot]], [pob], sig=last)
                        if hh == 3:
                            rec = rec_r.nxt()
                            p3 = pob[0:nq, 0:260].rearrange("p (t c) -> p t c", c=65)
                            S.op("dve", lambda e: e.reciprocal(out=rec[0:nq, :, :], in_=p3[:, :, 64:65]), reads=[pob], writes=[rec])
                            S.op("dve", lambda e: e.tensor_tensor(out=Oblk[0:nq, trow, hg * 256:(hg + 1) * 256].rearrange("p (h d) -> p h d", h=4), in0=p3[:, :, 0:64],
                                                                  in1=rec[0:nq, :, :].broadcast_to([nq, 4, 64]), op=ALU.mult), reads=[pob, rec], writes=[Oblk])

                    prev = None
                    for h in range(8):
                        PTa = qk(h)
                        if prev is not None:
                            pv(*prev)
                        prev = (h, PTa)
                        yield
                    pv(*prev)

                def b_attn(QTbt, nq, ktiles, nbias):
                    ntq = (nq + 127) // 128
                    qw = min(nq, 128)
                    pobs = {}

                    def qk(h, kt):
                        (j, nk, c0, diag) = kt
                        pr, bp = h // 2, 64 * (h % 2)
                        pst = psT.nxt()
                        mm(pst[0:nk, c0:nq], KTb[:, pr, j * 128:j * 128 + nk], QTbt[:, h, c0:nq], True, not diag,
                           [KTb_b[j], QTbt], [pst], sig=not diag)
                        if diag:
                            w = min(128, nq - c0)
                            mm(pst[0:nk, c0:c0 + w], ident[:, 0:nk], Mdiag[:, 0:w], False, True, [ident, Mdiag], [pst])
                        PT = PT_r.nxt()
                        S.op("act", lambda e: e.activation(out=PT[0:nk, c0:nq], in_=pst[0:nk, c0:nq], func=AF.Exp, scale=0.125, bias=nbias[0:nk, j, h:h + 1]),
                             reads=[pst, nbias], writes=[PT])
                        return PT

                    def pv(h, kt, PT, first, last):
                        (j, nk, c0, diag) = kt
                        if first:
                            pobs[h] = po.nxt()
                            assert c0 == 0
                        pob = pobs[h]
                        for t in range(c0 // 128, ntq):
                            mm(pob[0:qw, t * 65:(t + 1) * 65], PT[0:nk, t * 128:t * 128 + qw], Vst[0:nk, j, h, :], bool(first and t == 0), False,
                               [PT, Vst_b[j]], [pob], sig=(t == ntq - 1), skip_group_check=True)
                        if last:
                            rec = rec_r.nxt()
                            p3 = pob[0:qw, 0:ntq * 65].rearrange("p (t c) -> p t c", c=65)
                            S.op("dve", lambda e: e.reciprocal(out=rec[0:qw, 0:ntq, :], in_=p3[:, :, 64:65]), reads=[pob], writes=[rec])
                            S.op("dve", lambda e: e.tensor_tensor(out=Oblk[0:qw, 0:ntq, 512 + h * 64:512 + (h + 1) * 64], in0=p3[:, :, 0:64],
                                                                  in1=rec[0:qw, 0:ntq, :].broadcast_to([qw, ntq, 64]), op=ALU.mult), reads=[pob, rec], writes=[Oblk])

                    items = [(h, kt, i == 0, i == len(ktiles) - 1) for h in range(8) for i, kt in enumerate(ktiles)]
                    pq = []
                    for (h, kt, first, last) in items:
                        PT = qk(h, kt)
                        pq.append((h, kt, PT, first, last))
                        if len(pq) > 2:
                            pv(*pq.pop(0))
                        yield
                    while pq:
                        pv(*pq.pop(0))

                def run(gen):
                    for _ in gen:
                        pass

                def interleave(g1, n1, g2, n2):
                    acc = 0
                    done2 = False
                    for _ in g1:
                        acc += n2
                        while acc >= n1 and not done2:
                            acc -= n1
                            try:
                                next(g2)
                            except StopIteration:
                                done2 = True
                    if not done2:
                        run(g2)

                def o_out(trow, n, gidx):
                    r0 = gidx * 128 if gidx < NT else SEQ + (gidx - NT) * NS
                    S.dma("pool", Osc[r0:r0 + n, :], Oblk[0:n, trow, :], reads=[Oblk], writes=[Ob[gidx]])

                def cumsum_tiles(lf_ap, lf_buf, nt, n, j0, carry_idx):
                    pcs = pm.nxt(); ptot = pm.nxt()
                    mm(pcs[0:n, 0:nt * 8], U[0:n, 0:n], lf_ap, True, True, [U, lf_buf], [pcs])
                    mm(ptot[:, 0:nt * 8], ones_f[0:n, :], lf_ap, True, True, [ones_f, lf_buf], [ptot])
                    for i in range(nt):
                        S.op("dve", lambda e: e.tensor_tensor(out=c_all[0:n, j0 + i, :], in0=pcs[0:n, i * 8:(i + 1) * 8], in1=carry[0:n, carry_idx + i, :], op=ALU.add), reads=[pcs, carry], writes=[c_all])
                        S.op("dve", lambda e: e.tensor_tensor(out=carry[:, carry_idx + i + 1, :], in0=ptot[:, i * 8:(i + 1) * 8], in1=carry[:, carry_idx + i, :], op=ALU.add), reads=[ptot, carry], writes=[carry])

                def make_nbias(nj, carry_idx):
                    nb = nbias_r.nxt()
                    S.op("dve", lambda e: e.scalar_tensor_tensor(out=nb[:, 0:nj, :], in0=c_all[:, 0:nj, :], scalar=-1.0, in1=carry[:, carry_idx:carry_idx + 1, :].broadcast_to([128, nj, 8]), op0=ALU.mult, op1=ALU.add),
                         reads=[c_all, carry], writes=[nb])
                    return nb

                for b in range(2):
                    n = NS
                    for j in range(NT):
                        kc = tcp_r.nxt()
                        S.dma("sp", kc[:], cbk[b, j * 128:(j + 1) * 128, :], writes=[kc])
                        kb = knb_r.nxt()
                        cast(kb[:], kc[:], None, [kc], [kb])
                        transp_to(kb, 128, 4, KTb[:, :, j * 128:(j + 1) * 128], [KTb_b[j]])
                        vc = tcp_r.nxt()
                        S.dma("sp", vc[:], cbv[b, j * 128:(j + 1) * 128, :], writes=[vc])
                        cast(Vst[:, j, :, 0:64], vc[:].rearrange("p (h d) -> p h d", h=8), None, [vc], [Vst_b[j]])
                    for j in range(4):
                        kc = tcp_r.nxt()
                        S.dma("sp", kc[:], cak[b, j * 128:(j + 1) * 128, :], writes=[kc])
                        kb = knb_r.nxt()
                        cast(kb[:], kc[:], None, [kc], [kb])
                        transp_to(kb, 128, 4, KTa[:, :, j * 128:(j + 1) * 128], [KTa_b[j]])
                        vc = tcp_r.nxt()
                        S.dma("sp", vc[:], cav[b, j * 128:(j + 1) * 128, :], writes=[vc])
                        cast(VA[:, j, :, 0:64], vc[:].rearrange("p (h d) -> p h d", h=8), None, [vc], [VA_b[j]])
                    S.dma("sp", lfc[:], cbl[b].rearrange("(j p) h -> p j h", p=128), writes=[lfc])
                    S.op("pool", lambda e: e.memset(carry[:, 0, :], 0.0), writes=[carry])
                    chk(3)
                    cumsum_tiles(lfc[:, :, :].rearrange("p j h -> p (j h)"), lfc, NT, 128, 0, 0)
                    chk(31)
                    x = xt_r.nxt()
                    S.dma("sp", x[0:n, :], xs[b * NS:(b + 1) * NS, :], writes=[x])
                    norm_T(x, n, hT[:, :, 0:n], hT_b[0])
                    chk(32)
                    QTbt = QTb_r.nxt()
                    lfb = lfb_r.nxt()
                    rows = slice(b * NS, (b + 1) * NS)
                    for gi in range(7):
                        wg = wg_r.nxt()
                        S.dma("sp", wg[:, :, 0:min(512, INC - gi * 512)], WinS[gi].rearrange("p (c n) -> p c n", c=8)[:, :, 0:min(512, INC - gi * 512)], reads=[WinSb[gi]], writes=[wg])
                        pb = in_proj_group(gi, wg, lambda c: hT[:, c, 0:n], hT_b[0], n)
                        if gi == 0:
                            chk(33)
                        defer(tile_groups(gi, pb, n, QTbt, 0, NT, 4, (aks[rows, :], avs[rows, :], bks[rows, :], bvs[rows, :]), lfb[0:n, 0, :], lfb))
                        chk(34 + gi)
                    flush()
                    S.dma("pool", bls[rows, :], lfb[0:n, 0, :], reads=[lfb], writes=[outb])
                    cumsum_tiles(lfb[0:n, 0, :], lfb, 1, n, NT, NT)
                    nb = make_nbias(NT + 1, NT)
                    chk(4)
                    run(a_attn(0, n, [(0, 128, 0), (1, 128, 1), (2, 128, 2), (3, 128, 3), (4, n, 4)], 0))
                    chk(5)
                    run(b_attn(QTbt, n, [(j, 128, 0, False) for j in range(NT)] + [(NT, n, 0, True)], nb))
                    o_out(0, n, NT + b)
                    chk(6)

                chk(7)
                S.op("pool", lambda e: e.memset(carry[:, 0, :], 0.0), reads=[carry], writes=[carry])
                ipres = {}

                def inproj_gen(I):
                    QTbt = QTb_r.nxt()
                    lfb = lfb_r.nxt()
                    prevn = None
                    for t in range(4):
                        g = 4 * I + t
                        x = xt_r.nxt()
                        S.dma("sp", x[:], xp[g * 128:(g + 1) * 128, :], writes=[x])
                        xn_ = norm_a(x, 128)
                        if prevn is not None:
                            norm_b(prevn[0], 128, hT[:, :, prevn[1] * 128:(prevn[1] + 1) * 128], hT_b[prevn[1]])
                        prevn = (xn_, t)
                        yield
                    norm_b(prevn[0], 128, hT[:, :, prevn[1] * 128:(prevn[1] + 1) * 128], hT_b[prevn[1]])
                    for gi in (3, 4, 5, 6, 0, 1, 2):
                        wg = wg_r.nxt()
                        S.dma("sp", wg[:, :, 0:min(512, INC - gi * 512)], WinS[gi].rearrange("p (c n) -> p c n", c=8)[:, :, 0:min(512, INC - gi * 512)], reads=[WinSb[gi]], writes=[wg])
                        for t in range(4):
                            g = 4 * I + t
                            rows = slice(g * 128, (g + 1) * 128)
                            pb = in_proj_group(gi, wg, lambda c: hT[:, c, t * 128:(t + 1) * 128], hT_b[t], 128)
                            if g >= 28:
                                ar = slice((g - 28) * 128, (g - 27) * 128)
                                outs = (akp[ar, :], avp[ar, :], bkp[rows, :], bvp[rows, :])
                            else:
                                outs = (None, None, bkp[rows, :], bvp[rows, :])
                            defer(tile_groups(gi, pb, 128, QTbt, t * 128, g, g % 8, outs, lfb[:, t, :], lfb))
                            yield
                    flush()
                    S.dma("pool", blp[I * 512:(I + 1) * 512, :].rearrange("(t p) h -> p t h", p=128), lfb[:, :, :], reads=[lfb], writes=[outb])
                    cumsum_tiles(lfb[:, :, :].rearrange("p t h -> p (t h)"), lfb, 4, 128, 4 * I, 4 * I)
                    nb = make_nbias(4 * I + 4, 4 * I)
                    ipres[I] = (QTbt, nb)

                run(inproj_gen(0))
                for I in range(8):
                    QTbt, nb = ipres[I]
                    def attn_chain(I=I, QTbt=QTbt, nb=nb):
                        for t in range(4):
                            g = 4 * I + t
                            ktsa = [(gg % 8, 128, gg - g + 4) for gg in range(max(0, g - 4), g + 1)]
                            yield from a_attn(t * 128, 128, ktsa, t)
                        ktsb = [(j, 128, 0, False) for j in range(4 * I)] + [(4 * I + t, 128, 128 * t, True) for t in range(4)]
                        yield from b_attn(QTbt, 512, ktsb, nb)
                    nsteps = 32 + 8 * (4 * I + 4)
                    if I < 7:
                        interleave(attn_chain(), nsteps, inproj_gen(I + 1), 33)
                    else:
                        run(attn_chain())
                    for t in range(4):
                        g = 4 * I + t
                        o_out(t, 128, g)
                    if I == 0:
                        chk(8)
                chk(9)
                S.barrier()

            with contextlib.ExitStack() as st2:
                stg = alloc_stg(st2, "b", 5)
                Wmq = sb(st2, "Wmq", [128, 8, D], BF16); Wmo = sb(st2, "Wmo", [128, 8, D], BF16)
                load_w(Wmq, w_mq, 8, D, g2c)
                load_w(Wmo, w_mo, 8, D, None)
                tcp_r, sqh_r, knb_r = alloc_hn(st2, "b")
                EV[0] = "act"
                Wo = sb(st2, "Wo", [128, 8, D], BF16)
                load_w(Wo, w_o, 8, D, None)
                KmT_p = sb(st2, "KmT_p", [128, 8, 256], BF16)
                Vm_p = sb(st2, "Vm_p", [128, 2, D], BF16)
                with contextlib.ExitStack() as stk:
                    wkv = sb(stk, "wkv", [128, 8, 2 * D], BF16)
                    load_w(wkv, w_mkv, 8, 2 * D, gmc)
                    hTm = sb(stk, "hTm", [128, 8, 128], BF16)
                    vfm_r = Rot([sb(stk, "vfm%d" % i, [128, 512]) for i in range(2)])
                    for a in range(2):
                        x = xt_r.nxt()
                        S.dma("sp", x[:], memp[a * 128:(a + 1) * 128, :], writes=[x])
                        norm_T(x, 128, hTm[:], hTm)
                        for g in range(4):
                            pb = pm.nxt()
                            for c in range(8):
                                mm(pb[:, :], hTm[:, c, :], wkv[:, c, g * 512:(g + 1) * 512], c == 0, c == 7, [hTm, wkv], [pb], sig=(c == 7))
                            if g < 2:
                                kf, kb = headnorm(pb, 128, 2, gmk_bc, True)
                                S.dma("pool", mkp[a * 128:(a + 1) * 128, g * 512:(g + 1) * 512], kf[:], reads=[kf], writes=[outb])
                                transp_to(kb, 128, 4, KmT_p[:, g * 4:(g + 1) * 4, a * 128:(a + 1) * 128], [KmT_p])
                            else:
                                vf = vfm_r.nxt()
                                S.op("act", lambda e: e.copy(out=vf[:], in_=pb[:, :]), reads=[pb], writes=[vf])
                                S.dma("pool", mvp[a * 128:(a + 1) * 128, (g - 2) * 512:(g - 1) * 512], vf[:], reads=[vf], writes=[outb])
                                S.op("dve", lambda e: e.tensor_copy(out=Vm_p[:, a, (g - 2) * 512:(g - 1) * 512], in_=vf[:, :]), reads=[vf], writes=[Vm_p])
                    S.barrier()
                OT_r = Rot([sb(st2, "OT%d" % i, [128, 8, 128], BF16) for i in range(2)])
                ob_r = Rot([sb(st2, "ob%d" % i, [128, D], BF16) for i in range(2)])
                KmT_s = sb(st2, "KmT_s", [128, 8, 256], BF16); Vm_s = sb(st2, "Vm_s", [128, 2, D], BF16)
                hT2s = [sb(st2, "hT2_%d" % i, [128, 8, 512], BF16) for i in range(2)]
                hT2_bs = [[Buf("hT2_%d_%d" % (i, t)) for t in range(4)] for i in range(2)]
                qTs = [sb(st2, "qT_%d" % i, [128, 8, 512], BF16) for i in range(2)]
                qT_bs = [[Buf("qT%d_%d" % (i, t)) for t in range(4)] for i in range(2)]
                x1_r = Rot([sb(st2, "x1t%d" % i, [128, D]) for i in range(8)])
                PTm_r = Rot([sb(st2, "PTm%d" % i, [128, 2, 512], BF16) for i in range(3)])
                OmT = sb(st2, "OmT", [128, 8, 512], BF16)
                rcm_r = Rot([sb(st2, "rcm%d" % i, [128, 512]) for i in range(2)])
                psS = Rot([bank[2], bank[3]])
                psO = Rot([bank[4], bank[5], bank[6]])

                def mb_front(tiles, n, par, out):
                    nt = len(tiles)
                    ntok = nt * n
                    xts = [None] * nt
                    stt = [dict() for _ in range(nt)]
                    hT2 = hT2s[par]; hT2_b = hT2_bs[par]; qT = qTs[par]; qT_b = qT_bs[par]
                    out.update(xts=xts, tiles=tiles, n=n, par=par)

                    def s0(t):
                        (src, osrc, obuf, dst, dbuf) = tiles[t]
                        x = x1_r.nxt()
                        xts[t] = x
                        S.dma("sp", x[0:n, :], src, writes=[x])
                        ob = ob_r.nxt()
                        S.dma("sp", ob[0:n, :], osrc, reads=[obuf], writes=[ob])
                        OT = OT_r.nxt()
                        transp_to(ob, n, 8, OT[:, :, 0:n], [OT])
                        stt[t]["OT"] = OT

                    def s1(t):
                        x = xts[t]
                        OT = stt[t]["OT"]
                        for half in range(2):
                            pb = pm.nxt()
                            for c in range(8):
                                mm(pb[0:n, :], OT[:, c, 0:n], Wo[:, c, half * 512:(half + 1) * 512], c == 0, c == 7, [OT, Wo], [pb], sig=(c == 7))
                            S.op("dve", lambda e: e.tensor_tensor(out=x[0:n, half * 512:(half + 1) * 512], in0=pb[0:n, :], in1=x[0:n, half * 512:(half + 1) * 512], op=ALU.add), reads=[pb, x], writes=[x])
                        stt[t]["xn"] = norm_a(x, n)

                    def s2(t):
                        norm_b(stt[t]["xn"], n, hT2[:, :, t * n:(t + 1) * n], hT2_b[t])

                    def mq(t, half):
                        pb = pm.nxt()
                        for c in range(8):
                            mm(pb[0:n, :], hT2[:, c, t * n:(t + 1) * n], Wmq[:, c, half * 512:(half + 1) * 512], c == 0, c == 7, [hT2_b[t], Wmq], [pb], sig=(c == 7))
                        stt[t]["hn%d" % half] = hn_a(pb, n, 2)

                    def mqb(t, half):
                        _, qb = hn_b(stt[t]["hn%d" % half], gmq_bc, False)
                        stt[t]["qb%d" % half] = qb

                    def qtr(t, half):
                        transp_to(stt[t]["qb%d" % half], n, 4, qT[:, half * 4:(half + 1) * 4, t * n:(t + 1) * n], [qT_b[t]])

                    def s3(t):
                        mq(t, 0)

                    def s4(t):
                        mqb(t, 0)
                        mq(t, 1)

                    def s5(t):
                        mqb(t, 1)
                        qtr(t, 0)

                    def s6(t):
                        qtr(t, 1)

                    stages = [s0, s1, s2, s3, s4, s5, s6]
                    for step in range(nt + len(stages) - 1):
                        for si in reversed(range(len(stages))):
                            t = step - si
                            if 0 <= t < nt:
                                stages[si](t)
                        yield

                def mb_back(st_, KmT, Vm):
                    xts = st_["xts"]; tiles = st_["tiles"]; n = st_["n"]; par = st_["par"]
                    nt = len(tiles)
                    ntok = nt * n
                    qT = qTs[par]; qT_b = qT_bs[par]

                    def hA(h):
                        PTm = PTm_r.nxt()
                        for a in range(2):
                            pst = psS.nxt()
                            for dc in range(2):
                                mm(pst[:, 0:ntok], KmT[:, h * 2 + dc, a * 128:(a + 1) * 128], qT[:, h * 2 + dc, 0:ntok], dc == 0, dc == 1, [KmT] + qT_b[0:nt], [pst], sig=(dc == 1))
                            S.op("act", lambda e: e.activation(out=PTm[:, a, 0:ntok], in_=pst[:, 0:ntok], func=AF.Exp, scale=1.0 / 16), reads=[pst], writes=[PTm])
                        return PTm

                    def hB(h, PTm):
                        psum_ = psO.nxt()
                        for a in range(2):
                            mm(psum_[:, 0:ntok], ones_b[:, :], PTm[:, a, 0:ntok], a == 0, a == 1, [ones_b, PTm], [psum_], sig=(a == 1))
                        rcm = rcm_r.nxt()
                        S.op("dve", lambda e: e.reciprocal(out=rcm[:, 0:ntok], in_=psum_[:, 0:ntok]), reads=[psum_], writes=[rcm])
                        for dc in range(2):
                            pov = psO.nxt()
                            for a in range(2):
                                mm(pov[:, 0:ntok], Vm[:, a, h * 256 + dc * 128:h * 256 + (dc + 1) * 128], PTm[:, a, 0:ntok], a == 0, a == 1, [Vm, PTm], [pov], sig=(a == 1))
                            S.op("dve", lambda e: e.tensor_tensor(out=OmT[:, h * 2 + dc, 0:ntok], in0=pov[:, 0:ntok], in1=rcm[:, 0:ntok], op=ALU.mult), reads=[pov, rcm], writes=[OmT])

                    prev = None
                    for h in range(4):
                        PTm = hA(h)
                        if prev is not None:
                            hB(*prev)
                        prev = (h, PTm)
                        yield
                    hB(*prev)
                    yield
                    for t, (src, osrc, obuf, dst, dbuf) in enumerate(tiles):
                        x = xts[t]
                        for half in range(2):
                            pb = pm.nxt()
                            for c in range(8):
                                mm(pb[0:n, :], OmT[:, c, t * n:(t + 1) * n], Wmo[:, c, half * 512:(half + 1) * 512], c == 0, c == 7, [OmT, Wmo], [pb], sig=(c == 7))
                            S.op("dve", lambda e: e.tensor_tensor(out=x[0:n, half * 512:(half + 1) * 512], in0=pb[0:n, :], in1=x[0:n, half * 512:(half + 1) * 512], op=ALU.add), reads=[pb, x], writes=[x])
                        S.dma("pool", dst, x[0:n, :], reads=[x], writes=[dbuf])
                        yield

                def mem_block(tiles, n, KmT, Vm):
                    st_ = {}
                    run(mb_front(tiles, n, 0, st_))
                    run(mb_back(st_, KmT, Vm))

                if phases >= 2:
                    pblocks = []
                    for I in range(8):
                        tiles = []
                        for t in range(4):
                            g = 4 * I + t
                            tiles.append((xp[g * 128:(g + 1) * 128, :], Osc[g * 128:(g + 1) * 128, :], Ob[g], X2[g * 128:(g + 1) * 128, :], X2b[g]))
                        pblocks.append(tiles)
                    sts = [dict() for _ in range(8)]
                    run(mb_front(pblocks[0], 128, 0, sts[0]))
                    for I in range(8):
                        back = mb_back(sts[I], KmT_p, Vm_p)
                        if I < 7:
                            interleave(back, 9, mb_front(pblocks[I + 1], 128, (I + 1) % 2, sts[I + 1]), 10)
                        else:
                            run(back)
                    for b in range(2):
                        for a in range(2):
                            kc = x1_r.nxt()
                            S.dma("sp", kc[:], cmk[b, a * 128:(a + 1) * 128, :], writes=[kc])
                            for half in range(2):
                                kb = knb_r.nxt()
                                cast(kb[:], kc[:, half * 512:(half + 1) * 512], None, [kc], [kb])
                                transp_to(kb, 128, 4, KmT_s[:, half * 4:(half + 1) * 4, a * 128:(a + 1) * 128], [KmT_s])
                            vc = x1_r.nxt()
                            S.dma("sp", vc[:], cmv[b, a * 128:(a + 1) * 128, :], writes=[vc])
                            cast(Vm_s[:, a, :], vc[:], None, [vc], [Vm_s])
                        r0 = SEQ + b * NS
                        mem_block([(xs[b * NS:(b + 1) * NS, :], Osc[r0:r0 + NS, :], Ob[NT + b], X2[r0:r0 + NS, :], X2b[NT + b])], NS, KmT_s, Vm_s)
                chk(10)
                S.barrier()

            with contextlib.ExitStack() as st3:
                Wup = sb(st3, "Wup", [128, 8, 2 * DFF], BF16)
                Wdn = sb(st3, "Wdn", [128, NF, D], BF16)
                if phases >= 3:
                    with contextlib.ExitStack() as stw:
                        stg = alloc_stg(stw, "c", 8)
                        load_w(Wup, w_up, 8, 2 * DFF, g3c)
                        load_w(Wdn, w_down, NF, D, None)
                        S.barrier()
                TB = 256
                hT3s = [sb(st3, "hT3_%d" % i, [128, 8, TB], BF16) for i in range(2)]
                hT3_b = [[Buf("hT3_%d_%d" % (i, t)) for t in range(2)] for i in range(2)]
                actT = sb(st3, "actT", [128, NF, TB], BF16)
                x2_r = Rot([sb(st3, "x2t%d" % i, [128, D]) for i in range(4)])
                gst_r = Rot([sb(st3, "gst%d" % i, [128, TB + 2]) for i in range(3)])
                cv_r = Rot([sb(st3, "cv%d" % i, [128, TB]) for i in range(3)])
                sl_r = Rot([sb(st3, "sl%d" % i, [128, TB]) for i in range(3)])
                gprev = sb(st3, "gprev", [128, NF, 2])
                psG = Rot([bank[2], bank[3]])
                psV = Rot([bank[4], bank[5], bank[6]])

                def ffn_prep_l(tiles, n):
                    xts = []
                    for t, (src, sbuf_, dst) in enumerate(tiles):
                        x = x2_r.nxt()
                        xts.append(x)
                        S.dma("sp", x[0:n, :], src, reads=[sbuf_], writes=[x])
                    return [xts, None, n]

                def ffn_prep_a1(prep):
                    prep[1] = [norm_a1(x, prep[2]) for x in prep[0]]

                def ffn_prep_a(tiles, n):
                    prep = ffn_prep_l(tiles, n)
                    ffn_prep_a1(prep)
                    return prep

                def ffn_prep_a2(prep):
                    prep[1] = [norm_a2(x, prep[2], st) for x, st in zip(prep[0], prep[1])]

                def ffn_prep_b(prep, n, hb):
                    for t, xn in enumerate(prep[1]):
                        norm_b(xn, n, hT3s[hb][:, :, t * n:(t + 1) * n], hT3_b[hb][t])

                def ffn_main(tiles, n, xts, hb, hook):
                    nt = len(tiles)
                    ntok = nt * n
                    hT3 = hT3s[hb]
                    hbufs = hT3_b[hb][0:nt]
                    pendm = []
                    pends = []
                    for f in range(NF):
                        pg = psG.nxt(); pv = psV.nxt()
                        hook(f)
                        for c in range(8):
                            mm(pg[:, 0:ntok], Wup[:, c, f * 128:(f + 1) * 128], hT3[:, c, 0:ntok], c == 0, c == 7, [Wup] + hbufs, [pg], sig=(c == 7))
                        for c in range(8):
                            mm(pv[:, 0:ntok], Wup[:, c, DFF + f * 128:DFF + (f + 1) * 128], hT3[:, c, 0:ntok], c == 0, c == 7, [Wup] + hbufs, [pv], sig=(c == 7))
                        gst = gst_r.nxt(); cv = cv_r.nxt(); sl = sl_r.nxt()
                        S.op("pool", lambda e: e.tensor_copy(out=gst[:, 0:2], in_=gprev[:, f, :]), reads=[gprev], writes=[gst])
                        S.op("act", lambda e: e.copy(out=gst[:, 2:2 + ntok], in_=pg[:, 0:ntok]), reads=[pg], writes=[gst])
                        S.op("pool", lambda e: e.tensor_copy(out=gprev[:, f, :], in_=gst[:, ntok:ntok + 2]), reads=[gst], writes=[gprev])
                        S.op("act", lambda e: e.activation(out=cv[:, 0:ntok], in_=gst[:, 0:ntok], func=AF.Identity, scale=wc[:, 0, f:f + 1], bias=bcv[:, f:f + 1]), reads=[gst, wc, bcv], writes=[cv])
                        if pends:
                            pends.pop(0)()
                        S.op("dve", lambda e: e.scalar_tensor_tensor(out=cv[:, 0:ntok], in0=gst[:, 1:1 + ntok], scalar=wc[:, 1, f:f + 1], in1=cv[:, 0:ntok], op0=ALU.mult, op1=ALU.add), reads=[gst, wc, cv], writes=[cv])
                        S.op("dve", lambda e: e.scalar_tensor_tensor(out=cv[:, 0:ntok], in0=gst[:, 2:2 + ntok], scalar=wc[:, 2, f:f + 1], in1=cv[:, 0:ntok], op0=ALU.mult, op1=ALU.add), reads=[gst, wc, cv], writes=[cv])
                        if pendm:
                            pendm.pop(0)()
                        pends.append(lambda cv=cv, sl=sl: S.op("act", lambda e: e.activation(out=sl[:, 0:ntok], in_=cv[:, 0:ntok], func=AF.Silu), reads=[cv], writes=[sl]))
                        pendm.append(lambda f=f, pv=pv, sl=sl: S.op("dve", lambda e: e.tensor_tensor(out=actT[:, f, 0:ntok], in0=pv[:, 0:ntok], in1=sl[:, 0:ntok], op=ALU.mult), reads=[pv, sl], writes=[actT]))
                    while pends:
                        pends.pop(0)()
                    while pendm:
                        pendm.pop(0)()
                    for t, (src, sbuf_, dst) in enumerate(tiles):
                        x = xts[t]
                        for half in range(2):
                            pb = pm.nxt()
                            for f in range(NF):
                                mm(pb[0:n, :], actT[:, f, t * n:(t + 1) * n], Wdn[:, f, half * 512:(half + 1) * 512], f == 0, f == NF - 1, [actT, Wdn], [pb], sig=(f == NF - 1))
                            S.op("dve", lambda e: e.tensor_tensor(out=x[0:n, half * 512:(half + 1) * 512], in0=pb[0:n, :], in1=x[0:n, half * 512:(half + 1) * 512], op=ALU.add), reads=[pb, x], writes=[x])
                        S.dma("pool", dst, x[0:n, :], reads=[x], writes=[outb])

                if phases >= 3:
                    S.op("pool", lambda e: e.memset(gprev[:], 0.0), writes=[gprev])
                    blocks = []
                    for I in range(SEQ // TB):
                        tiles = []
                        for t in range(TB // 128):
                            g = (TB // 128) * I + t
                            tiles.append((X2[g * 128:(g + 1) * 128, :], X2b[g], y_p[g * 128:(g + 1) * 128, :]))
                        blocks.append((tiles, 128, None))
                    for b in range(2):
                        r0 = SEQ + b * NS
                        blocks.append(([(X2[r0:r0 + NS, :], X2b[NT + b], y_s[b * NS:(b + 1) * NS, :])], NS, b))
                    RS[0] = "pool"
                    preps = {0: ffn_prep_a(blocks[0][0], blocks[0][1])}
                    ffn_prep_a2(preps[0])
                    ffn_prep_b(preps[0], blocks[0][1], 0)
                    for i, (tiles, n, sb_) in enumerate(blocks):
                        nxt = blocks[i + 1] if i + 1 < len(blocks) else None

                        def hook(f, i=i, nxt=nxt):
                            if nxt is None:
                                return
                            if f == 0:
                                preps[i + 1] = ffn_prep_l(nxt[0], nxt[1])
                            if f == 5:
                                ffn_prep_a1(preps[i + 1])
                            if f == 9:
                                ffn_prep_a2(preps[i + 1])
                            if f == 13:
                                ffn_prep_b(preps[i + 1], nxt[1], (i + 1) % 2)
                        if sb_ is not None:
                            if sb_ == 0:
                                for j in range(2):
                                    S.dma("pool", cvp[j:j + 1, :].rearrange("o (c p) -> p (o c)", p=128), gprev[:, :, j], reads=[gprev], writes=[outb], allow_slow_non_contiguous=True)
                            for j in range(2):
                                S.dma("sp", gprev[:, :, j], scv[sb_, j:j + 1, :].rearrange("o (c p) -> p (o c)", p=128), reads=[gprev], writes=[gprev], allow_slow_non_contiguous=True)
                        ffn_main(tiles, n, preps[i][0], i % 2, hook)
                        if sb_ is not None:
                            for j in range(2):
                                S.dma("pool", cvs[sb_, j:j + 1, :].rearrange("o (c p) -> p (o c)", p=128), gprev[:, :, j], reads=[gprev], writes=[outb], allow_slow_non_contiguous=True)
                S.barrier()
        except _Stop:
            pass
        S.stopped = False
        S.barrier()
        print("ops", S.nops, "waits", S.nwaits, flush=True)
    return nc


_NC = None


def kernel(**inp):
    global _NC
    f = lambda a: np.ascontiguousarray(np.asarray(a, dtype=np.float32))
    if _NC is None:
        _NC = build()
    nc = _NC
    shared = dict(
        w_in=f(inp["w_in"][0]), b_f=f(inp["b_f"]), g_qa=f(inp["g_qa"]), g_ka=f(inp["g_ka"]), rel=f(inp["rel_bias"][0]),
        g_qb=f(inp["g_qb"]), g_kb=f(inp["g_kb"]), w_o=f(inp["w_o"][0]), g1=f(inp["g_norm1"]), g2=f(inp["g_norm2"]),
        gmem=f(inp["g_mem"]), w_mq=f(inp["w_mq"][0]), w_mkv=f(inp["w_mkv"][0]), g_mq=f(inp["g_mq"]), g_mk=f(inp["g_mk"]),
        w_mo=f(inp["w_mo"][0]), g3=f(inp["g_norm3"]), w_up=f(inp["w_up"][0]), w_conv=f(inp["w_conv"][0]),
        b_conv=f(inp["b_conv"]), w_down=f(inp["w_down"][0]))
    in_maps = []
    for c in range(8):
        s = slice(2 * c, 2 * c + 2)
        m = dict(shared)
        m.update(
            xp=f(inp["x_prompt"][c]), xs=f(inp["x_sample"][s]).reshape(2 * NS, D),
            cak=f(inp["cache_a_k"][0, s]).reshape(2, 512, 512), cav=f(inp["cache_a_v"][0, s]).reshape(2, 512, 512),
            cbk=f(inp["cache_b_k"][0, s]).reshape(2, SEQ, 512), cbv=f(inp["cache_b_v"][0, s]).reshape(2, SEQ, 512),
            cbl=f(inp["cache_b_logf"][0, s]), cmk=f(inp["cache_mem_k"][0, s]).reshape(2, 256, D),
            cmv=f(inp["cache_mem_v"][0, s]).reshape(2, 256, D), scv=f(inp["state_conv"][0, s]), memp=f(inp["mem_prompt"][c]))
        in_maps.append(m)
    res = run_bass_kernel_spmd(nc, in_maps, core_ids=list(range(8)))
    R = res.results
    cat = lambda k: np.stack([np.asarray(R[c][k], dtype=np.float32) for c in range(8)], 0)
    y_p = cat("y_p")
    y_s = cat("y_s").reshape(16, NS, D)
    akp = cat("akp").reshape(1, 8, 512, 8, 64); avp = cat("avp").reshape(1, 8, 512, 8, 64)
    bkp = cat("bkp").reshape(1, 8, SEQ, 8, 64); bvp = cat("bvp").reshape(1, 8, SEQ, 8, 64)
    blp = cat("blp").reshape(1, 8, SEQ, 8)
    mkp = cat("mkp").reshape(1, 8, 256, 4, 256); mvp = cat("mvp").reshape(1, 8, 256, 4, 256)
    cvp = cat("cvp").reshape(1, 8, 2, DFF)
    aks = cat("aks").reshape(1, 16, NS, 8, 64); avs = cat("avs").reshape(1, 16, NS, 8, 64)
    bks = cat("bks").reshape(1, 16, NS, 8, 64); bvs = cat("bvs").reshape(1, 16, NS, 8, 64)
    bls = cat("bls").reshape(1, 16, NS, 8)
    cvs = cat("cvs").reshape(1, 16, 2, DFF)
    return (y_p, y_s, akp, avp, bkp, bvp, blp, mkp, mvp, cvp, aks, avs, bks, bvs, bls, cvs)
```

```python
import contextlib
import numpy as np
import concourse.bass as bass
import concourse.mybir as mybir
from concourse.bass_utils import run_bass_kernel_spmd

F32 = mybir.dt.float32
BF16 = mybir.dt.bfloat16
AF = mybir.ActivationFunctionType
ALU = mybir.AluOpType
AX = mybir.AxisListType
NEG = -30000.0


class Buf:
    __slots__ = ("name", "w", "r")

    def __init__(self, name):
        self.name = name
        self.w = {}
        self.r = {}


class Tl(Buf):
    __slots__ = ("t",)

    def __init__(self, t, name):
        super().__init__(name)
        self.t = t

    def __getitem__(self, i):
        return self.t[i]


class Rot:
    def __init__(self, tls):
        self.tls = tls
        self.i = 0

    def nxt(self):
        t = self.tls[self.i]
        self.i = (self.i + 1) % len(self.tls)
        return t


class Sched:
    NDMA = 32

    def __init__(self, nc, es):
        self.nc = nc
        self.eng = {"pe": nc.tensor, "act": nc.scalar, "dve": nc.vector,
                    "pool": nc.gpsimd, "sp": nc.sync}
        self.sems = {}
        for e in self.eng:
            self.sems[e] = es.enter_context(nc.semaphore("s_" + e))
        for i in range(self.NDMA):
            self.sems[("dma", i)] = es.enter_context(nc.semaphore("s_dma%d" % i))
        self.cnt = {k: 0 for k in self.sems}
        self.waited = {e: {} for e in self.eng}
        self.rr = 0
        self.rrq = [0, 0]
        self.nwaits = 0
        self.nops = {e: 0 for e in self.eng}
        self.stopped = False

    def _wait(self, e, deps):
        w = self.waited[e]
        for k, v in deps.items():
            if k == e and e in ("pe", "sp"):
                continue
            if w.get(k, 0) >= v:
                continue
            self.eng[e].wait_ge(self.sems[k], v)
            self.nwaits += 1
            w[k] = v

    @staticmethod
    def _merge(d, src):
        for k, v in src.items():
            if d.get(k, 0) < v:
                d[k] = v

    def op(self, e, fn, reads=(), writes=(), sig=True):
        if self.stopped:
            return None
        deps = {}
        for b in reads:
            self._merge(deps, b.w)
        for b in writes:
            self._merge(deps, b.w)
            self._merge(deps, b.r)
        self._wait(e, deps)
        ins = fn(self.eng[e])
        self.nops[e] += 1
        if sig:
            self.cnt[e] += 1
            ins.then_inc(self.sems[e], 1)
            t = self.cnt[e]
        else:
            t = self.cnt[e] + 1
        for b in reads:
            if b.r.get(e, 0) < t:
                b.r[e] = t
        for b in writes:
            b.w = {e: t}
            b.r = {}
        return ins

    def dma(self, q, out, in_, reads=(), writes=(), **kw):
        if self.stopped:
            return None
        half = self.NDMA // 2
        qi = 0 if q == "sp" else 1
        i = qi * half + self.rrq[qi]
        self.rrq[qi] = (self.rrq[qi] + 1) % half
        k = ("dma", i)
        deps = {}
        if self.cnt[k] > 0:
            deps[k] = self.cnt[k]
        for b in reads:
            self._merge(deps, b.w)
        for b in writes:
            self._merge(deps, b.w)
            self._merge(deps, b.r)
        self._wait(q, deps)
        ins = self.eng[q].dma_start(out=out, in_=in_, **kw)
        self.nops[q] += 1
        self.cnt[k] += 16
        ins.then_inc(self.sems[k], 16)
        t = self.cnt[k]
        for b in reads:
            b.r[k] = t
        for b in writes:
            b.w = {k: t}
            b.r = {}
        return ins

    def barrier(self, engines=("pe", "act", "dve", "pool", "sp")):
        if self.stopped:
            return
        deps = {k: v for k, v in self.cnt.items() if v > 0}
        for e in engines:
            d = {k: v for k, v in deps.items() if k != e}
            w = self.waited[e]
            for k, v in d.items():
                if w.get(k, 0) >= v:
                    continue
                self.eng[e].wait_ge(self.sems[k], v)
                w[k] = v


D = 1024
SEQ = 4096
NT = 32
DFF = 2816
NF = 22
INC = 3080
NS = 64


class _Stop(Exception):
    pass


def build(phases=3, stage=99):
    nc = bass.Bass("TRN2", target_bir_lowering=False)

    def chk(k):
        if stage == k:
            S.stopped = True

    H = {}

    def din(n, shape):
        H[n] = nc.dram_tensor(n, list(shape), F32, kind="ExternalInput")
        return H[n].ap()

    def dout(n, shape):
        H[n] = nc.dram_tensor(n, list(shape), F32, kind="ExternalOutput")
        return H[n].ap()

    xp = din("xp", [SEQ, D]); xs = din("xs", [2 * NS, D])
    cak = din("cak", [2, 512, 512]); cav = din("cav", [2, 512, 512])
    cbk = din("cbk", [2, SEQ, 512]); cbv = din("cbv", [2, SEQ, 512]); cbl = din("cbl", [2, SEQ, 8])
    cmk = din("cmk", [2, 256, D]); cmv = din("cmv", [2, 256, D]); scv = din("scv", [2, 2, DFF])
    memp = din("memp", [256, D])
    w_in = din("w_in", [D, INC]); b_f = din("b_f", [1, 8])
    g_qa = din("g_qa", [1, 64]); g_ka = din("g_ka", [1, 64]); rel = din("rel", [8, 257])
    g_qb = din("g_qb", [1, 64]); g_kb = din("g_kb", [1, 64])
    w_o = din("w_o", [D, D]); g1 = din("g1", [1, D]); g2 = din("g2", [1, D]); gmem = din("gmem", [1, D])
    w_mq = din("w_mq", [D, D]); w_mkv = din("w_mkv", [D, 2 * D]); g_mq = din("g_mq", [1, 256]); g_mk = din("g_mk", [1, 256])
    w_mo = din("w_mo", [D, D]); g3 = din("g3", [1, D]); w_up = din("w_up", [D, 2 * DFF])
    w_conv = din("w_conv", [3, DFF]); b_conv = din("b_conv", [1, DFF]); w_down = din("w_down", [DFF, D])

    y_p = dout("y_p", [SEQ, D]); y_s = dout("y_s", [2 * NS, D])
    akp = dout("akp", [512, 512]); avp = dout("avp", [512, 512])
    bkp = dout("bkp", [SEQ, 512]); bvp = dout("bvp", [SEQ, 512]); blp = dout("blp", [SEQ, 8])
    mkp = dout("mkp", [256, D]); mvp = dout("mvp", [256, D]); cvp = dout("cvp", [2, DFF])
    aks = dout("aks", [2 * NS, 512]); avs = dout("avs", [2 * NS, 512])
    bks = dout("bks", [2 * NS, 512]); bvs = dout("bvs", [2 * NS, 512]); bls = dout("bls", [2 * NS, 8])
    cvs = dout("cvs", [2, 2, DFF])

    X1 = nc.dram_tensor("X1", [SEQ + 2 * NS, D], F32, kind="Internal").ap()
    X2 = nc.dram_tensor("X2", [SEQ + 2 * NS, D], F32, kind="Internal").ap()
    Esc_h = nc.dram_tensor("Esc", [8, 512], F32, kind="Internal")
    Esc = Esc_h.ap()
    Osc = nc.dram_tensor("Osc", [SEQ + 2 * NS, D], BF16, kind="Internal").ap()
    Ob = [Buf("Osc%d" % i) for i in range(NT + 2)]
    WinS = nc.dram_tensor("WinS", [7, 128, 8 * 512], BF16, kind="Internal").ap()
    X1b = [Buf("X1_%d" % i) for i in range(NT + 2)]
    X2b = [Buf("X2_%d" % i) for i in range(NT + 2)]
    outb = Buf("outputs")
    WinSb = [Buf("WinS%d" % i) for i in range(7)]

    es = contextlib.ExitStack()
    with es:
        S = Sched(nc, es)
        try:

            def sb(st, name, shape, dt=F32):
                return Tl(st.enter_context(nc.sbuf_tensor(name, list(shape), dt)), name)

            def psb(st, name, shape, dt=F32):
                return Tl(st.enter_context(nc.psum_tensor(name, list(shape), dt)), name)

            bank = [psb(es, "bk%d" % i, [128, 512]) for i in range(7)]
            ptb = psb(es, "bkT", [128, 1024], BF16)

            cf = sb(es, "cf", [128, 128])
            ident = sb(es, "ident", [128, 128], BF16)
            J = sb(es, "J", [128, 128], BF16)
            U = sb(es, "U", [128, 128])
            ones_f = sb(es, "ones_f", [128, 128])
            ones_b = sb(es, "ones_b", [128, 128], BF16)
            Mdiag = sb(es, "Mdiag", [128, 128], BF16)
            Mask0 = sb(es, "Mask0", [128, 128], BF16)
            mhalf = sb(es, "mhalf", [128, 8])
            one_col = sb(es, "one_col", [128, 1])
            eps_col = sb(es, "eps_col", [128, 1])
            Hb = sb(es, "Hb", [128, 8, 2, 128], BF16)

            def aff(dst_bf, init, pattern, cmp, fill, base, cm):
                S.op("pool", lambda e: e.memset(cf[:], init), writes=[cf])
                S.op("pool", lambda e: e.affine_select(out=cf[:], in_=cf[:], pattern=pattern, compare_op=cmp, fill=fill, base=base, channel_multiplier=cm), reads=[cf], writes=[cf])
                if dst_bf is not None:
                    S.op("pool", lambda e: e.tensor_copy(out=dst_bf[:], in_=cf[:]), reads=[cf], writes=[dst_bf])

            aff(ident, 0.0, [[-1, 128]], ALU.not_equal, 1.0, 0, 1)
            aff(J, 0.0, [[1, 128]], ALU.not_equal, 1.0, -127, 1)
            aff(Mdiag, 0.0, [[1, 128]], ALU.is_ge, NEG, 0, -1)
            S.op("pool", lambda e: e.memset(U[:], 1.0), writes=[U])
            S.op("pool", lambda e: e.affine_select(out=U[:], in_=U[:], pattern=[[1, 128]], compare_op=ALU.is_ge, fill=0.0, base=0, channel_multiplier=-1), reads=[U], writes=[U])
            S.op("pool", lambda e: e.memset(ones_f[:], 1.0), writes=[ones_f])
            S.op("pool", lambda e: e.memset(ones_b[:], 1.0), writes=[ones_b])
            S.op("pool", lambda e: e.memset(Mask0[:], 0.0), writes=[Mask0])
            S.op("pool", lambda e: e.memset(Mask0[0:64, 64:128], NEG), writes=[Mask0])
            S.op("pool", lambda e: e.memset(mhalf[:], -0.5), writes=[mhalf])
            S.op("pool", lambda e: e.memset(one_col[:], 1.0), writes=[one_col])
            S.op("pool", lambda e: e.memset(eps_col[:], 1e-6), writes=[eps_col])

            def bc_load(name, src, n):
                t = sb(es, name, [128, n])
                S.dma("sp", t[:], src[0:1, 0:n].broadcast_to([128, n]), writes=[t])
                return t

            gqa_bc = bc_load("gqa_bc", g_qa, 64); gka_bc = bc_load("gka_bc", g_ka, 64)
            gqb_bc = bc_load("gqb_bc", g_qb, 64); gkb_bc = bc_load("gkb_bc", g_kb, 64)
            gmq_bc = bc_load("gmq_bc", g_mq, 256); gmk_bc = bc_load("gmk_bc", g_mk, 256)
            bf_bc = bc_load("bf_bc", b_f, 8)

            def col_load(name, src):
                t = sb(es, name, [128, 8])
                S.dma("sp", t[:], src.rearrange("o (c p) -> p (o c)", p=128), writes=[t], allow_slow_non_contiguous=True)
                return t

            g1c = col_load("g1c", g1); g2c = col_load("g2c", g2); g3c = col_load("g3c", g3); gmc = col_load("gmc", gmem)
            wc = sb(es, "wc", [128, 3, NF])
            for j in range(3):
                S.dma("sp", wc[:, j, :], w_conv[j:j + 1, :].rearrange("o (c p) -> p (o c)", p=128), writes=[wc], allow_slow_non_contiguous=True)
            bcv = sb(es, "bcv", [128, NF])
            S.dma("sp", bcv[:], b_conv.rearrange("o (c p) -> p (o c)", p=128), writes=[bcv], allow_slow_non_contiguous=True)

            stg = None

            def alloc_stg(st, tag, k):
                return Rot([sb(st, "stg%s%d" % (tag, i), [128, 512]) for i in range(k)])
            cast_i = [0]

            def cast(out_ap, in_ap, gcol_ap, reads, writes):
                e = "act" if cast_i[0] % 2 == 0 else "dve"
                cast_i[0] += 1
                if e == "act":
                    if gcol_ap is None:
                        S.op(e, lambda en: en.copy(out=out_ap, in_=in_ap), reads=reads, writes=writes)
                    else:
                        S.op(e, lambda en: en.activation(out=out_ap, in_=in_ap, func=AF.Copy, scale=gcol_ap), reads=reads, writes=writes)
                elif gcol_ap is None:
                    S.op(e, lambda en: en.tensor_copy(out=out_ap, in_=in_ap), reads=reads, writes=writes)
                else:
                    S.op(e, lambda en: en.tensor_scalar(out=out_ap, in0=in_ap, scalar1=gcol_ap, scalar2=None, op0=ALU.mult), reads=reads, writes=writes)

            def load_w(dst, src, nch, ncols, gcol):
                for c in range(nch):
                    for c0 in range(0, ncols, 512):
                        c1 = min(ncols, c0 + 512)
                        s = stg.nxt()
                        S.dma("sp", s[:, 0:c1 - c0], src[c * 128:(c + 1) * 128, c0:c1], writes=[s])
                        cast(dst[:, c, c0:c1], s[:, 0:c1 - c0], None if gcol is None else gcol[:, c:c + 1], [s] + ([gcol] if gcol is not None else []), [dst])

            xt_r = Rot([sb(es, "xt%d" % i, [128, D]) for i in range(2)])
            sqn = sb(es, "sqn", [128, D], BF16)
            xn_r = Rot([sb(es, "xn%d" % i, [128, D], BF16) for i in range(2)])
            st_r = Rot([sb(es, "st%d" % i, [128, 2]) for i in range(4)])

            EV = ["act"]

            def _cp(e):
                return e.copy if EV[0] == "act" else e.tensor_copy

            def evac(fn, reads=(), writes=()):
                S.op(EV[0], fn, reads=reads, writes=writes)

            RS = ["auto"]

            def rstd(out_ap, ss_ap, dim, n, k, buf):
                if EV[0] == "act" and RS[0] != "pool":
                    S.op("act", lambda e: e.activation(out=out_ap, in_=ss_ap, func=AF.Ln, scale=1.0 / dim, bias=eps_col[0:n, :]), reads=[buf, eps_col], writes=[buf])
                    S.op("act", lambda e: e.activation(out=out_ap, in_=out_ap, func=AF.Exp, scale=-0.5), reads=[buf], writes=[buf])
                else:
                    S.op("pool", lambda e: e.tensor_scalar(out=ss_ap, in0=ss_ap, scalar1=1.0 / dim, scalar2=1e-6, op0=ALU.mult, op1=ALU.add), reads=[buf], writes=[buf])
                    S.op("pool", lambda e: e.tensor_tensor(out=out_ap, in0=ss_ap, in1=mhalf[0:n, 0:k], op=ALU.pow), reads=[buf, mhalf], writes=[buf])

            def norm_a1(x, n):
                st = st_r.nxt()
                S.op("dve", lambda e: e.tensor_tensor(out=sqn[0:n, :], in0=x[0:n, :], in1=x[0:n, :], op=ALU.mult), reads=[x], writes=[sqn])
                S.op("dve", lambda e: e.tensor_reduce(out=st[0:n, 0:1], in_=sqn[0:n, :], axis=AX.X, op=ALU.add), reads=[sqn], writes=[st])
                rstd(st[0:n, 1:2], st[0:n, 0:1], D, n, 1, st)
                return st

            def norm_a2(x, n, st):
                xn = xn_r.nxt()
                S.op("dve", lambda e: e.tensor_scalar(out=xn[0:n, :], in0=x[0:n, :], scalar1=st[0:n, 1:2], scalar2=None, op0=ALU.mult), reads=[x, st], writes=[xn])
                return xn

            def norm_a(x, n):
                return norm_a2(x, n, norm_a1(x, n))

            def norm_b(xn, n, dst_ap, dst_buf):
                for c in range(8):
                    S.op("pe", lambda e: e.transpose(out=ptb[:, c * n:(c + 1) * n], in_=xn[0:n, c * 128:(c + 1) * 128], identity=ident[0:n, 0:n]), reads=[xn, ident], writes=[ptb], sig=(c == 7))
                evac(lambda e: _cp(e)(out=dst_ap, in_=ptb[:, 0:8 * n].rearrange("p (c t) -> p c t", c=8)), reads=[ptb], writes=[dst_buf])

            def norm_T(x, n, dst_ap, dst_buf):
                norm_b(norm_a(x, n), n, dst_ap, dst_buf)

            tcp_r = sqh_r = knb_r = None

            def alloc_hn(st, tag):
                return (Rot([sb(st, "tcp%s%d" % (tag, i), [128, 512]) for i in range(4)]),
                        Rot([sb(st, "sqh%s%d" % (tag, i), [128, 512], BF16) for i in range(2)]),
                        Rot([sb(st, "knb%s%d" % (tag, i), [128, 512], BF16) for i in range(4)]))
            hs_r = Rot([sb(es, "hs%d" % i, [128, 16]) for i in range(6)])

            def hn_a(src, n, nh):
                hd = 512 // nh
                tcp = tcp_r.nxt(); sqh = sqh_r.nxt(); hs = hs_r.nxt()
                v3 = lambda t: t[0:n, :].rearrange("p (h d) -> p h d", h=nh)
                evac(lambda e: _cp(e)(out=tcp[0:n, :], in_=src[0:n, :]), reads=[src], writes=[tcp])
                S.op("dve", lambda e: e.tensor_tensor(out=sqh[0:n, :], in0=tcp[0:n, :], in1=tcp[0:n, :], op=ALU.mult), reads=[tcp], writes=[sqh])
                S.op("dve", lambda e: e.tensor_reduce(out=hs[0:n, 0:nh], in_=v3(sqh), axis=AX.X, op=ALU.add), reads=[sqh], writes=[hs])
                rstd(hs[0:n, 8:8 + nh], hs[0:n, 0:nh], hd, n, nh, hs)
                return (tcp, hs, n, nh)

            def hn_b(stt_, gbc, want_f32):
                (tcp, hs, n, nh) = stt_
                hd = 512 // nh
                knb = knb_r.nxt()
                v3 = lambda t: t[0:n, :].rearrange("p (h d) -> p h d", h=nh)
                S.op("dve", lambda e: e.tensor_tensor(out=v3(tcp), in0=v3(tcp), in1=hs[0:n, 8:8 + nh].unsqueeze(2).broadcast_to([n, nh, hd]), op=ALU.mult), reads=[tcp, hs], writes=[tcp])
                gb3 = gbc[0:n, 0:hd].unsqueeze(1).broadcast_to([n, nh, hd])
                if want_f32:
                    S.op("dve", lambda e: e.tensor_tensor(out=v3(tcp), in0=v3(tcp), in1=gb3, op=ALU.mult), reads=[tcp, gbc], writes=[tcp])
                    evac(lambda e: _cp(e)(out=knb[0:n, :], in_=tcp[0:n, :]), reads=[tcp], writes=[knb])
                    return tcp, knb
                S.op("dve", lambda e: e.tensor_tensor(out=v3(knb), in0=v3(tcp), in1=gb3, op=ALU.mult), reads=[tcp, gbc], writes=[knb])
                return None, knb

            def headnorm(src, n, nh, gbc, want_f32):
                return hn_b(hn_a(src, n, nh), gbc, want_f32)

            def transp_to(src_bf, n, nblk, dst_ap, dst_bufs):
                for i in range(nblk):
                    S.op("pe", lambda e: e.transpose(out=ptb[:, i * n:(i + 1) * n], in_=src_bf[0:n, i * 128:(i + 1) * 128], identity=ident[0:n, 0:n]), reads=[src_bf, ident], writes=[ptb], sig=(i == nblk - 1))
                evac(lambda e: _cp(e)(out=dst_ap, in_=ptb[:, 0:nblk * n].rearrange("p (c t) -> p c t", c=nblk)), reads=[ptb], writes=dst_bufs)

            def mm(out, lhsT, rhs, start, stop, reads, writes, sig=True, **kw):
                S.op("pe", lambda e: e.matmul(out, lhsT=lhsT, rhs=rhs, start=start, stop=stop, **kw), reads=reads, writes=writes, sig=sig)

            pm = Rot([bank[0], bank[1]])
            with contextlib.ExitStack() as st0:
                Esb = sb(st0, "Esb", [8, 512]); t256 = sb(st0, "t256", [8, 1]); Hf = sb(st0, "Hf", [128, 8, 2, 128])
                S.dma("sp", Esb[:, 0:257], rel[:, :], writes=[Esb])
                S.op("dve", lambda e: e.tensor_copy(out=t256[:], in_=Esb[:, 256:257]), reads=[Esb], writes=[t256])
                S.op("dve", lambda e: e.tensor_copy(out=Esb[:, 257:512], in_=t256[:, 0:1].broadcast_to([8, 255])), reads=[t256], writes=[Esb])
                S.op("dve", lambda e: e.tensor_scalar(out=Esb[:], in0=Esb[:], scalar1=t256[:, 0:1], scalar2=8.0, op0=ALU.subtract, op1=ALU.mult), reads=[Esb, t256], writes=[Esb])
                eb = Buf("Esc")
                S.dma("pool", Esc[:, :], Esb[:], reads=[Esb], writes=[eb])
                stg = alloc_stg(st0, "a", 8)
                wst_r = Rot([sb(st0, "wst%d" % i, [128, 8, 512], BF16) for i in range(2)])
                for gi in range(7):
                    c0 = gi * 512
                    w = min(512, INC - c0)
                    wst = wst_r.nxt()
                    for c in range(8):
                        s = stg.nxt()
                        S.dma("sp", s[:, 0:w], w_in[c * 128:(c + 1) * 128, c0:c0 + w], writes=[s])
                        cast(wst[:, c, 0:w], s[:, 0:w], g1c[:, c:c + 1], [s, g1c], [wst])
                    S.dma("pool", WinS[gi].rearrange("p (c n) -> p c n", c=8)[:, :, 0:w], wst[:, :, 0:w], reads=[wst], writes=[WinSb[gi]])
                for h in range(8):
                    S.dma("sp", Hf[:, h, :, :], bass.AP(Esc_h, h * 512 + 1, [[1, 128], [128, 2], [1, 128]]), reads=[eb], writes=[Hf])
                S.op("dve", lambda e: e.tensor_copy(out=Hb[:], in_=Hf[:]), reads=[Hf], writes=[Hb])
                S.op("pool", lambda e: e.memset(Hb[0:64, :, 0, 0:64], NEG), reads=[Hb], writes=[Hb])
                chk(2)
                S.barrier()

            with contextlib.ExitStack() as st1:
                tcp_r, sqh_r, knb_r = alloc_hn(st1, "a")
                EV[0] = "dve"
                wg_r = Rot([sb(st1, "wg%d" % i, [128, 8, 512], BF16) for i in range(2)])
                KTb = sb(st1, "KTb", [128, 4, SEQ + NS], BF16); KTb_b = [Buf("KTb%d" % j) for j in range(NT + 1)]
                Vst = sb(st1, "Vst", [128, NT + 1, 8, 65], BF16); Vst_b = [Buf("Vst%d" % j) for j in range(NT + 1)]
                KTa = sb(st1, "KTa", [128, 4, 1024], BF16); KTa_b = [Buf("KTa%d" % j) for j in range(8)]
                VA = sb(st1, "VA", [128, 8, 8, 65], BF16); VA_b = [Buf("VA%d" % j) for j in range(8)]
                S.op("pool", lambda e: e.memset(Vst[:, :, :, 64:65], 1.0), writes=Vst_b)
                S.op("pool", lambda e: e.memset(VA[:, :, :, 64:65], 1.0), writes=VA_b)
                hT = sb(st1, "hT", [128, 8, 512], BF16); hT_b = [Buf("hT%d" % t) for t in range(4)]
                QTb_r = Rot([sb(st1, "QTb%d" % i, [128, 8, 512], BF16) for i in range(2)])
                QTa = sb(st1, "QTa", [128, 8, 512], BF16)
                for qt_ in QTb_r.tls + [QTa]:
                    S.op("pool", lambda e: e.memset(qt_[:], 0.0), writes=[qt_])
                vf_r = Rot([sb(st1, "vf%d" % i, [128, 512]) for i in range(2)])
                PT_r = Rot([sb(st1, "PT%d" % i, [128, 512], BF16) for i in range(5)])
                PTa_r = Rot([sb(st1, "PTa%d" % i, [128, 5, 128], BF16) for i in range(2)])
                Oblk = sb(st1, "Oblk", [128, 4, D], BF16)
                lfb_r = Rot([sb(st1, "lfb%d" % i, [128, 4, 8]) for i in range(2)])
                lft = sb(st1, "lft", [128, 8]); lfe = sb(st1, "lfe", [128, 8])
                c_all = sb(st1, "c_all", [128, NT + 1, 8])
                S.op("pool", lambda e: e.memset(c_all[:], 0.0), writes=[c_all])
                nbias_r = Rot([sb(st1, "nbias%d" % i, [128, NT + 1, 8]) for i in range(2)])
                carry = sb(st1, "carry", [128, NT + 2, 8])
                rec_r = Rot([sb(st1, "rec%d" % i, [128, 4, 1]) for i in range(4)])
                lfc = sb(st1, "lfc", [128, NT, 8])
                psT = Rot([bank[2], bank[3], bank[4]])
                po = Rot([bank[5], bank[6]])

                def in_proj_group(gi, wg, hslice, hbuf, n):
                    pb = pm.nxt()
                    w = min(512, INC - gi * 512)
                    for c in range(8):
                        mm(pb[0:n, 0:w], hslice(c), wg[:, c, 0:w], c == 0, c == 7, [hbuf, wg], [pb], sig=(c == 7))
                    return pb

                def logf_from(pb, n, dst_ap, dst_buf):
                    S.op("dve", lambda e: e.tensor_tensor(out=lft[0:n, :], in0=pb[0:n, 0:8], in1=bf_bc[0:n, :], op=ALU.add), reads=[pb, bf_bc], writes=[lft])
                    S.op("act", lambda e: e.activation(out=lfe[0:n, :], in_=lft[0:n, :], func=AF.Exp, scale=-1.0), reads=[lft], writes=[lfe])
                    S.op("act", lambda e: e.activation(out=lft[0:n, :], in_=lfe[0:n, :], func=AF.Ln, bias=one_col[0:n, :], scale=1.0), reads=[lfe, one_col], writes=[lft])
                    S.op("dve", lambda e: e.tensor_scalar(out=dst_ap, in0=lft[0:n, :], scalar1=-1.0, scalar2=None, op0=ALU.mult), reads=[lft], writes=[dst_buf])

                def tile_groups(gi, pb, n, QTbt, qcol, kt_idx, a_slot, outs, lf_ap, lf_buf):
                    o_ak, o_av, o_bk, o_bv = outs
                    if gi in (0, 1, 3, 4):
                        sta = hn_a(pb, n, 8)
                        box = {}

                        def part_b():
                            if gi == 0:
                                box["r"] = hn_b(sta, gqa_bc, False)
                            elif gi == 3:
                                box["r"] = hn_b(sta, gqb_bc, False)
                            elif gi == 1:
                                box["r"] = hn_b(sta, gka_bc, True)
                                if o_ak is not None:
                                    S.dma("pool", o_ak, box["r"][0][0:n, :], reads=[box["r"][0]], writes=[outb])
                            else:
                                box["r"] = hn_b(sta, gkb_bc, True)
                                S.dma("pool", o_bk, box["r"][0][0:n, :], reads=[box["r"][0]], writes=[outb])

                        def part_t():
                            kb = box["r"][1]
                            if gi == 0:
                                transp_q(kb, n, QTa, qcol)
                            elif gi == 3:
                                transp_q(kb, n, QTbt, qcol)
                            elif gi == 1:
                                transp_to(kb, n, 4, KTa[:, :, a_slot * 128:a_slot * 128 + n], [KTa_b[a_slot]])
                            else:
                                transp_to(kb, n, 4, KTb[:, :, kt_idx * 128:kt_idx * 128 + n], [KTb_b[kt_idx]])
                        return [part_b, part_t]
                    elif gi in (2, 5):
                        vf = vf_r.nxt()
                        evac(lambda e: _cp(e)(out=vf[0:n, :], in_=pb[0:n, :]), reads=[pb], writes=[vf])
                        o = o_av if gi == 2 else o_bv
                        if o is not None:
                            S.dma("pool", o, vf[0:n, :], reads=[vf], writes=[outb])
                        if gi == 2:
                            S.op("dve", lambda e: e.tensor_copy(out=VA[0:n, a_slot, :, 0:64], in_=vf[0:n, :].rearrange("p (h d) -> p h d", h=8)), reads=[vf], writes=[VA_b[a_slot]])
                        else:
                            S.op("dve", lambda e: e.tensor_copy(out=Vst[0:n, kt_idx, :, 0:64], in_=vf[0:n, :].rearrange("p (h d) -> p h d", h=8)), reads=[vf], writes=[Vst_b[kt_idx]])
                    else:
                        logf_from(pb, n, lf_ap, lf_buf)
                    return None

                def transp_q(src_bf, n, QT, qcol):
                    for i in range(4):
                        S.op("pe", lambda e: e.transpose(out=ptb[:, i * n:(i + 1) * n], in_=src_bf[0:n, i * 128:(i + 1) * 128], identity=ident[0:n, 0:n]), reads=[src_bf, ident], writes=[ptb], sig=(i == 3))
                    for e2 in range(2):
                        evac(lambda e: _cp(e)(out=QT.t[:, :, :].rearrange("p (a e) t -> p a e t", e=2)[e2 * 64:(e2 + 1) * 64, :, e2, qcol:qcol + n],
                                                     in_=ptb[e2 * 64:(e2 + 1) * 64, 0:4 * n].rearrange("p (c t) -> p c t", c=4)), reads=[ptb], writes=[QT])

                pend = []

                def defer(fns):
                    for ent in list(pend):
                        ent.pop(0)()
                        if not ent:
                            pend.remove(ent)
                    if fns:
                        pend.append(list(fns))

                def flush():
                    while pend:
                        defer(None)

                def a_attn(q0, nq, ktiles, trow):
                    jjs = [k[2] for k in ktiles]
                    jlo = min(j for j in jjs if j >= 1)
                    kts = sorted(ktiles, key=lambda k: (k[2] == 0, k[2]))
                    state = {}

                    def qk(h):
                        pr, bp = h // 2, 64 * (h % 2)
                        pX = psT.nxt()
                        pY = psT.nxt() if 0 in jjs else None
                        for (slot, nk, jj) in kts:
                            dst = pY[0:nk, 0:nq] if jj == 0 else pX[0:nk, (jj - 1) * 128:(jj - 1) * 128 + nq]
                            dbuf = pY if jj == 0 else pX
                            extra = jj in (3, 4) or (jj == 0 and nq == 128)
                            mm(dst, KTa[:, pr, slot * 128:slot * 128 + nk], QTa[:, h, q0:q0 + nq], True, not extra,
                               [KTa_b[slot], QTa], [dbuf], sig=not extra)
                            if jj == 3:
                                mm(dst, J[:, :], Hb[:, h, 1, 0:nq], False, True, [J, Hb], [dbuf])
                            elif jj == 4:
                                mm(dst, J[:, 0:nk], Hb[:, h, 0, 0:nq], False, True, [J, Hb], [dbuf])
                            elif jj == 0 and nq == 128:
                                mm(dst, ident[:, :], Mask0[:, :], False, True, [ident, Mask0], [dbuf])
                        PTa = PTa_r.nxt()
                        nk4 = [k[1] for k in kts if k[2] == 4][0]
                        jhi = 5 if nk4 == 128 else 4
                        if jhi > jlo:
                            S.op("act", lambda e: e.activation(out=PTa[:, jlo:jhi, 0:nq], in_=pX[:, (jlo - 1) * 128:(jhi - 1) * 128].rearrange("p (j c) -> p j c", c=128)[:, :, 0:nq], func=AF.Exp, scale=0.125),
                                 reads=[pX], writes=[PTa])
                        if jhi == 4:
                            S.op("act", lambda e: e.activation(out=PTa[0:nk4, 4, 0:nq], in_=pX[0:nk4, 384:384 + nq], func=AF.Exp, scale=0.125), reads=[pX], writes=[PTa])
                        if pY is not None:
                            S.op("act", lambda e: e.activation(out=PTa[:, 0, 0:nq], in_=pY[:, 0:nq], func=AF.Exp, scale=0.125), reads=[pY], writes=[PTa])
                        return PTa

                    def pv(h, PTa):
                        hg, hh = h // 4, h % 4
                        if hh == 0:
                            state["pob"] = po.nxt()
                        pob = state["pob"]
                        for i, (slot, nk, jj) in enumerate(ktiles):
                            last = i == len(ktiles) - 1
                            mm(pob[0:nq, hh * 65:(hh + 1) * 65], PTa[0:nk, jj, 0:nq], VA[0:nk, slot, h, :], i == 0, last, [PTa, VA_b[slot]], [pob], sig=last)
                        if hh == 3:
                            rec = rec_r.nxt()
                            p3 = pob[0:nq, 0:260].rearrange("p (t c) -> p t c", c=65)
                            S.op("dve", lambda e: e.reciprocal(out=rec[0:nq, :, :], in_=p3[:, :, 64:65]), reads=[pob], writes=[rec])
                            S.op("dve", lambda e: e.tensor_tensor(out=Oblk[0:nq, trow, hg * 256:(hg + 1) * 256].rearrange("p (h d) -> p h d", h=4), in0=p3[:, :, 0:64],
                                                                  in1=rec[0:nq, :, :].broadcast_to([nq, 4, 64]), op=ALU.mult), reads=[pob, rec], writes=[Oblk])

                    prev = None
                    for h in range(8):
                        PTa = qk(h)
                        if prev is not None:
                            pv(*prev)
                        prev = (h, PTa)
                        yield
                    pv(*prev)

                def b_attn(QTbt, nq, ktiles, nbias):
                    ntq = (nq + 127) // 128
                    qw = min(nq, 128)
                    pobs = {}

                    def qk(h, kt):
                        (j, nk, c0, diag) = kt
                        pr, bp = h // 2, 64 * (h % 2)
                        pst = psT.nxt()
                        mm(pst[0:nk, c0:nq], KTb[:, pr, j * 128:j * 128 + nk], QTbt[:, h, c0:nq], True, not diag,
                           [KTb_b[j], QTbt], [pst], sig=not diag)
                        if diag:
                            w = min(128, nq - c0)
                            mm(pst[0:nk, c0:c0 + w], ident[:, 0:nk], Mdiag[:, 0:w], False, True, [ident, Mdiag], [pst])
                        PT = PT_r.nxt()
                        S.op("act", lambda e: e.activation(out=PT[0:nk, c0:nq], in_=pst[0:nk, c0:nq], func=AF.Exp, scale=0.125, bias=nbias[0:nk, j, h:h + 1]),
                             reads=[pst, nbias], writes=[PT])
                        return PT

                    def pv(h, kt, PT, first, last):
                        (j, nk, c0, diag) = kt
                        if first:
                            pobs[h] = po.nxt()
                            assert c0 == 0
                        pob = pobs[h]
                        for t in range(c0 // 128, ntq):
                            mm(pob[0:qw, t * 65:(t + 1) * 65], PT[0:nk, t * 128:t * 128 + qw], Vst[0:nk, j, h, :], bool(first and t == 0), False,
                               [PT, Vst_b[j]], [pob], sig=(t == ntq - 1), skip_group_check=True)
                        if last:
                            rec = rec_r.nxt()
                            p3 = pob[0:qw, 0:ntq * 65].rearrange("p (t c) -> p t c", c=65)
                            S.op("dve", lambda e: e.reciprocal(out=rec[0:qw, 0:ntq, :], in_=p3[:, :, 64:65]), reads=[pob], writes=[rec])
                            S.op("dve", lambda e: e.tensor_tensor(out=Oblk[0:qw, 0:ntq, 512 + h * 64:512 + (h + 1) * 64], in0=p3[:, :, 0:64],
                                                                  in1=rec[0:qw, 0:ntq, :].broadcast_to([qw, ntq, 64]), op=ALU.mult), reads=[pob, rec], writes=[Oblk])

                    items = [(h, kt, i == 0, i == len(ktiles) - 1) for h in range(8) for i, kt in enumerate(ktiles)]
                    pq = []
                    for (h, kt, first, last) in items:
                        PT = qk(h, kt)
                        pq.append((h, kt, PT, first, last))
                        if len(pq) > 3:
                            pv(*pq.pop(0))
                        yield
                    while pq:
                        pv(*pq.pop(0))

                def run(gen):
                    for _ in gen:
                        pass

                def interleave(g1, n1, g2, n2):
                    acc = 0
                    done2 = False
                    for _ in g1:
                        acc += n2
                        while acc >= n1 and not done2:
                            acc -= n1
                            try:
                                next(g2)
                            except StopIteration:
                                done2 = True
                    if not done2:
                        run(g2)

                def o_out(trow, n, gidx):
                    r0 = gidx * 128 if gidx < NT else SEQ + (gidx - NT) * NS
                    S.dma("pool", Osc[r0:r0 + n, :], Oblk[0:n, trow, :], reads=[Oblk], writes=[Ob[gidx]])

                def cumsum_tiles(lf_ap, lf_buf, nt, n, j0, carry_idx):
                    pcs = pm.nxt(); ptot = pm.nxt()
                    mm(pcs[0:n, 0:nt * 8], U[0:n, 0:n], lf_ap, True, True, [U, lf_buf], [pcs])
                    mm(ptot[:, 0:nt * 8], ones_f[0:n, :], lf_ap, True, True, [ones_f, lf_buf], [ptot])
                    for i in range(nt):
                        S.op("dve", lambda e: e.tensor_tensor(out=c_all[0:n, j0 + i, :], in0=pcs[0:n, i * 8:(i + 1) * 8], in1=carry[0:n, carry_idx + i, :], op=ALU.add), reads=[pcs, carry], writes=[c_all])
                        S.op("dve", lambda e: e.tensor_tensor(out=carry[:, carry_idx + i + 1, :], in0=ptot[:, i * 8:(i + 1) * 8], in1=carry[:, carry_idx + i, :], op=ALU.add), reads=[ptot, carry], writes=[carry])

                def make_nbias(nj, carry_idx):
                    nb = nbias_r.nxt()
                    S.op("dve", lambda e: e.scalar_tensor_tensor(out=nb[:, 0:nj, :], in0=c_all[:, 0:nj, :], scalar=-1.0, in1=carry[:, carry_idx:carry_idx + 1, :].broadcast_to([128, nj, 8]), op0=ALU.mult, op1=ALU.add),
                         reads=[c_all, carry], writes=[nb])
                    return nb

                for b in range(2):
                    n = NS
                    for j in range(NT):
                        kc = tcp_r.nxt()
                        S.dma("sp", kc[:], cbk[b, j * 128:(j + 1) * 128, :], writes=[kc])
                        kb = knb_r.nxt()
                        cast(kb[:], kc[:], None, [kc], [kb])
                        transp_to(kb, 128, 4, KTb[:, :, j * 128:(j + 1) * 128], [KTb_b[j]])
                        vc = tcp_r.nxt()
                        S.dma("sp", vc[:], cbv[b, j * 128:(j + 1) * 128, :], writes=[vc])
                        cast(Vst[:, j, :, 0:64], vc[:].rearrange("p (h d) -> p h d", h=8), None, [vc], [Vst_b[j]])
                    for j in range(4):
                        kc = tcp_r.nxt()
                        S.dma("sp", kc[:], cak[b, j * 128:(j + 1) * 128, :], writes=[kc])
                        kb = knb_r.nxt()
                        cast(kb[:], kc[:], None, [kc], [kb])
                        transp_to(kb, 128, 4, KTa[:, :, j * 128:(j + 1) * 128], [KTa_b[j]])
                        vc = tcp_r.nxt()
                        S.dma("sp", vc[:], cav[b, j * 128:(j + 1) * 128, :], writes=[vc])
                        cast(VA[:, j, :, 0:64], vc[:].rearrange("p (h d) -> p h d", h=8), None, [vc], [VA_b[j]])
                    S.dma("sp", lfc[:], cbl[b].rearrange("(j p) h -> p j h", p=128), writes=[lfc])
                    S.op("pool", lambda e: e.memset(carry[:, 0, :], 0.0), writes=[carry])
                    chk(3)
                    cumsum_tiles(lfc[:, :, :].rearrange("p j h -> p (j h)"), lfc, NT, 128, 0, 0)
                    chk(31)
                    x = xt_r.nxt()
                    S.dma("sp", x[0:n, :], xs[b * NS:(b + 1) * NS, :], writes=[x])
                    norm_T(x, n, hT[:, :, 0:n], hT_b[0])
                    chk(32)
                    QTbt = QTb_r.nxt()
                    lfb = lfb_r.nxt()
                    rows = slice(b * NS, (b + 1) * NS)
                    for gi in range(7):
                        wg = wg_r.nxt()
                        S.dma("sp", wg[:, :, 0:min(512, INC - gi * 512)], WinS[gi].rearrange("p (c n) -> p c n", c=8)[:, :, 0:min(512, INC - gi * 512)], reads=[WinSb[gi]], writes=[wg])
                        pb = in_proj_group(gi, wg, lambda c: hT[:, c, 0:n], hT_b[0], n)
                        if gi == 0:
                            chk(33)
                        defer(tile_groups(gi, pb, n, QTbt, 0, NT, 4, (aks[rows, :], avs[rows, :], bks[rows, :], bvs[rows, :]), lfb[0:n, 0, :], lfb))
                        chk(34 + gi)
                    flush()
                    S.dma("pool", bls[rows, :], lfb[0:n, 0, :], reads=[lfb], writes=[outb])
                    cumsum_tiles(lfb[0:n, 0, :], lfb, 1, n, NT, NT)
                    nb = make_nbias(NT + 1, NT)
                    chk(4)
                    run(a_attn(0, n, [(0, 128, 0), (1, 128, 1), (2, 128, 2), (3, 128, 3), (4, n, 4)], 0))
                    chk(5)
                    run(b_attn(QTbt, n, [(j, 128, 0, False) for j in range(NT)] + [(NT, n, 0, True)], nb))
                    o_out(0, n, NT + b)
                    chk(6)

                chk(7)
                S.op("pool", lambda e: e.memset(carry[:, 0, :], 0.0), reads=[carry], writes=[carry])
                ipres = {}

                def inproj_gen(I):
                    QTbt = QTb_r.nxt()
                    lfb = lfb_r.nxt()
                    prevn = None
                    for t in range(4):
                        g = 4 * I + t
                        x = xt_r.nxt()
                        S.dma("sp", x[:], xp[g * 128:(g + 1) * 128, :], writes=[x])
                        xn_ = norm_a(x, 128)
                        if prevn is not None:
                            norm_b(prevn[0], 128, hT[:, :, prevn[1] * 128:(prevn[1] + 1) * 128], hT_b[prevn[1]])
                        prevn = (xn_, t)
                        yield
                    norm_b(prevn[0], 128, hT[:, :, prevn[1] * 128:(prevn[1] + 1) * 128], hT_b[prevn[1]])
                    for gi in (3, 4, 5, 6, 0, 1, 2):
                        wg = wg_r.nxt()
                        S.dma("sp", wg[:, :, 0:min(512, INC - gi * 512)], WinS[gi].rearrange("p (c n) -> p c n", c=8)[:, :, 0:min(512, INC - gi * 512)], reads=[WinSb[gi]], writes=[wg])
                        for t in range(4):
                            g = 4 * I + t
                            rows = slice(g * 128, (g + 1) * 128)
                            pb = in_proj_group(gi, wg, lambda c: hT[:, c, t * 128:(t + 1) * 128], hT_b[t], 128)
                            if g >= 28:
                                ar = slice((g - 28) * 128, (g - 27) * 128)
                                outs = (akp[ar, :], avp[ar, :], bkp[rows, :], bvp[rows, :])
                            else:
                                outs = (None, None, bkp[rows, :], bvp[rows, :])
                            defer(tile_groups(gi, pb, 128, QTbt, t * 128, g, g % 8, outs, lfb[:, t, :], lfb))
                            yield
                    flush()
                    S.dma("pool", blp[I * 512:(I + 1) * 512, :].rearrange("(t p) h -> p t h", p=128), lfb[:, :, :], reads=[lfb], writes=[outb])
                    cumsum_tiles(lfb[:, :, :].rearrange("p t h -> p (t h)"), lfb, 4, 128, 4 * I, 4 * I)
                    nb = make_nbias(4 * I + 4, 4 * I)
                    ipres[I] = (QTbt, nb)

                run(inproj_gen(0))
                for I in range(8):
                    QTbt, nb = ipres[I]
                    def attn_chain(I=I, QTbt=QTbt, nb=nb):
                        for t in range(4):
                            g = 4 * I + t
                            ktsa = [(gg % 8, 128, gg - g + 4) for gg in range(max(0, g - 4), g + 1)]
                            yield from a_attn(t * 128, 128, ktsa, t)
                        ktsb = [(j, 128, 0, False) for j in range(4 * I)] + [(4 * I + t, 128, 128 * t, True) for t in range(4)]
                        yield from b_attn(QTbt, 512, ktsb, nb)
                    nsteps = 32 + 8 * (4 * I + 4)
                    if I < 7:
                        interleave(attn_chain(), nsteps, inproj_gen(I + 1), 33)
                    else:
                        run(attn_chain())
                    for t in range(4):
                        g = 4 * I + t
                        o_out(t, 128, g)
                    if I == 0:
                        chk(8)
                chk(9)
                S.barrier()

            with contextlib.ExitStack() as st2:
                stg = alloc_stg(st2, "b", 5)
                Wmq = sb(st2, "Wmq", [128, 8, D], BF16); Wmo = sb(st2, "Wmo", [128, 8, D], BF16)
                load_w(Wmq, w_mq, 8, D, g2c)
                load_w(Wmo, w_mo, 8, D, None)
                tcp_r, sqh_r, knb_r = alloc_hn(st2, "b")
                EV[0] = "act"
                Wo = sb(st2, "Wo", [128, 8, D], BF16)
                load_w(Wo, w_o, 8, D, None)
                KmT_p = sb(st2, "KmT_p", [128, 8, 256], BF16)
                Vm_p = sb(st2, "Vm_p", [128, 2, D], BF16)
                with contextlib.ExitStack() as stk:
                    wkv = sb(stk, "wkv", [128, 8, 2 * D], BF16)
                    load_w(wkv, w_mkv, 8, 2 * D, gmc)
                    hTm = sb(stk, "hTm", [128, 8, 128], BF16)
                    vfm_r = Rot([sb(stk, "vfm%d" % i, [128, 512]) for i in range(2)])
                    for a in range(2):
                        x = xt_r.nxt()
                        S.dma("sp", x[:], memp[a * 128:(a + 1) * 128, :], writes=[x])
                        norm_T(x, 128, hTm[:], hTm)
                        for g in range(4):
                            pb = pm.nxt()
                            for c in range(8):
                                mm(pb[:, :], hTm[:, c, :], wkv[:, c, g * 512:(g + 1) * 512], c == 0, c == 7, [hTm, wkv], [pb], sig=(c == 7))
                            if g < 2:
                                kf, kb = headnorm(pb, 128, 2, gmk_bc, True)
                                S.dma("pool", mkp[a * 128:(a + 1) * 128, g * 512:(g + 1) * 512], kf[:], reads=[kf], writes=[outb])
                                transp_to(kb, 128, 4, KmT_p[:, g * 4:(g + 1) * 4, a * 128:(a + 1) * 128], [KmT_p])
                            else:
                                vf = vfm_r.nxt()
                                S.op("act", lambda e: e.copy(out=vf[:], in_=pb[:, :]), reads=[pb], writes=[vf])
                                S.dma("pool", mvp[a * 128:(a + 1) * 128, (g - 2) * 512:(g - 1) * 512], vf[:], reads=[vf], writes=[outb])
                                S.op("dve", lambda e: e.tensor_copy(out=Vm_p[:, a, (g - 2) * 512:(g - 1) * 512], in_=vf[:, :]), reads=[vf], writes=[Vm_p])
                    S.barrier()
                OT_r = Rot([sb(st2, "OT%d" % i, [128, 8, 128], BF16) for i in range(2)])
                ob_r = Rot([sb(st2, "ob%d" % i, [128, D], BF16) for i in range(2)])
                KmT_s = sb(st2, "KmT_s", [128, 8, 256], BF16); Vm_s = sb(st2, "Vm_s", [128, 2, D], BF16)
                hT2s = [sb(st2, "hT2_%d" % i, [128, 8, 512], BF16) for i in range(2)]
                hT2_bs = [[Buf("hT2_%d_%d" % (i, t)) for t in range(4)] for i in range(2)]
                qTs = [sb(st2, "qT_%d" % i, [128, 8, 512], BF16) for i in range(2)]
                qT_bs = [[Buf("qT%d_%d" % (i, t)) for t in range(4)] for i in range(2)]
                x1_r = Rot([sb(st2, "x1t%d" % i, [128, D]) for i in range(8)])
                PTm_r = Rot([sb(st2, "PTm%d" % i, [128, 2, 512], BF16) for i in range(3)])
                OmT = sb(st2, "OmT", [128, 8, 512], BF16)
                rcm_r = Rot([sb(st2, "rcm%d" % i, [128, 512]) for i in range(2)])
                psS = Rot([bank[2], bank[3]])
                psO = Rot([bank[4], bank[5], bank[6]])

                def mb_front(tiles, n, par, out):
                    nt = len(tiles)
                    ntok = nt * n
                    xts = [None] * nt
                    stt = [dict() for _ in range(nt)]
                    hT2 = hT2s[par]; hT2_b = hT2_bs[par]; qT = qTs[par]; qT_b = qT_bs[par]
                    out.update(xts=xts, tiles=tiles, n=n, par=par)

                    def s0(t):
                        (src, osrc, obuf, dst, dbuf) = tiles[t]
                        x = x1_r.nxt()
                        xts[t] = x
                        S.dma("sp", x[0:n, :], src, writes=[x])
                        ob = ob_r.nxt()
                        S.dma("sp", ob[0:n, :], osrc, reads=[obuf], writes=[ob])
                        OT = OT_r.nxt()
                        transp_to(ob, n, 8, OT[:, :, 0:n], [OT])
                        stt[t]["OT"] = OT

                    def s1(t):
                        x = xts[t]
                        OT = stt[t]["OT"]
                        for half in range(2):
                            pb = pm.nxt()
                            for c in range(8):
                                mm(pb[0:n, :], OT[:, c, 0:n], Wo[:, c, half * 512:(half + 1) * 512], c == 0, c == 7, [OT, Wo], [pb], sig=(c == 7))
                            S.op("dve", lambda e: e.tensor_tensor(out=x[0:n, half * 512:(half + 1) * 512], in0=pb[0:n, :], in1=x[0:n, half * 512:(half + 1) * 512], op=ALU.add), reads=[pb, x], writes=[x])
                        stt[t]["xn"] = norm_a(x, n)

                    def s2(t):
                        norm_b(stt[t]["xn"], n, hT2[:, :, t * n:(t + 1) * n], hT2_b[t])

                    def mq(t, half):
                        pb = pm.nxt()
                        for c in range(8):
                            mm(pb[0:n, :], hT2[:, c, t * n:(t + 1) * n], Wmq[:, c, half * 512:(half + 1) * 512], c == 0, c == 7, [hT2_b[t], Wmq], [pb], sig=(c == 7))
                        stt[t]["hn%d" % half] = hn_a(pb, n, 2)

                    def mqb(t, half):
                        _, qb = hn_b(stt[t]["hn%d" % half], gmq_bc, False)
                        stt[t]["qb%d" % half] = qb

                    def qtr(t, half):
                        transp_to(stt[t]["qb%d" % half], n, 4, qT[:, half * 4:(half + 1) * 4, t * n:(t + 1) * n], [qT_b[t]])

                    def s3(t):
                        mq(t, 0)

                    def s4(t):
                        mqb(t, 0)
                        mq(t, 1)

                    def s5(t):
                        mqb(t, 1)
                        qtr(t, 0)

                    def s6(t):
                        qtr(t, 1)

                    stages = [s0, s1, s2, s3, s4, s5, s6]
                    for step in range(nt + len(stages) - 1):
                        for si in reversed(range(len(stages))):
                            t = step - si
                            if 0 <= t < nt:
                                stages[si](t)
                        yield

                def mb_back(st_, KmT, Vm):
                    xts = st_["xts"]; tiles = st_["tiles"]; n = st_["n"]; par = st_["par"]
                    nt = len(tiles)
                    ntok = nt * n
                    qT = qTs[par]; qT_b = qT_bs[par]

                    def hA(h):
                        PTm = PTm_r.nxt()
                        for a in range(2):
                            pst = psS.nxt()
                            for dc in range(2):
                                mm(pst[:, 0:ntok], KmT[:, h * 2 + dc, a * 128:(a + 1) * 128], qT[:, h * 2 + dc, 0:ntok], dc == 0, dc == 1, [KmT] + qT_b[0:nt], [pst], sig=(dc == 1))
                            S.op("act", lambda e: e.activation(out=PTm[:, a, 0:ntok], in_=pst[:, 0:ntok], func=AF.Exp, scale=1.0 / 16), reads=[pst], writes=[PTm])
                        return PTm

                    def hB(h, PTm):
                        psum_ = psO.nxt()
                        for a in range(2):
                            mm(psum_[:, 0:ntok], ones_b[:, :], PTm[:, a, 0:ntok], a == 0, a == 1, [ones_b, PTm], [psum_], sig=(a == 1))
                        rcm = rcm_r.nxt()
                        S.op("dve", lambda e: e.reciprocal(out=rcm[:, 0:ntok], in_=psum_[:, 0:ntok]), reads=[psum_], writes=[rcm])
                        for dc in range(2):
                            pov = psO.nxt()
                            for a in range(2):
                                mm(pov[:, 0:ntok], Vm[:, a, h * 256 + dc * 128:h * 256 + (dc + 1) * 128], PTm[:, a, 0:ntok], a == 0, a == 1, [Vm, PTm], [pov], sig=(a == 1))
                            S.op("dve", lambda e: e.tensor_tensor(out=OmT[:, h * 2 + dc, 0:ntok], in0=pov[:, 0:ntok], in1=rcm[:, 0:ntok], op=ALU.mult), reads=[pov, rcm], writes=[OmT])

                    prev = None
                    for h in range(4):
                        PTm = hA(h)
                        if prev is not None:
                            hB(*prev)
                        prev = (h, PTm)
                        yield
                    hB(*prev)
                    yield
                    for t, (src, osrc, obuf, dst, dbuf) in enumerate(tiles):
                        x = xts[t]
                        for half in range(2):
                            pb = pm.nxt()
                            for c in range(8):
                                mm(pb[0:n, :], OmT[:, c, t * n:(t + 1) * n], Wmo[:, c, half * 512:(half + 1) * 512], c == 0, c == 7, [OmT, Wmo], [pb], sig=(c == 7))
                            S.op("dve", lambda e: e.tensor_tensor(out=x[0:n, half * 512:(half + 1) * 512], in0=pb[0:n, :], in1=x[0:n, half * 512:(half + 1) * 512], op=ALU.add), reads=[pb, x], writes=[x])
                        S.dma("pool", dst, x[0:n, :], reads=[x], writes=[dbuf])
                        yield

                def mem_block(tiles, n, KmT, Vm):
                    st_ = {}
                    run(mb_front(tiles, n, 0, st_))
                    run(mb_back(st_, KmT, Vm))

                if phases >= 2:
                    pblocks = []
                    for I in range(8):
                        tiles = []
                        for t in range(4):
                            g = 4 * I + t
                            tiles.append((xp[g * 128:(g + 1) * 128, :], Osc[g * 128:(g + 1) * 128, :], Ob[g], X2[g * 128:(g + 1) * 128, :], X2b[g]))
                        pblocks.append(tiles)
                    sts = [dict() for _ in range(8)]
                    run(mb_front(pblocks[0], 128, 0, sts[0]))
                    for I in range(8):
                        back = mb_back(sts[I], KmT_p, Vm_p)
                        if I < 7:
                            interleave(back, 9, mb_front(pblocks[I + 1], 128, (I + 1) % 2, sts[I + 1]), 10)
                        else:
                            run(back)
                    for b in range(2):
                        for a in range(2):
                            kc = x1_r.nxt()
                            S.dma("sp", kc[:], cmk[b, a * 128:(a + 1) * 128, :], writes=[kc])
                            for half in range(2):
                                kb = knb_r.nxt()
                                cast(kb[:], kc[:, half * 512:(half + 1) * 512], None, [kc], [kb])
                                transp_to(kb, 128, 4, KmT_s[:, half * 4:(half + 1) * 4, a * 128:(a + 1) * 128], [KmT_s])
                            vc = x1_r.nxt()
                            S.dma("sp", vc[:], cmv[b, a * 128:(a + 1) * 128, :], writes=[vc])
                            cast(Vm_s[:, a, :], vc[:], None, [vc], [Vm_s])
                        r0 = SEQ + b * NS
                        mem_block([(xs[b * NS:(b + 1) * NS, :], Osc[r0:r0 + NS, :], Ob[NT + b], X2[r0:r0 + NS, :], X2b[NT + b])], NS, KmT_s, Vm_s)
                chk(10)
                S.barrier()

            with contextlib.ExitStack() as st3:
                Wup = sb(st3, "Wup", [128, 8, 2 * DFF], BF16)
                Wdn = sb(st3, "Wdn", [128, NF, D], BF16)
                if phases >= 3:
                    with contextlib.ExitStack() as stw:
                        stg = alloc_stg(stw, "c", 8)
                        load_w(Wup, w_up, 8, 2 * DFF, g3c)
                        load_w(Wdn, w_down, NF, D, None)
                        S.barrier()
                TB = 256
                hT3s = [sb(st3, "hT3_%d" % i, [128, 8, TB], BF16) for i in range(2)]
                hT3_b = [[Buf("hT3_%d_%d" % (i, t)) for t in range(2)] for i in range(2)]
                actT = sb(st3, "actT", [128, NF, TB], BF16)
                x2_r = Rot([sb(st3, "x2t%d" % i, [128, D]) for i in range(4)])
                gst_r = Rot([sb(st3, "gst%d" % i, [128, TB + 2]) for i in range(3)])
                cv_r = Rot([sb(st3, "cv%d" % i, [128, TB]) for i in range(3)])
                sl_r = Rot([sb(st3, "sl%d" % i, [128, TB]) for i in range(3)])
                gprev = sb(st3, "gprev", [128, NF, 2])
                psG = Rot([bank[2], bank[3]])
                psV = Rot([bank[4], bank[5], bank[6]])

                def ffn_prep_a(tiles, n):
                    xts, sts = [], []
                    for t, (src, sbuf_, dst) in enumerate(tiles):
                        x = x2_r.nxt()
                        xts.append(x)
                        S.dma("sp", x[0:n, :], src, reads=[sbuf_], writes=[x])
                        sts.append(norm_a1(x, n))
                    return [xts, sts, n]

                def ffn_prep_a2(prep):
                    prep[1] = [norm_a2(x, prep[2], st) for x, st in zip(prep[0], prep[1])]

                def ffn_prep_b(prep, n, hb):
                    for t, xn in enumerate(prep[1]):
                        norm_b(xn, n, hT3s[hb][:, :, t * n:(t + 1) * n], hT3_b[hb][t])

                def ffn_main(tiles, n, xts, hb, hook):
                    nt = len(tiles)
                    ntok = nt * n
                    hT3 = hT3s[hb]
                    hbufs = hT3_b[hb][0:nt]
                    pendm = []
                    pends = []
                    for f in range(NF):
                        pg = psG.nxt(); pv = psV.nxt()
                        hook(f)
                        for c in range(8):
                            mm(pg[:, 0:ntok], Wup[:, c, f * 128:(f + 1) * 128], hT3[:, c, 0:ntok], c == 0, c == 7, [Wup] + hbufs, [pg], sig=(c == 7))
                        for c in range(8):
                            mm(pv[:, 0:ntok], Wup[:, c, DFF + f * 128:DFF + (f + 1) * 128], hT3[:, c, 0:ntok], c == 0, c == 7, [Wup] + hbufs, [pv], sig=(c == 7))
                        gst = gst_r.nxt(); cv = cv_r.nxt(); sl = sl_r.nxt()
                        S.op("pool", lambda e: e.tensor_copy(out=gst[:, 0:2], in_=gprev[:, f, :]), reads=[gprev], writes=[gst])
                        S.op("act", lambda e: e.copy(out=gst[:, 2:2 + ntok], in_=pg[:, 0:ntok]), reads=[pg], writes=[gst])
                        S.op("pool", lambda e: e.tensor_copy(out=gprev[:, f, :], in_=gst[:, ntok:ntok + 2]), reads=[gst], writes=[gprev])
                        S.op("act", lambda e: e.activation(out=cv[:, 0:ntok], in_=gst[:, 0:ntok], func=AF.Identity, scale=wc[:, 0, f:f + 1], bias=bcv[:, f:f + 1]), reads=[gst, wc, bcv], writes=[cv])
                        if pends:
                            pends.pop(0)()
                        S.op("dve", lambda e: e.scalar_tensor_tensor(out=cv[:, 0:ntok], in0=gst[:, 1:1 + ntok], scalar=wc[:, 1, f:f + 1], in1=cv[:, 0:ntok], op0=ALU.mult, op1=ALU.add), reads=[gst, wc, cv], writes=[cv])
                        S.op("dve", lambda e: e.scalar_tensor_tensor(out=cv[:, 0:ntok], in0=gst[:, 2:2 + ntok], scalar=wc[:, 2, f:f + 1], in1=cv[:, 0:ntok], op0=ALU.mult, op1=ALU.add), reads=[gst, wc, cv], writes=[cv])
                        if pendm:
                            pendm.pop(0)()
                        pends.append(lambda cv=cv, sl=sl: S.op("act", lambda e: e.activation(out=sl[:, 0:ntok], in_=cv[:, 0:ntok], func=AF.Silu), reads=[cv], writes=[sl]))
                        pendm.append(lambda f=f, pv=pv, sl=sl: S.op("dve", lambda e: e.tensor_tensor(out=actT[:, f, 0:ntok], in0=pv[:, 0:ntok], in1=sl[:, 0:ntok], op=ALU.mult), reads=[pv, sl], writes=[actT]))
                    while pends:
                        pends.pop(0)()
                    while pendm:
                        pendm.pop(0)()
                    for t, (src, sbuf_, dst) in enumerate(tiles):
                        x = xts[t]
                        for half in range(2):
                            pb = pm.nxt()
                            for f in range(NF):
                                mm(pb[0:n, :], actT[:, f, t * n:(t + 1) * n], Wdn[:, f, half * 512:(half + 1) * 512], f == 0, f == NF - 1, [actT, Wdn], [pb], sig=(f == NF - 1))
                            S.op("dve", lambda e: e.tensor_tensor(out=x[0:n, half * 512:(half + 1) * 512], in0=pb[0:n, :], in1=x[0:n, half * 512:(half + 1) * 512], op=ALU.add), reads=[pb, x], writes=[x])
                        S.dma("pool", dst, x[0:n, :], reads=[x], writes=[outb])

                if phases >= 3:
                    S.op("pool", lambda e: e.memset(gprev[:], 0.0), writes=[gprev])
                    blocks = []
                    for I in range(SEQ // TB):
                        tiles = []
                        for t in range(TB // 128):
                            g = (TB // 128) * I + t
                            tiles.append((X2[g * 128:(g + 1) * 128, :], X2b[g], y_p[g * 128:(g + 1) * 128, :]))
                        blocks.append((tiles, 128, None))
                    for b in range(2):
                        r0 = SEQ + b * NS
                        blocks.append(([(X2[r0:r0 + NS, :], X2b[NT + b], y_s[b * NS:(b + 1) * NS, :])], NS, b))
                    RS[0] = "pool"
                    preps = {0: ffn_prep_a(blocks[0][0], blocks[0][1])}
                    ffn_prep_a2(preps[0])
                    ffn_prep_b(preps[0], blocks[0][1], 0)
                    for i, (tiles, n, sb_) in enumerate(blocks):
                        nxt = blocks[i + 1] if i + 1 < len(blocks) else None

                        def hook(f, i=i, nxt=nxt):
                            if nxt is None:
                                return
                            if f == 3:
                                preps[i + 1] = ffn_prep_a(nxt[0], nxt[1])
                            if f == 7:
                                ffn_prep_a2(preps[i + 1])
                            if f == 12:
                                ffn_prep_b(preps[i + 1], nxt[1], (i + 1) % 2)
                        if sb_ is not None:
                            if sb_ == 0:
                                for j in range(2):
                                    S.dma("pool", cvp[j:j + 1, :].rearrange("o (c p) -> p (o c)", p=128), gprev[:, :, j], reads=[gprev], writes=[outb], allow_slow_non_contiguous=True)
                            for j in range(2):
                                S.dma("sp", gprev[:, :, j], scv[sb_, j:j + 1, :].rearrange("o (c p) -> p (o c)", p=128), reads=[gprev], writes=[gprev], allow_slow_non_contiguous=True)
                        ffn_main(tiles, n, preps[i][0], i % 2, hook)
                        if sb_ is not None:
                            for j in range(2):
                                S.dma("pool", cvs[sb_, j:j + 1, :].rearrange("o (c p) -> p (o c)", p=128), gprev[:, :, j], reads=[gprev], writes=[outb], allow_slow_non_contiguous=True)
                S.barrier()
        except _Stop:
            pass
        S.stopped = False
        S.barrier()
        print("ops", S.nops, "waits", S.nwaits, flush=True)
    return nc


_NC = None


def kernel(**inp):
    global _NC
    f = lambda a: np.ascontiguousarray(np.asarray(a, dtype=np.float32))
    if _NC is None:
        _NC = build()
    nc = _NC
    shared = dict(
        w_in=f(inp["w_in"][0]), b_f=f(inp["b_f"]), g_qa=f(inp["g_qa"]), g_ka=f(inp["g_ka"]), rel=f(inp["rel_bias"][0]),
        g_qb=f(inp["g_qb"]), g_kb=f(inp["g_kb"]), w_o=f(inp["w_o"][0]), g1=f(inp["g_norm1"]), g2=f(inp["g_norm2"]),
        gmem=f(inp["g_mem"]), w_mq=f(inp["w_mq"][0]), w_mkv=f(inp["w_mkv"][0]), g_mq=f(inp["g_mq"]), g_mk=f(inp["g_mk"]),
        w_mo=f(inp["w_mo"][0]), g3=f(inp["g_norm3"]), w_up=f(inp["w_up"][0]), w_conv=f(inp["w_conv"][0]),
        b_conv=f(inp["b_conv"]), w_down=f(inp["w_down"][0]))
    in_maps = []
    for c in range(8):
        s = slice(2 * c, 2 * c + 2)
        m = dict(shared)
        m.update(
            xp=f(inp["x_prompt"][c]), xs=f(inp["x_sample"][s]).reshape(2 * NS, D),
            cak=f(inp["cache_a_k"][0, s]).reshape(2, 512, 512), cav=f(inp["cache_a_v"][0, s]).reshape(2, 512, 512),
            cbk=f(inp["cache_b_k"][0, s]).reshape(2, SEQ, 512), cbv=f(inp["cache_b_v"][0, s]).reshape(2, SEQ, 512),
            cbl=f(inp["cache_b_logf"][0, s]), cmk=f(inp["cache_mem_k"][0, s]).reshape(2, 256, D),
            cmv=f(inp["cache_mem_v"][0, s]).reshape(2, 256, D), scv=f(inp["state_conv"][0, s]), memp=f(inp["mem_prompt"][c]))
        in_maps.append(m)
    res = run_bass_kernel_spmd(nc, in_maps, core_ids=list(range(8)))
    R = res.results
    cat = lambda k: np.stack([np.asarray(R[c][k], dtype=np.float32) for c in range(8)], 0)
    y_p = cat("y_p")
    y_s = cat("y_s").reshape(16, NS, D)
    akp = cat("akp").reshape(1, 8, 512, 8, 64); avp = cat("avp").reshape(1, 8, 512, 8, 64)
    bkp = cat("bkp").reshape(1, 8, SEQ, 8, 64); bvp = cat("bvp").reshape(1, 8, SEQ, 8, 64)
    blp = cat("blp").reshape(1, 8, SEQ, 8)
    mkp = cat("mkp").reshape(1, 8, 256, 4, 256); mvp = cat("mvp").reshape(1, 8, 256, 4, 256)
    cvp = cat("cvp").reshape(1, 8, 2, DFF)
    aks = cat("aks").reshape(1, 16, NS, 8, 64); avs = cat("avs").reshape(1, 16, NS, 8, 64)
    bks = cat("bks").reshape(1, 16, NS, 8, 64); bvs = cat("bvs").reshape(1, 16, NS, 8, 64)
    bls = cat("bls").reshape(1, 16, NS, 8)
    cvs = cat("cvs").reshape(1, 16, 2, DFF)
    return (y_p, y_s, akp, avp, bkp, bvp, blp, mkp, mvp, cvp, aks, avs, bks, bvs, bls, cvs)
```
